# Optimizing a Trainium2 kernel written in Bass

```python
import jax, jax.numpy as jnp
from jax import lax
import numpy as np

D_MODEL = 2048
BATCH = 4
SEQ = 8192
DEPTH = 4

CTX_LEN = 256
GRID_W = 64
MIX_W = 2 * D_MODEL
SSD_W = MIX_W // 2
SSD_HEADS = 32
SSD_HEAD_DIM = SSD_W // SSD_HEADS
SSD_GROUPS = 8
SSD_STATE = 128
SSD_CHUNK = 128
CONV_W = 5
MLP_W = MIX_W - SSD_W
MLP_GROUPS = 16
MLP_GROUP_DIM = MLP_W // MLP_GROUPS
MLP_CHUNK = 128
GN = SSD_GROUPS * SSD_STATE
XBC_W = SSD_W + 2 * GN
DT_W = 2 * SSD_HEADS
IN_W = XBC_W + DT_W + SSD_W + 3 * MLP_W
EPS = 1e-6

kernel_name = "hybrid_ssd_chunkmlp_prefix_dit"


def _rmsnorm(x, g):
    xf = x.astype(jnp.float32)
    r = lax.rsqrt(jnp.mean(xf * xf, axis=-1, keepdims=True) + EPS)
    return (xf * r).astype(x.dtype) * g


def _dwconv_rows(x, w, b, n_rows, row_len):
    bsz, L, C = x.shape
    pad = CONV_W // 2
    xp = jnp.pad(x.reshape(bsz, n_rows, row_len, C), ((0, 0), (0, 0), (pad, pad), (0, 0)))
    y = b
    for k in range(CONV_W):
        y = y + xp[:, :, k:k + row_len] * w[k]
    return y.reshape(bsz, L, C)


def _ssd(xh, dt, A, Bm, Cm, h0, with_output):
    f32 = jnp.float32
    bsz, L, H, P = xh.shape
    G, N, Q = SSD_GROUPS, SSD_STATE, SSD_CHUNK
    R = H // G
    nc = L // Q
    x = xh.astype(f32).reshape(bsz, nc, Q, G, R, P)
    dtc = dt.reshape(bsz, nc, Q, G, R)
    a = dtc * A.reshape(G, R)
    xdt = x * dtc[..., None]
    Bc = Bm.astype(f32).reshape(bsz, nc, Q, G, N)
    Cc = Cm.astype(f32).reshape(bsz, nc, Q, G, N)
    acs = jnp.cumsum(a, axis=2)
    a_tot = acs[:, :, -1]
    decay_to_end = jnp.exp(a_tot[:, :, None] - acs)
    states = jnp.einsum('bckgn,bckgr,bckgrp->bcgrpn', Bc, decay_to_end, xdt)

    def step(h, inp):
        s, at = inp
        return h * jnp.exp(at)[..., None, None] + s, h

    hT, h_starts = lax.scan(step, h0.reshape(bsz, G, R, P, N),
                            (jnp.moveaxis(states, 1, 0), jnp.moveaxis(a_tot, 1, 0)))
    final = hT.reshape(bsz, H, P, N)
    if not with_output:
        return None, final
    h_starts = jnp.moveaxis(h_starts, 0, 1)
    CB = jnp.einsum('bcqgn,bckgn->bcgqk', Cc, Bc)
    acs_t = jnp.moveaxis(acs, 2, -1)
    diff = acs_t[..., :, None] - acs_t[..., None, :]
    mask = jnp.tril(jnp.ones((Q, Q), dtype=bool))
    Lmat = jnp.exp(jnp.where(mask, diff, -jnp.inf))
    y_diag = jnp.einsum('bcgqk,bcgrqk,bckgrp->bcqgrp', CB, Lmat, xdt)
    y_off = jnp.einsum('bcqgn,bcgrpn,bcqgr->bcqgrp', Cc, h_starts, jnp.exp(acs))
    return (y_diag + y_off).reshape(bsz, L, H, P), final


def _ssd_prep(z, conv_w, conv_b, dt_bias, a_log, n_rows, row_len):
    bsz, L, _ = z.shape
    xbc = jax.nn.silu(_dwconv_rows(z[..., :XBC_W], conv_w, conv_b, n_rows, row_len))
    xh = xbc[..., :SSD_W].reshape(bsz, L, SSD_HEADS, SSD_HEAD_DIM)
    Bm = xbc[..., SSD_W:SSD_W + GN].reshape(bsz, L, SSD_GROUPS, SSD_STATE)
    Cm = xbc[..., SSD_W + GN:XBC_W].reshape(bsz, L, SSD_GROUPS, SSD_STATE)
    dt_raw = z[..., XBC_W:XBC_W + DT_W].astype(jnp.float32).reshape(bsz, L, 2, SSD_HEADS)
    dt = jax.nn.softplus(dt_raw + dt_bias.astype(jnp.float32))
    A = -jnp.exp(a_log.astype(jnp.float32))
    return xh, Bm, Cm, dt, A


def _ssd_bidir(xh, Bm, Cm, dt, A, h0f, h0b, with_output):
    flip = lambda t: jnp.flip(t, axis=1)
    y_f, h_f = _ssd(xh, dt[:, :, 0], A[0], Bm, Cm, h0f, with_output)
    y_b, h_b = _ssd(flip(xh), flip(dt[:, :, 1]), A[1], flip(Bm), flip(Cm), h0b, with_output)
    if not with_output:
        return None, h_f, h_b
    return y_f + flip(y_b), h_f, h_b


def _mix_out(z, xh, y_ssd, d_skip, g_ssd, g_v, w_s, b_s, g_mlp, w_out):
    bsz, L, _ = z.shape
    o = XBC_W + DT_W
    z_ssd = z[..., o:o + SSD_W]
    u = z[..., o + SSD_W:o + SSD_W + MLP_W]
    v = z[..., o + SSD_W + MLP_W:o + SSD_W + 2 * MLP_W]
    z_mlp = z[..., o + SSD_W + 2 * MLP_W:]
    y = (y_ssd + d_skip.astype(jnp.float32)[:, None] * xh.astype(jnp.float32))
    y = y.reshape(bsz, L, SSD_W).astype(z.dtype)
    y_a = _rmsnorm(y * jax.nn.silu(z_ssd), g_ssd)
    vn = _rmsnorm(v, g_v).reshape(bsz, L // MLP_CHUNK, MLP_CHUNK, MLP_GROUPS, MLP_GROUP_DIM)
    sg = jnp.einsum('gqk,bckgd->bcqgd', w_s, vn) + jnp.swapaxes(b_s, 0, 1)[:, :, None]
    y_b = _rmsnorm(u * sg.reshape(bsz, L, MLP_W) * jax.nn.silu(z_mlp), g_mlp)
    return jnp.concatenate([y_a, y_b], axis=-1) @ w_out


def setup_inputs(seed: int = 0) -> dict:
    key = jax.random.key(seed)
    ks = jax.random.split(key, 20)
    D = D_MODEL
    nrm = jax.random.normal
    x = nrm(ks[0], (BATCH, SEQ, D), jnp.float32)
    c = nrm(ks[1], (BATCH, D), jnp.float32)
    ctx = nrm(ks[2], (BATCH, CTX_LEN, D), jnp.float32)
    c_ctx = nrm(ks[3], (D,), jnp.float32)
    w_ada = nrm(ks[4], (DEPTH, D, 3 * D), jnp.float32) * (0.5 * D ** -0.5)
    b_ada = 0.01 * nrm(ks[5], (DEPTH, 3 * D), jnp.float32)
    g_pre = 1.0 + 0.05 * nrm(ks[6], (DEPTH, D), jnp.float32)
    g_post = 1.0 + 0.05 * nrm(ks[7], (DEPTH, D), jnp.float32)
    w_in = nrm(ks[8], (DEPTH, D, IN_W), jnp.float32) * D ** -0.5
    conv_w = nrm(ks[9], (DEPTH, CONV_W, XBC_W), jnp.float32) * CONV_W ** -0.5
    conv_b = 0.01 * nrm(ks[10], (DEPTH, XBC_W), jnp.float32)
    u_dt = jax.random.uniform(ks[11], (DEPTH, 2, SSD_HEADS), jnp.float32)
    dt0 = jnp.exp(u_dt * (np.log(0.1) - np.log(0.001)) + np.log(0.001))
    dt_bias = dt0 + jnp.log(-jnp.expm1(-dt0))
    a_log = jnp.log(jax.random.uniform(ks[12], (DEPTH, 2, SSD_HEADS), jnp.float32, 1.0, 16.0))
    d_skip = 1.0 + 0.1 * nrm(ks[13], (DEPTH, SSD_HEADS), jnp.float32)
    g_ssd = 1.0 + 0.05 * nrm(ks[14], (DEPTH, SSD_W), jnp.float32)
    g_v = 1.0 + 0.05 * nrm(ks[15], (DEPTH, MLP_W), jnp.float32)
    w_s = nrm(ks[16], (DEPTH, MLP_GROUPS, MLP_CHUNK, MLP_CHUNK), jnp.float32) * (0.5 * MLP_CHUNK ** -0.5)
    b_s = 1.0 + 0.05 * nrm(ks[17], (DEPTH, MLP_GROUPS, MLP_CHUNK), jnp.float32)
    g_mlp = 1.0 + 0.05 * nrm(ks[18], (DEPTH, MLP_W), jnp.float32)
    w_out = nrm(ks[19], (DEPTH, MIX_W, D), jnp.float32) * MIX_W ** -0.5
    return {"x": x, "c": c, "ctx": ctx, "c_ctx": c_ctx, "w_ada": w_ada, "b_ada": b_ada,
            "g_pre": g_pre, "g_post": g_post, "w_in": w_in, "conv_w": conv_w, "conv_b": conv_b,
            "dt_bias": dt_bias, "a_log": a_log, "d_skip": d_skip, "g_ssd": g_ssd, "g_v": g_v,
            "w_s": w_s, "b_s": b_s, "g_mlp": g_mlp, "w_out": w_out}


def reference(x, c, ctx, c_ctx, w_ada, b_ada, g_pre, g_post, w_in, conv_w, conv_b,
              dt_bias, a_log, d_skip, g_ssd, g_v, w_s, b_s, g_mlp, w_out):
    bsz, L, _ = x.shape
    ROWS = L // GRID_W
    sc = jax.nn.silu(c)
    scc = jax.nn.silu(c_ctx)
    h0 = jnp.zeros((bsz, SSD_HEADS, SSD_HEAD_DIM, SSD_STATE), jnp.float32)
    for l in range(DEPTH):
        last = l == DEPTH - 1
        shift, scale, gate = jnp.split(sc @ w_ada[l] + b_ada[l], 3, axis=-1)
        shift_c, scale_c, gate_c = jnp.split(scc @ w_ada[l] + b_ada[l], 3, axis=-1)
        hc = _rmsnorm(ctx, g_pre[l]) * (1.0 + scale_c) + shift_c
        zc = hc @ (w_in[l][:, :XBC_W + DT_W] if last else w_in[l])
        xh_c, B_c, C_c, dt_c, A = _ssd_prep(zc, conv_w[l], conv_b[l], dt_bias[l], a_log[l], 1, CTX_LEN)
        y_c, h_f, h_b = _ssd_bidir(xh_c, B_c, C_c, dt_c, A, h0, h0, not last)
        hx = _rmsnorm(x, g_pre[l]) * (1.0 + scale[:, None]) + shift[:, None]
        zx = hx @ w_in[l]
        xh, Bm, Cm, dt, A = _ssd_prep(zx, conv_w[l], conv_b[l], dt_bias[l], a_log[l], ROWS, GRID_W)
        y_x, _, _ = _ssd_bidir(xh, Bm, Cm, dt, A, h_f, h_b, True)
        out = _rmsnorm(_mix_out(zx, xh, y_x, d_skip[l], g_ssd[l], g_v[l], w_s[l], b_s[l],
                                g_mlp[l], w_out[l]), g_post[l])
        x = x + gate[:, None] * out
        if not last:
            out_c = _rmsnorm(_mix_out(zc, xh_c, y_c, d_skip[l], g_ssd[l], g_v[l], w_s[l], b_s[l],
                                      g_mlp[l], w_out[l]), g_post[l])
            ctx = ctx + gate_c * out_c
    return x
```

```python
import numpy as np
from contextlib import ExitStack
import concourse.bass as bass
import concourse.mybir as mybir
from concourse.bass_utils import run_bass_kernel_spmd

F32 = mybir.dt.float32
BF16 = mybir.dt.bfloat16
AF = mybir.ActivationFunctionType
ALU = mybir.AluOpType

D = 2048
DEPTH = 4
NCTX = 2
H = 32
XBC = 4096
INW = 12352
EPS = 1e-6
NCORES = 4


class Sch:
    ENG = ("pe", "act", "dve", "pool", "sp")

    def __init__(s, nc):
        s.nc = nc
        s.esem = {k: nc.alloc_semaphore("es_" + k) for k in ("pe", "act", "dve", "pool")}
        s.ecnt = {k: 0 for k in s.esem}
        s.pool = []
        s.poolcnt = []
        s.ninstr = 0
        s.reset()

    def reset(s):
        s.st = {k: [] for k in s.ENG}
        s.lastw = {}
        s.rd = {}
        s.seen = {k: {} for k in s.ENG}
        s.keysem = {}

    def _deps(s, eng, reads, writes):
        deps = []
        for k in reads:
            t = s.lastw.get(k)
            if t is not None:
                deps.append(t)
        for k in writes:
            t = s.lastw.get(k)
            if t is not None:
                deps.append(t)
            deps.extend(s.rd.get(k, ()))
        out = []
        seen = s.seen[eng]
        for t in deps:
            if t[0] == "E" and t[1] == eng and eng == "pe":
                continue
            kk = (t[0], t[1])
            if seen.get(kk, -1) >= t[2]:
                continue
            seen[kk] = t[2]
            out.append(t)
        return out

    def _post(s, tok, reads, writes):
        for k in writes:
            s.lastw[k] = tok
            s.rd[k] = []
        for k in reads:
            s.rd.setdefault(k, []).append(tok)

    def op(s, eng, fn, reads=(), writes=()):
        deps = s._deps(eng, reads, writes)
        idx = len(s.st[eng])
        s.st[eng].append([fn, deps, False, None])
        tok = ("E", eng, idx)
        s._post(tok, reads, writes)
        return tok

    def dma(s, out, in_, reads=(), writes=(), key=None, q="sp"):
        if key is None:
            key = writes[0]
        deps = s._deps(q, reads, writes)
        if key not in s.keysem:
            n = len(s.keysem)
            if n >= len(s.pool):
                s.pool.append(s.nc.alloc_semaphore("ds%d" % n))
                s.poolcnt.append(0)
            s.keysem[key] = n
        si = s.keysem[key]
        s.poolcnt[si] += 16
        tok = ("D", si, s.poolcnt[si])
        s.st[q].append([lambda e: e.dma_start(out=out, in_=in_), deps, False, tok])
        s._post(tok, reads, writes)
        return tok

    def mm(s, out, lhsT, rhs, start, stop, R, W):
        s.op("pe", lambda e: e.matmul(out, lhsT=lhsT, rhs=rhs, start=start, stop=stop), R, W)

    def tr(s, out, in_, ident, R, W):
        s.op("pe", lambda e: e.transpose(out=out, in_=in_, identity=ident), R, W)

    def act(s, out, in_, func, R, W, bias=None, scale=None, accum=None):
        def f(e):
            kw = {}
            if bias is not None:
                kw["bias"] = bias
            if scale is not None:
                kw["scale"] = scale
            if accum is not None:
                kw["accum_out"] = accum
            return e.activation(out=out, in_=in_, func=func, **kw)
        s.op("act", f, R, W)

    def tt(s, eng, out, in0, in1, op, R, W):
        s.op(eng, lambda e: e.tensor_tensor(out=out, in0=in0, in1=in1, op=op), R, W)

    def ts(s, eng, out, in0, s1, s2, op0, op1, R, W):
        if op1 is None:
            s.op(eng, lambda e: e.tensor_scalar(out=out, in0=in0, scalar1=s1, scalar2=None, op0=op0), R, W)
        else:
            s.op(eng, lambda e: e.tensor_scalar(out=out, in0=in0, scalar1=s1, scalar2=s2, op0=op0, op1=op1), R, W)

    def stt(s, out, in0, sc, in1, op0, op1, R, W):
        s.op("dve", lambda e: e.scalar_tensor_tensor(out=out, in0=in0, scalar=sc, in1=in1, op0=op0, op1=op1), R, W)

    def cp(s, eng, out, in_, R, W):
        if eng == "act":
            s.op(eng, lambda e: e.copy(out=out, in_=in_), R, W)
        else:
            s.op(eng, lambda e: e.tensor_copy(out=out, in_=in_), R, W)

    def recip(s, out, in_, R, W):
        s.op("dve", lambda e: e.reciprocal(out=out, in_=in_), R, W)

    def memset(s, eng, ap, val, W):
        s.op(eng, lambda e: e.memset(ap, val), (), W)

    def emit(s):
        nc = s.nc
        fin = [("D", si, s.poolcnt[si]) for si in sorted(s.keysem.values())]
        s.st["sp"].append([None, fin, False, None])
        needed = {k: set() for k in s.esem}
        for eng in s.ENG:
            for ent in s.st[eng]:
                for t in ent[1]:
                    if t[0] == "E":
                        needed[t[1]].add(t[2])
        vals = {}
        for eng in s.esem:
            c = s.ecnt[eng]
            v = {}
            for i in sorted(needed[eng]):
                c += 1
                v[i] = c
            vals[eng] = v
            s.ecnt[eng] = c
        engobj = dict(pe="tensor", act="scalar", dve="vector", pool="gpsimd", sp="sync")
        with nc.Block() as block:
            for eng in s.ENG:
                stream = s.st[eng]
                if not stream:
                    continue

                def body(e, eng=eng, stream=stream):
                    nd = needed.get(eng, ())
                    for i, ent in enumerate(stream):
                        for t in ent[1]:
                            if t[0] == "E":
                                e.wait_ge(s.esem[t[1]], vals[t[1]][t[2]])
                            else:
                                e.wait_ge(s.pool[t[1]], t[2])
                        if ent[0] is None:
                            continue
                        ins = ent[0](e)
                        s.ninstr += 1
                        if ent[3] is not None:
                            ins.then_inc(s.pool[ent[3][1]], 16)
                        elif i in nd:
                            ins.then_inc(s.esem[eng], 1)
                getattr(block, engobj[eng])(body)
        s.reset()


def build(NLAT, layers, first, last_is_final):
    C = NCTX + NLAT
    L = len(layers)
    nc = bass.Bass("TRN2", target_bir_lowering=False)

    def din(name, shape, dt=F32):
        return nc.dram_tensor(name, list(shape), dt, kind="ExternalInput").ap()

    def dscr(name, shape, dt=F32):
        return nc.dram_tensor(name, list(shape), dt, kind=("ExternalOutput" if DEBUG else "Internal")).ap()

    x_in = din("x_in", [NLAT * 128, D])
    ctx_in = din("ctx_in", [NCTX * 128, D])
    cT_in = din("cT", [128, 16, 2])
    consts_in = din("consts", [128, 6, 128])
    w_ada = din("w_ada", [L, D, 3 * D])
    w_in = din("w_in", [L, D, INW])
    w_out = din("w_out", [L, 2 * D, D])
    badaT_in = din("badaT", [L, 128, 32])
    bgate_in = din("bgate", [L, 2, D])
    gpost_in = din("gpost", [L, 2, D])
    gpreT_in = din("gpreT", [L, 128, 16])
    convw_in = din("convwT", [L, 128, 32, 5])
    convb_in = din("convbT", [L, 128, 32])
    dtb_in = din("dtb", [L, 128, 64])
    alog_in = din("alog", [L, 128, 64])
    dskip_in = din("dskip", [L, 128, 32])
    gssd_in = din("gssd", [L, D])
    gv_in = din("gv", [L, D])
    gmlp_in = din("gmlp", [L, D])
    wsT_in = din("wsT", [L, 128, 16, 128])
    bsT_in = din("bsT", [L, 128, 16])

    out_d = nc.dram_tensor("out", [NLAT * 128, D], F32, kind="ExternalOutput").ap()
    ctx_out = nc.dram_tensor("ctx_out", [NCTX * 128, D], F32, kind="ExternalOutput").ap()

    hxT_d = dscr("hxT_d", [C, 128, 2048], BF16)
    W1s_d = dscr("W1s_d", [32, 128, 2048], BF16)
    W1dt_d = dscr("W1dt_d", [128, 1024], BF16)
    W2s_d = dscr("W2s_d", [16, 128, 8192], BF16)
    WOs_d = dscr("WOs_d", [4, 128, 16384], BF16)
    xtok_d = dscr("xtok_d", [C, 128, 2048], BF16)
    xdt_d = dscr("xdt_d", [2, C, 128, 2048], BF16)
    BT_d = dscr("BT_d", [C, 128, 1024], BF16)
    CT_d = dscr("CT_d", [C, 128, 1024], BF16)
    a_d = dscr("a_d", [C, 128, 64])
    cdec_d = dscr("cdec_d", [C, 128, 64])
    S_d = dscr("S_d", [2, C, 128, 2048])
    hst_d = dscr("hst_d", [2, C, 128, 2048], BF16)
    z2_d = dscr("z2_d", [C, 128, 8192], BF16)
    yT_d = dscr("yT_d", [C, 128, 4096], BF16)
    gg_d = dscr("gg_d", [2, D])

    s = Sch(nc)
    pc = [0]

    uid = [0]

    def sb(name, shape, dt=F32):
        return nc.alloc_sbuf_tensor("g_" + name, list(shape), dt)

    consts = sb("consts", [128, 6, 128])
    ident = consts[:, 0, :]
    Um = consts[:, 1, :]
    Lm = consts[:, 2, :]
    SLm = consts[:, 3, :]
    SUm = consts[:, 4, :]
    ones = consts[:, 5, :]
    identb = sb("identb", [128, 128], BF16)
    sT = sb("sT", [128, 16, 2])
    G1 = sb("G1", [128, 16, 2])
    shT = sb("shT", [128, 16, 2])
    adaT = sb("adaT", [128, 32, 2])
    badaT = sb("badaT", [128, 32])
    gpreT = sb("gpreT", [128, 16])
    convw = sb("convw", [128, 32, 5])
    convb = sb("convb", [128, 32])
    dtb = sb("dtb", [128, 64])
    Abc = sb("Abc", [128, 64])
    dskip = sb("dskip", [128, 32])
    bsT = sb("bsT", [128, 16])
    wsTb = sb("wsTb", [128, 16, 128], BF16)
    epst = sb("epst", [128, 1])
    ps = nc.alloc_psum_tensor("ps", [128, 8, 512], F32)

    def B(i):
        return "ps%d" % i

    s.dma(consts[:], consts_in, (), ["consts"])
    s.dma(sT[:], cT_in, (), ["sT"])
    s.cp("dve", identb[:], ident, ["consts"], ["identb"])
    s.act(sT[:], sT[:], AF.Silu, ["sT"], ["sT"])
    s.memset("dve", epst[:], EPS, ["epst"])
    s.emit()

    def rms_r(ssum, rr, n, R, W):
        s.ts("dve", rr, ssum, 1.0 / n, EPS, ALU.mult, ALU.add, R, W)
        s.act(rr, rr, AF.Sqrt, W, W)
        s.recip(rr, rr, W, W)

    for li, l in enumerate(layers):
        last = last_is_final and (li == L - 1)
        x_src = x_in if (li == 0) else out_d
        ctx_src = ctx_in if (li == 0) else ctx_out
        chunks2 = list(range(NCTX, C)) if last else list(range(C))

        def xrow(c, src_x=x_src, src_c=ctx_src):
            if c < NCTX:
                return src_c[c * 128:(c + 1) * 128, :]
            return src_x[(c - NCTX) * 128:(c - NCTX + 1) * 128, :]

        def xdst(c):
            if c < NCTX:
                return ctx_out[c * 128:(c + 1) * 128, :]
            return out_d[(c - NCTX) * 128:(c - NCTX + 1) * 128, :]

        with ExitStack() as es:
            def T(name, shape, dt=F32):
                uid[0] += 1
                return es.enter_context(nc.sbuf_tensor("t%d_%s" % (uid[0], name), list(shape), dt))
            s.dma(badaT[:], badaT_in[li], (), ["badaT"])
            s.dma(gpreT[:], gpreT_in[li], (), ["gpreT"])
            s.dma(convw[:], convw_in[li], (), ["convw"])
            s.dma(convb[:], convb_in[li], (), ["convb"])
            s.dma(dtb[:], dtb_in[li], (), ["dtb"])
            s.dma(Abc[:], alog_in[li], (), ["Abc"])
            s.dma(dskip[:], dskip_in[li], (), ["dskip"])
            s.dma(bsT[:], bsT_in[li], (), ["bsT"])
            wsf = T("wsf", [128, 16, 128])
            s.dma(wsf[:], wsT_in[li], (), ["wsf"])
            s.cp("dve", wsTb[:], wsf[:], ["wsf"], ["wsTb"])
            s.act(Abc[:], Abc[:], AF.Exp, ["Abc"], ["Abc"])
            s.ts("dve", Abc[:], Abc[:], -1.0, None, ALU.mult, None, ["Abc"], ["Abc"])
            bg = T("bg", [2, D])
            gp = T("gp", [2, D])
            gsb = T("gsb", [2, D])
            s.dma(bg[:], bgate_in[li], (), ["bg"])
            s.dma(gp[:], gpost_in[li], (), ["gp"])
            wblk = [T("wblk%d" % i, [128, 16, 512]) for i in range(2)]
            wa = w_ada[li].rearrange("(k p) c -> p k c", p=128)
            for jb in range(12):
                wb = wblk[jb % 2]
                wk = "wblk%d" % (jb % 2)
                for hh in range(2):
                    s.dma(wb[:, hh * 8:(hh + 1) * 8, :], wa[:, hh * 8:(hh + 1) * 8, jb * 512:(jb + 1) * 512],
                          (), [wk + "h%d" % hh])
                Rw = [wk + "h0", wk + "h1", "sT"]
                if jb < 8:
                    for t in range(4):
                        o = ps[:, 0, (jb * 4 + t) * 2:(jb * 4 + t) * 2 + 2]
                        for k in range(16):
                            s.mm(o, wb[:, k, t * 128:(t + 1) * 128], sT[:, k, :], k == 0, k == 15, Rw, [B(0)])
                else:
                    o = ps[0:2, jb - 7, :]
                    for k in range(16):
                        s.mm(o, sT[:, k, :], wb[:, k, :], k == 0, k == 15, Rw, [B(jb - 7)])
                    s.tt("dve", gsb[:, (jb - 8) * 512:(jb - 7) * 512], o, bg[:, (jb - 8) * 512:(jb - 7) * 512],
                         ALU.add, [B(jb - 7), "bg"], ["gsb"])
            s.tt("dve", adaT[:], ps[:, 0, 0:64].rearrange("p (j r) -> p j r", r=2),
                 badaT[:, :, None].broadcast_to([128, 32, 2]), ALU.add, [B(0), "badaT"], ["adaT"])
            s.cp("dve", shT[:], adaT[:, 0:16, :], ["adaT"], ["shT"])
            s.ts("dve", G1[:], adaT[:, 16:32, :], 1.0, None, ALU.add, None, ["adaT"], ["G1"])
            s.tt("dve", G1[:], G1[:], gpreT[:, :, None].broadcast_to([128, 16, 2]), ALU.mult, ["G1", "gpreT"], ["G1"])
            s.tt("dve", gsb[:], gsb[:], gp[:], ALU.mult, ["gsb", "gp"], ["gsb"])
            s.dma(gg_d, gsb[:], ["gsb"], ["gg_d"])
            stf = [T("stf%d" % i, [128, 4096]) for i in range(2)]
            stb = [T("stb%d" % i, [128, 4096], BF16) for i in range(2)]
            wi = w_in[li].rearrange("(k p) c -> p k c", p=128)
            wo = w_out[li].rearrange("(k p) c -> p k c", p=128)
            jobs = []
            for j in range(32):
                jobs.append((wi[:, :, j * 128:(j + 1) * 128], [16, 128], W1s_d[j]))
            jobs.append((wi[:, :, 4096:4160], [16, 64], W1dt_d))
            for jb in range(16):
                for hh in range(2):
                    jobs.append((wi[:, hh * 8:(hh + 1) * 8, 4160 + jb * 512:4160 + (jb + 1) * 512], [8, 512],
                                 W2s_d[jb][:, hh * 4096:(hh + 1) * 4096]))
            for jb in range(4):
                for hh in range(4):
                    jobs.append((wo[:, hh * 8:(hh + 1) * 8, jb * 512:(jb + 1) * 512], [8, 512],
                                 WOs_d[jb][:, hh * 4096:(hh + 1) * 4096]))
            cast_engs = ["act", "pool", "dve"]
            for n, (src, shp, dst) in enumerate(jobs):
                i = n % 2
                ne = shp[0] * shp[1]
                fv = stf[i][:, 0:ne].rearrange("p (a b) -> p a b", b=shp[1])
                s.dma(fv, src, (), ["stf%d" % i])
                s.cp(cast_engs[n % 3], stb[i][:, 0:ne], stf[i][:, 0:ne], ["stf%d" % i], ["stb%d" % i])
                s.dma(dst, stb[i][:, 0:ne], ["stb%d" % i], [("w", n)], key="st_stb%d" % i)
            s.emit()
            pc[0] += 1
            if pc[0] >= STOP:
                return nc, s

        with ExitStack() as es:
            def T(name, shape, dt=F32):
                uid[0] += 1
                return es.enter_context(nc.sbuf_tensor("t%d_%s" % (uid[0], name), list(shape), dt))
            xin = [T("xin%d" % i, [128, D]) for i in range(2)]
            xn = [T("xn%d" % i, [128, D]) for i in range(2)]
            junk = T("junk", [128, D], BF16)
            ssr = [T("ssr%d" % i, [128, 2]) for i in range(2)]
            hxs = [T("hxs%d" % i, [128, 16, 128], BF16) for i in range(2)]
            for c in range(C):
                i = c % 2
                r = 1 if c < NCTX else 0
                s.dma(xin[i][:], xrow(c), (), ["xin%d" % i])
                s.act(junk[:], xin[i][:], AF.Square, ["xin%d" % i], ["junk", "ss%d" % i], accum=ssr[i][:, 0:1])
                rms_r(ssr[i][:, 0:1], ssr[i][:, 1:2], D, ["ss%d" % i], ["rr%d" % i])
                s.ts("dve", xn[i][:], xin[i][:], ssr[i][:, 1:2], None, ALU.mult, None, ["xin%d" % i, "rr%d" % i], ["xn%d" % i])
                for kb in range(4):
                    bk = (c % 2) * 4 + kb
                    for kq in range(4):
                        k = kb * 4 + kq
                        s.mm(ps[:, bk, kq * 128:(kq + 1) * 128], xn[i][:, k * 128:(k + 1) * 128], ident, True, True,
                             ["xn%d" % i, "consts"], [B(bk)])
                    for kq in range(4):
                        k = kb * 4 + kq
                        s.ts("dve", hxs[i][:, k, :], ps[:, bk, kq * 128:(kq + 1) * 128], G1[:, k, r:r + 1], shT[:, k, r:r + 1],
                             ALU.mult, ALU.add, [B(bk), "G1", "shT"], [("hxs%d" % i, k)])
                s.dma(hxT_d[c].rearrange("p (k t) -> p k t", t=128), hxs[i][:],
                      [("hxs%d" % i, k) for k in range(16)], [("hxT_d", c)], key="st_hxs%d" % i)
            s.emit()
            pc[0] += 1
            if pc[0] >= STOP:
                return nc, s

        with ExitStack() as es:
            def T(name, shape, dt=F32):
                uid[0] += 1
                return es.enter_context(nc.sbuf_tensor("t%d_%s" % (uid[0], name), list(shape), dt))
            hx = [T("hx%d" % i, [128, 16, 512], BF16) for i in range(2)]
            w1 = [T("w1_%d" % i, [128, 16, 128], BF16) for i in range(3)]
            w1dt = T("w1dt", [128, 16, 64], BF16)
            acc = [T("acc%d" % i, [128, 512]) for i in range(2)]
            xbcj = [T("xbcj%d" % i, [128, 512], BF16) for i in range(2)]
            btsb = T("btsb", [128, 8, 512], BF16)
            ctsb = T("ctsb", [128, 8, 512], BF16)
            xtok = T("xtok", [128, 4, 2048], BF16)
            btok = T("btok", [128, 4, 1024], BF16)
            dtv = [T("dtv%d" % i, [128, 64]) for i in range(2)]
            asb = [T("asb%d" % i, [128, 64]) for i in range(2)]
            dec = [T("dec%d" % i, [128, 64]) for i in range(2)]
            cdec = [T("cdec%d" % i, [128, 64]) for i in range(2)]
            xdt = [[T("xdt%d_%d" % (d_, i), [128, 2048], BF16) for i in range(2)] for d_ in range(2)]
            xw = [T("xw%d" % d_, [128, 2048], BF16) for d_ in range(2)]
            Ssb = [T("Ssb%d" % i, [128, 2048]) for i in range(2)]
            s.dma(w1dt[:], W1dt_d.rearrange("p (k c) -> p k c", c=64), (), ["w1dt"])
            sbs = [list(range(NCTX))] + [list(range(NCTX + 4 * i, NCTX + 4 * i + 4)) for i in range(NLAT // 4)]
            wcnt = 0
            ccnt = 0
            for sbi, chs in enumerate(sbs):
                Tn = 128 * len(chs)
                rowlen = 256 if sbi == 0 else 64
                nrows = Tn // rowlen
                hb = hx[sbi % 2]
                hk = "hx%d" % (sbi % 2)
                for ci, c in enumerate(chs):
                    s.dma(hb[:, :, ci * 128:(ci + 1) * 128], hxT_d[c].rearrange("p (k t) -> p k t", t=128),
                          (), [(hk, ci)])
                hR = [(hk, ci) for ci in range(len(chs))]
                for j in range(32):
                    wi_ = wcnt % 3
                    wcnt += 1
                    s.dma(w1[wi_][:], W1s_d[j].rearrange("p (k c) -> p k c", c=128), (), ["w1_%d" % wi_])
                    gb = j % 2
                    o = ps[:, gb, 0:Tn]
                    for k in range(16):
                        s.mm(o, w1[wi_][:, k, :], hb[:, k, 0:Tn], k == 0, k == 15, ["w1_%d" % wi_] + hR, [B(gb)])
                    a_ = acc[gb]
                    ak = "acc%d" % gb
                    s.ts("dve", a_[:, 0:Tn], o, convw[:, j, 2:3], convb[:, j:j + 1], ALU.mult, ALU.add,
                         [B(gb), "convw", "convb"], [ak])
                    ov = o.rearrange("p (r t) -> p r t", t=rowlen)
                    av = a_[:, 0:Tn].rearrange("p (r t) -> p r t", t=rowlen)
                    for kk in (0, 1, 3, 4):
                        sh = kk - 2
                        if sh < 0:
                            src = ov[:, :, 0:rowlen + sh]
                            dst = av[:, :, -sh:rowlen]
                        else:
                            src = ov[:, :, sh:rowlen]
                            dst = av[:, :, 0:rowlen - sh]
                        s.stt(dst, src, convw[:, j, kk:kk + 1], dst, ALU.mult, ALU.add, [B(gb), "convw", ak], [ak])
                    if j < 16:
                        tgt = xbcj[gb][:, 0:Tn]
                        tk = "xbcj%d" % gb
                    elif j < 24:
                        tgt = btsb[:, j - 16, 0:Tn]
                        tk = ("btsb", j - 16)
                    else:
                        tgt = ctsb[:, j - 24, 0:Tn]
                        tk = ("ctsb", j - 24)
                    s.act(tgt, a_[:, 0:Tn], AF.Silu, [ak], [tk])
                    if j < 24:
                        nch = len(chs)
                        for ci in range(nch):
                            s.mm(ps[:, 2, ci * 128:(ci + 1) * 128], tgt[:, ci * 128:(ci + 1) * 128], identb[:], True, True,
                                 [tk, "identb"], [B(2)])
                        pv = ps[:, 2, 0:nch * 128].rearrange("p (c t) -> p c t", t=128)
                        if j < 16:
                            s.cp("act", xtok[:, 0:nch, j * 128:(j + 1) * 128], pv, [B(2)], [("xtok", j)])
                        else:
                            s.cp("act", btok[:, 0:nch, (j - 16) * 128:(j - 15) * 128], pv, [B(2)], [("btok", j - 16)])
                for ci, c in enumerate(chs):
                    i = ccnt % 2
                    ccnt += 1
                    o = ps[:, 3, 0:64]
                    for k in range(16):
                        s.mm(o, hb[:, k, ci * 128:(ci + 1) * 128], w1dt[:, k, :], k == 0, k == 15, [(hk, ci), "w1dt"], [B(3)])
                    dk_ = "dtv%d" % i
                    s.tt("dve", dtv[i][:], o, dtb[:], ALU.add, [B(3), "dtb"], [dk_])
                    s.act(dtv[i][:], dtv[i][:], AF.Exp, [dk_], [dk_])
                    s.ts("dve", dtv[i][:], dtv[i][:], 1.0, None, ALU.add, None, [dk_], [dk_])
                    s.act(dtv[i][:], dtv[i][:], AF.Ln, [dk_], [dk_])
                    s.tt("dve", asb[i][:], dtv[i][:], Abc[:], ALU.mult, [dk_, "Abc"], ["asb%d" % i])
                    o2 = ps[:, 3, 64:192]
                    s.mm(o2[:, 0:32], SLm, asb[i][:, 0:32], True, True, ["consts", "asb%d" % i], [B(3)])
                    s.mm(o2[:, 32:64], SUm, asb[i][:, 32:64], True, True, ["consts", "asb%d" % i], [B(3)])
                    s.mm(o2[:, 64:128], ones, asb[i][:, 0:64], True, True, ["consts", "asb%d" % i], [B(3)])
                    s.act(dec[i][:], o2[:, 0:64], AF.Exp, [B(3)], ["dec%d" % i])
                    s.act(cdec[i][:], o2[:, 64:128], AF.Exp, [B(3)], ["cdec%d" % i])
                    xR = [("xtok", j) for j in range(16)]
                    xv = xtok[:, ci, :].rearrange("p (h e) -> p h e", e=64)
                    for d_ in range(2):
                        xd = xdt[d_][i]
                        s.tt("dve" if d_ == 0 else "pool", xd[:].rearrange("p (h e) -> p h e", e=64), xv,
                             dtv[i][:, d_ * 32:(d_ + 1) * 32, None].broadcast_to([128, 32, 64]), ALU.mult,
                             xR + [dk_], ["xdt%d_%d" % (d_, i)])
                        s.tt("dve" if d_ == 0 else "pool", xw[d_][:].rearrange("p (h e) -> p h e", e=64),
                             xd[:].rearrange("p (h e) -> p h e", e=64),
                             dec[i][:, d_ * 32:(d_ + 1) * 32, None].broadcast_to([128, 32, 64]), ALU.mult,
                             ["xdt%d_%d" % (d_, i), "dec%d" % i], ["xw%d" % d_])
                        sk = "Ssb%d" % d_
                        for hf in range(2):
                            b0 = 4 + 2 * hf
                            for gq in range(4):
                                g = hf * 4 + gq
                                s.mm(ps[:, b0 + gq // 2, (gq % 2) * 256:(gq % 2 + 1) * 256], btok[:, ci, g * 128:(g + 1) * 128],
                                     xw[d_][:, g * 256:(g + 1) * 256], True, True,
                                     [("btok", g), "xw%d" % d_], [B(b0 + gq // 2)])
                            s.cp("act", Ssb[d_][:, hf * 1024:(hf + 1) * 1024].rearrange("p (b f) -> p b f", f=512),
                                 ps[:, b0:b0 + 2, :], [B(b0), B(b0 + 1)], [(sk, hf)])
                        s.dma(S_d[d_][c], Ssb[d_][:], [(sk, 0), (sk, 1)], [("S_d", d_, c)], key="st_" + sk)
                        s.dma(xdt_d[d_][c], xd[:], ["xdt%d_%d" % (d_, i)], [("xdt_d", d_, c)], key="st_xdt%d_%d" % (d_, i))
                    s.dma(xtok_d[c], xtok[:, ci, :], xR, [("xtok_d", c)], key="st_xtok%d" % ci)
                    s.dma(BT_d[c].rearrange("p (g t) -> p g t", t=128), btsb[:, :, ci * 128:(ci + 1) * 128],
                          [("btsb", g) for g in range(8)], [("BT_d", c)], key="st_bt%d" % ci)
                    s.dma(CT_d[c].rearrange("p (g t) -> p g t", t=128), ctsb[:, :, ci * 128:(ci + 1) * 128],
                          [("ctsb", g) for g in range(8)], [("CT_d", c)], key="st_ct%d" % ci)
                    s.dma(a_d[c], asb[i][:], ["asb%d" % i], [("a_d", c)], key="st_asb%d" % i)
                    s.dma(cdec_d[c], cdec[i][:], ["cdec%d" % i], [("cdec_d", c)], key="st_cdec%d" % i)
            s.emit()
            pc[0] += 1
            if pc[0] >= STOP:
                return nc, s

        with ExitStack() as es:
            def T(name, shape, dt=F32):
                uid[0] += 1
                return es.enter_context(nc.sbuf_tensor("t%d_%s" % (uid[0], name), list(shape), dt))
            hh_ = [T("h%d" % d_, [128, 2048]) for d_ in range(2)]
            Sl = [[T("Sl%d_%d" % (d_, i), [128, 2048]) for i in range(2)] for d_ in range(2)]
            cdl = [[T("cdl%d_%d" % (d_, i), [128, 64]) for i in range(2)] for d_ in range(2)]
            hbf = [[T("hbf%d_%d" % (d_, i), [128, 2048], BF16) for i in range(2)] for d_ in range(2)]
            orders = [list(range(C)), [1, 0] + list(range(C - 1, NCTX - 1, -1))]
            engs = ["dve", "pool"]
            for d_ in range(2):
                s.memset(engs[d_], hh_[d_][:], 0.0, ["h%d" % d_])
            for n in range(C):
                for d_ in range(2):
                    c = orders[d_][n]
                    i = n % 2
                    e_ = engs[d_]
                    hk = "h%d" % d_
                    s.dma(Sl[d_][i][:], S_d[d_][c], (), ["Sl%d_%d" % (d_, i)])
                    s.dma(cdl[d_][i][:], cdec_d[c], (), ["cdl%d_%d" % (d_, i)])
                    s.cp("act", hbf[d_][i][:], hh_[d_][:], [hk], ["hbf%d_%d" % (d_, i)])
                    s.dma(hst_d[d_][c], hbf[d_][i][:], ["hbf%d_%d" % (d_, i)], [("hst_d", d_, c)], key="st_hbf%d_%d" % (d_, i))
                    hv = hh_[d_][:].rearrange("p (h e) -> p h e", e=64)
                    s.tt(e_, hv, hv, cdl[d_][i][:, d_ * 32:(d_ + 1) * 32, None].broadcast_to([128, 32, 64]), ALU.mult,
                         [hk, "cdl%d_%d" % (d_, i)], [hk])
                    s.tt(e_, hh_[d_][:], hh_[d_][:], Sl[d_][i][:], ALU.add, [hk, "Sl%d_%d" % (d_, i)], [hk])
            s.emit()
            pc[0] += 1
            if pc[0] >= STOP:
                return nc, s

        with ExitStack() as es:
            def T(name, shape, dt=F32):
                uid[0] += 1
                return es.enter_context(nc.sbuf_tensor("t%d_%s" % (uid[0], name), list(shape), dt))
            hx = [T("hx%d" % i, [128, 16, 512], BF16) for i in range(2)]
            w2 = [T("w2_%d" % i, [128, 16, 512], BF16) for i in range(2)]
            zst = [T("zst%d" % i, [128, 512], BF16) for i in range(4)]
            sbs = [list(range(NCTX))] + [list(range(NCTX + 4 * i, NCTX + 4 * i + 4)) for i in range(NLAT // 4)]
            if last:
                sbs = sbs[1:]
            wcnt = 0
            ecnt = 0
            for sbi, chs in enumerate(sbs):
                hb = hx[sbi % 2]
                hk = "hx%d" % (sbi % 2)
                for ci, c in enumerate(chs):
                    s.dma(hb[:, :, ci * 128:(ci + 1) * 128], hxT_d[c].rearrange("p (k t) -> p k t", t=128), (), [(hk, ci)])
                for jb in range(16):
                    wi_ = wcnt % 2
                    wcnt += 1
                    for hh2 in range(2):
                        s.dma(w2[wi_][:, hh2 * 8:(hh2 + 1) * 8, :],
                              W2s_d[jb][:, hh2 * 4096:(hh2 + 1) * 4096].rearrange("p (k c) -> p k c", c=512),
                              (), [("w2_%d" % wi_, hh2)])
                    for ci, c in enumerate(chs):
                        bk = ecnt % 8
                        zi = ecnt % 4
                        ecnt += 1
                        o = ps[:, bk, :]
                        for k in range(16):
                            s.mm(o, hb[:, k, ci * 128:(ci + 1) * 128], w2[wi_][:, k, :], k == 0, k == 15,
                                 [(hk, ci), ("w2_%d" % wi_, k // 8)], [B(bk)])
                        s.cp("act" if ecnt % 2 == 0 else "dve", zst[zi][:], o, [B(bk)], ["zst%d" % zi])
                        s.dma(z2_d[c][:, jb * 512:(jb + 1) * 512], zst[zi][:], ["zst%d" % zi], [("z2_d", c, jb)],
                              key="st_zst%d" % zi)
            s.emit()
            pc[0] += 1
            if pc[0] >= STOP:
                return nc, s

        with ExitStack() as es:
            def T(name, shape, dt=F32):
                uid[0] += 1
                return es.enter_context(nc.sbuf_tensor("t%d_%s" % (uid[0], name), list(shape), dt))
            gssd = T("gssd", [128, D])
            gv = T("gv", [128, D])
            gmlp = T("gmlp", [128, D])
            s.dma(gssd[:], gssd_in[li].partition_broadcast(128), (), ["gssd"])
            s.dma(gv[:], gv_in[li].partition_broadcast(128), (), ["gv"])
            s.dma(gmlp[:], gmlp_in[li].partition_broadcast(128), (), ["gmlp"])
            z2 = T("z2", [128, 8192], BF16)
            NB = 2
            xtk = [T("xtk%d" % i, [128, 2048], BF16) for i in range(NB)]
            xdl = [[T("xdl%d_%d" % (d_, i), [128, 2048], BF16) for i in range(NB)] for d_ in range(2)]
            hsl = [[T("hsl%d_%d" % (d_, i), [128, 2048], BF16) for i in range(NB)] for d_ in range(2)]
            btl = [T("btl%d" % i, [128, 8, 128], BF16) for i in range(NB)]
            ctl = [T("ctl%d" % i, [128, 8, 128], BF16) for i in range(NB)]
            al = [T("al%d" % i, [128, 64]) for i in range(NB)]
            cbm = [T("cbm%d" % d_, [128, 8, 128]) for d_ in range(2)]
            rhs4 = [T("rhs4_%d" % i, [128, 4, 128]) for i in range(2)]
            E4 = [T("E4_%d" % i, [128, 4, 128]) for i in range(2)]
            F4 = [T("F4_%d" % i, [128, 4, 128]) for i in range(2)]
            M4 = [T("M4_%d" % i, [128, 4, 128], BF16) for i in range(2)]
            N4 = [T("N4_%d" % i, [128, 4, 128], BF16) for i in range(2)]
            ybuf = T("ybuf", [128, D])
            tmp = T("tmp", [128, D])
            mbuf = T("mbuf", [128, D])
            vn = T("vn", [128, D], BF16)
            ycat = T("ycat", [128, 2 * D], BF16)
            yTs = [T("yTs%d" % i, [128, 32, 128], BF16) for i in range(2)]
            junk = T("junk5", [128, D], BF16)
            st5 = T("st5", [128, 8])
            masks = [(SLm, Um, Um), (SUm, Lm, Lm)]

            def loads(n):
                c = chunks2[n]
                i = n % NB
                s.dma(xtk[i][:], xtok_d[c], (), ["xtk%d" % i])
                for d_ in range(2):
                    s.dma(xdl[d_][i][:], xdt_d[d_][c], (), ["xdl%d_%d" % (d_, i)])
                    s.dma(hsl[d_][i][:], hst_d[d_][c], (), ["hsl%d_%d" % (d_, i)])
                s.dma(btl[i][:], BT_d[c].rearrange("p (g t) -> p g t", t=128), (), ["btl%d" % i])
                s.dma(ctl[i][:], CT_d[c].rearrange("p (g t) -> p g t", t=128), (), ["ctl%d" % i])
                s.dma(al[i][:], a_d[c], (), ["al%d" % i])

            loads(0)
            gcnt = 0
            for n, c in enumerate(chunks2):
                i = n % NB
                if n + 1 < len(chunks2):
                    loads(n + 1)
                for q4 in range(4):
                    s.dma(z2[:, q4 * 2048:(q4 + 1) * 2048], z2_d[c][:, q4 * 2048:(q4 + 1) * 2048], (), [("z2", q4)])
                for g in range(8):
                    s.mm(ps[:, g // 4, (g % 4) * 128:(g % 4 + 1) * 128], btl[i][:, g, :], ctl[i][:, g, :], True, True,
                         ["btl%d" % i, "ctl%d" % i], [B(g // 4)])
                for d_ in range(2):
                    mk = masks[d_][2]
                    for hb_ in range(2):
                        s.tt("dve", cbm[d_][:, hb_ * 4:(hb_ + 1) * 4, :],
                             ps[:, hb_, :].rearrange("p (g t) -> p g t", t=128),
                             mk[:, None, :].broadcast_to([128, 4, 128]), ALU.mult,
                             [B(hb_), "consts"], [("cbm%d" % d_, hb_)])
                for d_ in range(2):
                    m1, m2, _ = masks[d_]
                    for g in range(8):
                        gi = gcnt % 2
                        gcnt += 1
                        for r4 in range(4):
                            hcol = d_ * 32 + g * 4 + r4
                            s.ts("pool", rhs4[gi][:, r4, :], m2, al[i][:, hcol:hcol + 1], None, ALU.mult, None,
                                 ["consts", "al%d" % i], [("rhs4_%d" % gi, r4)])
                        rR = [("rhs4_%d" % gi, r4) for r4 in range(4)] + ["consts"]
                        bD = 2 + gi * 2
                        bA = 3 + gi * 2
                        rv = rhs4[gi][:].rearrange("p a b -> p (a b)")
                        s.mm(ps[:, bD, :], m1, rv, True, True, rR, [B(bD)])
                        s.mm(ps[:, bA, :], ones, rv, True, True, rR, [B(bA)])
                        s.act(E4[gi][:].rearrange("p a b -> p (a b)"), ps[:, bD, :], AF.Exp, [B(bD)], ["E4_%d" % gi])
                        s.act(F4[gi][:].rearrange("p a b -> p (a b)"), ps[:, bA, :], AF.Exp, [B(bA)], ["F4_%d" % gi])
                        s.tt("dve", M4[gi][:], E4[gi][:], cbm[d_][:, g:g + 1, :].broadcast_to([128, 4, 128]), ALU.mult,
                             ["E4_%d" % gi, ("cbm%d" % d_, g // 4)], ["M4_%d" % gi])
                        s.tt("dve", N4[gi][:], F4[gi][:], ctl[i][:, g:g + 1, :].broadcast_to([128, 4, 128]), ALU.mult,
                             ["F4_%d" % gi, "ctl%d" % i], ["N4_%d" % gi])
                        yo = ps[:, 6 + g % 2, 0:256]
                        for r4 in range(4):
                            h = g * 4 + r4
                            s.mm(yo[:, r4 * 64:(r4 + 1) * 64], M4[gi][:, r4, :], xdl[d_][i][:, h * 64:(h + 1) * 64],
                                 True, False, ["M4_%d" % gi, "xdl%d_%d" % (d_, i)], [B(6 + g % 2)])
                            s.mm(yo[:, r4 * 64:(r4 + 1) * 64], N4[gi][:, r4, :], hsl[d_][i][:, h * 64:(h + 1) * 64],
                                 False, True, ["N4_%d" % gi, "hsl%d_%d" % (d_, i)], [B(6 + g % 2)])
                        yb = ybuf[:, g * 256:(g + 1) * 256]
                        if d_ == 0:
                            s.cp("dve", yb, yo, [B(6 + g % 2)], [("ybuf", g)])
                        else:
                            s.tt("dve", yb, yo, yb, ALU.add, [B(6 + g % 2), ("ybuf", g)], [("ybuf", g)])
                yR = [("ybuf", g) for g in range(8)]
                s.tt("pool", tmp[:].rearrange("p (h e) -> p h e", e=64), xtk[i][:].rearrange("p (h e) -> p h e", e=64),
                     dskip[:, :, None].broadcast_to([128, 32, 64]), ALU.mult, ["xtk%d" % i, "dskip"], ["tmp"])
                s.tt("dve", ybuf[:], ybuf[:], tmp[:], ALU.add, yR + ["tmp"], ["ybuf"])
                s.act(tmp[:], z2[:, 0:2048], AF.Silu, [("z2", 0)], ["tmp"])
                s.tt("dve", ybuf[:], ybuf[:], tmp[:], ALU.mult, ["ybuf", "tmp"], ["ybuf"])
                s.act(junk[:], ybuf[:], AF.Square, ["ybuf"], ["junk5", "ss_a"], accum=st5[:, 0:1])
                rms_r(st5[:, 0:1], st5[:, 1:2], D, ["ss_a"], ["r_a"])
                s.stt(ycat[:, 0:2048], ybuf[:], st5[:, 1:2], gssd[:], ALU.mult, ALU.mult, ["ybuf", "r_a", "gssd"], ["ycat_a"])
                s.act(junk[:], z2[:, 4096:6144], AF.Square, [("z2", 2)], ["junk5", "ss_v"], accum=st5[:, 2:3])
                rms_r(st5[:, 2:3], st5[:, 3:4], D, ["ss_v"], ["r_v"])
                s.stt(vn[:], z2[:, 4096:6144], st5[:, 3:4], gv[:], ALU.mult, ALU.mult, [("z2", 2), "r_v", "gv"], ["vn"])
                for g in range(16):
                    bk = 2 + g // 4
                    s.mm(ps[:, bk, (g % 4) * 128:(g % 4 + 1) * 128], wsTb[:, g, :], vn[:, g * 128:(g + 1) * 128], True, True,
                         ["wsTb", "vn"], [B(bk)])
                s.tt("dve", mbuf[:].rearrange("p (g e) -> p g e", e=128), ps[:, 2:6, :].rearrange("p b (g e) -> p (b g) e", e=128),
                     bsT[:, :, None].broadcast_to([128, 16, 128]), ALU.add, [B(2), B(3), B(4), B(5), "bsT"], ["mbuf"])
                s.tt("pool", mbuf[:], mbuf[:], z2[:, 2048:4096], ALU.mult, ["mbuf", ("z2", 1)], ["mbuf"])
                s.act(tmp[:], z2[:, 6144:8192], AF.Silu, [("z2", 3)], ["tmp"])
                s.tt("dve", mbuf[:], mbuf[:], tmp[:], ALU.mult, ["mbuf", "tmp"], ["mbuf"])
                s.act(junk[:], mbuf[:], AF.Square, ["mbuf"], ["junk5", "ss_m"], accum=st5[:, 4:5])
                rms_r(st5[:, 4:5], st5[:, 5:6], D, ["ss_m"], ["r_m"])
                s.stt(ycat[:, 2048:4096], mbuf[:], st5[:, 5:6], gmlp[:], ALU.mult, ALU.mult, ["mbuf", "r_m", "gmlp"], ["ycat_b"])
                yt = yTs[n % 2]
                ytk = "yTs%d" % (n % 2)
                for kb in range(8):
                    bk = kb % 2
                    for kq in range(4):
                        k = kb * 4 + kq
                        s.mm(ps[:, bk, kq * 128:(kq + 1) * 128], ycat[:, k * 128:(k + 1) * 128], identb[:], True, True,
                             ["ycat_a" if k < 16 else "ycat_b", "identb"], [B(bk)])
                    s.cp("act" if kb % 2 == 0 else "dve", yt[:, kb * 4:(kb + 1) * 4, :],
                         ps[:, bk, :].rearrange("p (c t) -> p c t", t=128), [B(bk)], [(ytk, kb)])
                s.dma(yT_d[c].rearrange("p (k t) -> p k t", t=128), yt[:], [(ytk, k) for k in range(8)], [("yT_d", c)],
                      key="st_" + ytk)
            s.emit()
            pc[0] += 1
            if pc[0] >= STOP:
                return nc, s

        with ExitStack() as es:
            def T(name, shape, dt=F32):
                uid[0] += 1
                return es.enter_context(nc.sbuf_tensor("t%d_%s" % (uid[0], name), list(shape), dt))
            ggb = [T("ggb%d" % r, [128, D]) for r in range(2)]
            for r in range(2):
                s.dma(ggb[r][:], gg_d[r].partition_broadcast(128), (), ["ggb%d" % r])
            yTl = T("yTl", [128, 32, 512], BF16)
            wo_ = [T("wo%d" % i, [128, 32, 512], BF16) for i in range(2)]
            osb = T("osb", [128, 4, D])
            xr = [T("xr%d" % i, [128, D]) for i in range(2)]
            junk = T("junk6", [128, D], BF16)
            st6 = [T("st6_%d" % i, [128, 2]) for i in range(2)]
            sbs = [list(range(NCTX))] + [list(range(NCTX + 4 * i, NCTX + 4 * i + 4)) for i in range(NLAT // 4)]
            if last:
                sbs = sbs[1:]
            wcnt = 0
            ecnt = 0
            ccnt = 0
            for sbi, chs in enumerate(sbs):
                for ci, c in enumerate(chs):
                    s.dma(yTl[:, :, ci * 128:(ci + 1) * 128], yT_d[c].rearrange("p (k t) -> p k t", t=128), (), [("yTl", ci)])
                for jb in range(4):
                    wi_ = wcnt % 2
                    wcnt += 1
                    for hh2 in range(4):
                        s.dma(wo_[wi_][:, hh2 * 8:(hh2 + 1) * 8, :],
                              WOs_d[jb][:, hh2 * 4096:(hh2 + 1) * 4096].rearrange("p (k c) -> p k c", c=512),
                              (), [("wo%d" % wi_, hh2)])
                    for ci, c in enumerate(chs):
                        bk = ecnt % 8
                        ecnt += 1
                        o = ps[:, bk, :]
                        for k in range(32):
                            s.mm(o, yTl[:, k, ci * 128:(ci + 1) * 128], wo_[wi_][:, k, :], k == 0, k == 31,
                                 [("yTl", ci), ("wo%d" % wi_, k // 8)], [B(bk)])
                        s.cp("act" if ecnt % 2 == 0 else "dve", osb[:, ci, jb * 512:(jb + 1) * 512], o, [B(bk)], [("osb", ci, jb)])
                for ci, c in enumerate(chs):
                    i = ccnt % 2
                    ccnt += 1
                    r = 1 if c < NCTX else 0
                    oR = [("osb", ci, jb) for jb in range(4)]
                    s.dma(xr[i][:], xrow(c), (), ["xr%d" % i])
                    s.act(junk[:], osb[:, ci, :], AF.Square, oR, ["junk6", "ss6_%d" % i], accum=st6[i][:, 0:1])
                    rms_r(st6[i][:, 0:1], st6[i][:, 1:2], D, ["ss6_%d" % i], ["r6_%d" % i])
                    s.stt(osb[:, ci, :], osb[:, ci, :], st6[i][:, 1:2], ggb[r][:], ALU.mult, ALU.mult,
                          oR + ["r6_%d" % i, "ggb%d" % r], oR)
                    s.tt("pool", xr[i][:], xr[i][:], osb[:, ci, :], ALU.add, ["xr%d" % i] + oR, ["xr%d" % i])
                    s.dma(xdst(c), xr[i][:], ["xr%d" % i], [("xout", c)], key="st_xr%d" % i)
            s.emit()
            pc[0] += 1
            if pc[0] >= STOP:
                return nc, s
    return nc, s


def _consts():
    j = np.arange(128)[:, None]
    q = np.arange(128)[None, :]
    c = np.zeros((128, 6, 128), np.float32)
    c[:, 0] = (j == q)
    c[:, 1] = (j <= q)
    c[:, 2] = (j >= q)
    c[:, 3] = (j > q)
    c[:, 4] = (j < q)
    c[:, 5] = 1.0
    return c


def _prep(inp, b, layers, x_b, ctx_b):
    ls = list(layers)
    f = lambda a: np.ascontiguousarray(a, dtype=np.float32)
    cT = np.stack([inp["c"][b].reshape(16, 128).T, inp["c_ctx"].reshape(16, 128).T], axis=-1)
    b_ada = inp["b_ada"][ls]
    m = {
        "x_in": f(x_b), "ctx_in": f(ctx_b), "cT": f(cT), "consts": _consts(),
        "w_ada": f(inp["w_ada"][ls]), "w_in": f(inp["w_in"][ls]), "w_out": f(inp["w_out"][ls]),
        "badaT": f(b_ada[:, :4096].reshape(len(ls), 32, 128).transpose(0, 2, 1)),
        "bgate": f(np.repeat(b_ada[:, None, 4096:], 2, axis=1)),
        "gpost": f(np.repeat(inp["g_post"][ls][:, None, :], 2, axis=1)),
        "gpreT": f(inp["g_pre"][ls].reshape(len(ls), 16, 128).transpose(0, 2, 1)),
        "convwT": f(inp["conv_w"][ls].reshape(len(ls), 5, 32, 128).transpose(0, 3, 2, 1)),
        "convbT": f(inp["conv_b"][ls].reshape(len(ls), 32, 128).transpose(0, 2, 1)),
        "dtb": f(np.broadcast_to(inp["dt_bias"][ls].reshape(len(ls), 1, 64), (len(ls), 128, 64))),
        "alog": f(np.broadcast_to(inp["a_log"][ls].reshape(len(ls), 1, 64), (len(ls), 128, 64))),
        "dskip": f(np.broadcast_to(inp["d_skip"][ls].reshape(len(ls), 1, 32), (len(ls), 128, 32))),
        "gssd": f(inp["g_ssd"][ls]), "gv": f(inp["g_v"][ls]), "gmlp": f(inp["g_mlp"][ls]),
        "wsT": f(inp["w_s"][ls].transpose(0, 3, 1, 2)),
        "bsT": f(inp["b_s"][ls].transpose(0, 2, 1)),
    }
    return m


LAUNCH_GROUPS = [[0, 1, 2, 3]]
STOP = 10 ** 9
DEBUG = False
LAST = {}


def kernel(**inp):
    inp = {k: np.asarray(v) for k, v in inp.items()}
    x = inp["x"]
    ctx = inp["ctx"]
    nb, Lx, _ = x.shape
    NLAT = Lx // 128
    xs = [x[b] for b in range(nb)]
    cs = [ctx[b] for b in range(nb)]
    for gi, layers in enumerate(LAUNCH_GROUPS):
        final = (gi == len(LAUNCH_GROUPS) - 1)
        nc, _ = build(NLAT, layers, gi == 0, final)
        in_maps = [_prep(inp, b, layers, xs[b], cs[b]) for b in range(nb)]
        res = run_bass_kernel_spmd(nc, in_maps, core_ids=list(range(nb)))
        LAST["res"] = res
        xs = [np.asarray(res.results[b]["out"]) for b in range(nb)]
        cs = [np.asarray(res.results[b]["ctx_out"]) for b in range(nb)]
    return np.stack(xs, axis=0).astype(np.float32)
```

```python
import numpy as np
from contextlib import ExitStack
import concourse.bass as bass
import concourse.mybir as mybir
from concourse.bass_utils import run_bass_kernel_spmd

F32 = mybir.dt.float32
BF16 = mybir.dt.bfloat16
AF = mybir.ActivationFunctionType
ALU = mybir.AluOpType

D = 2048
DEPTH = 4
NCTX = 2
H = 32
XBC = 4096
INW = 12352
EPS = 1e-6
NCORES = 4


class Sch:
    ENG = ("pe", "act", "dve", "pool", "sp")

    def __init__(s, nc):
        s.nc = nc
        s.esem = {k: nc.alloc_semaphore("es_" + k) for k in ("pe", "act", "dve", "pool")}
        s.ecnt = {k: 0 for k in s.esem}
        s.pool = []
        s.poolcnt = []
        s.csems = []
        s.ninstr = 0
        s.reset()

    def reset(s):
        s.st = {k: [] for k in s.ENG}
        s.lastw = {}
        s.rd = {}
        s.seen = {k: {} for k in s.ENG}
        s.keysem = {}

    def _deps(s, eng, reads, writes):
        deps = []
        for k in reads:
            t = s.lastw.get(k)
            if t is not None:
                deps.append(t)
        for k in writes:
            t = s.lastw.get(k)
            if t is not None:
                deps.append(t)
            deps.extend(s.rd.get(k, ()))
        out = []
        seen = s.seen[eng]
        for t in deps:
            if t[0] == "E" and t[1] == eng and eng == "pe":
                continue
            kk = (t[0], t[1])
            if seen.get(kk, -1) >= t[2]:
                continue
            seen[kk] = t[2]
            out.append(t)
        return out

    def _post(s, tok, reads, writes):
        for k in writes:
            s.lastw[k] = tok
            s.rd[k] = []
        for k in reads:
            s.rd.setdefault(k, []).append(tok)

    def op(s, eng, fn, reads=(), writes=()):
        deps = s._deps(eng, reads, writes)
        idx = len(s.st[eng])
        s.st[eng].append([fn, deps, False, None])
        tok = ("E", eng, idx)
        s._post(tok, reads, writes)
        return tok

    def dma(s, out, in_, reads=(), writes=(), key=None, q="sp"):
        if key is None:
            key = writes[0]
        deps = s._deps(q, reads, writes)
        if key not in s.keysem:
            n = len(s.keysem)
            if n >= len(s.pool):
                s.pool.append(s.nc.alloc_semaphore("ds%d" % n))
                s.poolcnt.append(0)
            s.keysem[key] = n
        si = s.keysem[key]
        s.poolcnt[si] += 16
        tok = ("D", si, s.poolcnt[si])
        s.st[q].append([lambda e: e.dma_start(out=out, in_=in_), deps, False, tok])
        s._post(tok, reads, writes)
        return tok

    def coll(s, fn, reads=(), writes=()):
        deps = s._deps("pool", reads, writes)
        sem = s.nc.alloc_semaphore("cc%d" % len(s.csems))
        s.csems.append(sem)
        tok = ("C", len(s.csems) - 1, 1)
        s.st["pool"].append([fn, deps, False, tok])
        s._post(tok, reads, writes)
        return tok

    def mm(s, out, lhsT, rhs, start, stop, R, W):
        s.op("pe", lambda e: e.matmul(out, lhsT=lhsT, rhs=rhs, start=start, stop=stop), R, W)

    def tr(s, out, in_, ident, R, W):
        s.op("pe", lambda e: e.transpose(out=out, in_=in_, identity=ident), R, W)

    def act(s, out, in_, func, R, W, bias=None, scale=None, accum=None):
        def f(e):
            kw = {}
            if bias is not None:
                kw["bias"] = bias
            if scale is not None:
                kw["scale"] = scale
            if accum is not None:
                kw["accum_out"] = accum
            return e.activation(out=out, in_=in_, func=func, **kw)
        s.op("act", f, R, W)

    def tt(s, eng, out, in0, in1, op, R, W):
        s.op(eng, lambda e: e.tensor_tensor(out=out, in0=in0, in1=in1, op=op), R, W)

    def ts(s, eng, out, in0, s1, s2, op0, op1, R, W):
        if op1 is None:
            s.op(eng, lambda e: e.tensor_scalar(out=out, in0=in0, scalar1=s1, scalar2=None, op0=op0), R, W)
        else:
            s.op(eng, lambda e: e.tensor_scalar(out=out, in0=in0, scalar1=s1, scalar2=s2, op0=op0, op1=op1), R, W)

    def stt(s, out, in0, sc, in1, op0, op1, R, W):
        s.op("dve", lambda e: e.scalar_tensor_tensor(out=out, in0=in0, scalar=sc, in1=in1, op0=op0, op1=op1), R, W)

    def cp(s, eng, out, in_, R, W):
        if eng == "act":
            s.op(eng, lambda e: e.copy(out=out, in_=in_), R, W)
        else:
            s.op(eng, lambda e: e.tensor_copy(out=out, in_=in_), R, W)

    def recip(s, out, in_, R, W):
        s.op("dve", lambda e: e.reciprocal(out=out, in_=in_), R, W)

    def memset(s, eng, ap, val, W):
        s.op(eng, lambda e: e.memset(ap, val), (), W)

    def emit(s):
        nc = s.nc
        fin = [("D", si, s.poolcnt[si]) for si in sorted(s.keysem.values())]
        s.st["sp"].append([None, fin, False, None])
        needed = {k: set() for k in s.esem}
        for eng in s.ENG:
            for ent in s.st[eng]:
                for t in ent[1]:
                    if t[0] == "E":
                        needed[t[1]].add(t[2])
        vals = {}
        for eng in s.esem:
            c = s.ecnt[eng]
            v = {}
            for i in sorted(needed[eng]):
                c += 1
                v[i] = c
            vals[eng] = v
            s.ecnt[eng] = c
        engobj = dict(pe="tensor", act="scalar", dve="vector", pool="gpsimd", sp="sync")
        with nc.Block() as block:
            for eng in s.ENG:
                stream = s.st[eng]
                if not stream:
                    continue

                def body(e, eng=eng, stream=stream):
                    nd = needed.get(eng, ())
                    for i, ent in enumerate(stream):
                        for t in ent[1]:
                            if t[0] == "E":
                                e.wait_ge(s.esem[t[1]], vals[t[1]][t[2]])
                            elif t[0] == "C":
                                e.wait_ge(s.csems[t[1]], 1)
                            else:
                                e.wait_ge(s.pool[t[1]], t[2])
                        if ent[0] is None:
                            continue
                        ins = ent[0](e)
                        s.ninstr += 1
                        if ent[3] is not None and ent[3][0] == "C":
                            ins.then_inc(s.csems[ent[3][1]])
                        elif ent[3] is not None:
                            ins.then_inc(s.pool[ent[3][1]], 16)
                        elif i in nd:
                            ins.then_inc(s.esem[eng], 1)
                getattr(block, engobj[eng])(body)
        s.reset()


def build(NLAT, layers, first, last_is_final):
    C = NCTX + NLAT
    L = len(layers)
    nc = bass.Bass("TRN2", target_bir_lowering=False)

    def din(name, shape, dt=F32):
        return nc.dram_tensor(name, list(shape), dt, kind="ExternalInput").ap()

    def dscr(name, shape, dt=F32):
        return nc.dram_tensor(name, list(shape), dt, kind=("ExternalOutput" if DEBUG else "Internal")).ap()

    x_in = din("x_in", [NLAT * 128, D])
    ctx_in = din("ctx_in", [NCTX * 128, D])
    cT_in = din("cT", [128, 16, 2])
    consts_in = din("consts", [128, 6, 128])
    w_ada = din("w_ada", [L, D, 3 * D])
    w_in = din("w_in", [L, D, INW])
    w_out = din("w_out", [L, 2 * D, D])
    badaT_in = din("badaT", [L, 128, 32])
    bgate_in = din("bgate", [L, 2, D])
    gpost_in = din("gpost", [L, 2, D])
    gpreT_in = din("gpreT", [L, 128, 16])
    convw_in = din("convwT", [L, 128, 32, 5])
    convb_in = din("convbT", [L, 128, 32])
    dtb_in = din("dtb", [L, 128, 64])
    alog_in = din("alog", [L, 128, 64])
    dskip_in = din("dskip", [L, 128, 32])
    gssd_in = din("gssd", [L, D])
    gv_in = din("gv", [L, D])
    gmlp_in = din("gmlp", [L, D])
    wsT_in = din("wsT", [L, 128, 16, 128])
    bsT_in = din("bsT", [L, 128, 16])
    sel_in = din("sel", [128, 2])

    out_d = nc.dram_tensor("out", [NLAT * 128, D], F32, kind="ExternalOutput").ap()
    ctx_out = nc.dram_tensor("ctx_out", [NCTX * 128, D], F32, kind="ExternalOutput").ap()

    hxT_d = dscr("hxT_d", [C, 128, 2048], BF16)
    W1s_d = dscr("W1s_d", [32, 128, 2048], BF16)
    W1dt_d = dscr("W1dt_d", [128, 1024], BF16)
    W2s_d = dscr("W2s_d", [16, 128, 8192], BF16)
    WOs_d = dscr("WOs_d", [4, 128, 16384], BF16)
    xtok_d = dscr("xtok_d", [C, 128, 2048], BF16)
    xdt_d = dscr("xdt_d", [2, C, 128, 2048], BF16)
    BT_d = dscr("BT_d", [C, 128, 1024], BF16)
    CT_d = dscr("CT_d", [C, 128, 1024], BF16)
    a_d = dscr("a_d", [C, 128, 64])
    cdec_d = dscr("cdec_d", [C, 128, 64])
    S_d = dscr("S_d", [2, C, 128, 2048])
    hst_d = dscr("hst_d", [2, C, 128, 2048], BF16)
    z2_d = dscr("z2_d", [C, 128, 8192], BF16)
    yT_d = dscr("yT_d", [C, 128, 4096], BF16)
    gg_d = dscr("gg_d", [2, D])
    send_d = [nc.dram_tensor("send_d%d" % i, [128, 2048], F32, kind="Internal").ap() for i in range(L)]
    recv_d = [nc.dram_tensor("recv_d%d" % i, [256, 2048], F32, kind="Internal").ap() for i in range(L)]

    s = Sch(nc)
    pc = [0]

    uid = [0]

    def sb(name, shape, dt=F32):
        return nc.alloc_sbuf_tensor("g_" + name, list(shape), dt)

    consts = sb("consts", [128, 6, 128])
    ident = consts[:, 0, :]
    Um = consts[:, 1, :]
    Lm = consts[:, 2, :]
    SLm = consts[:, 3, :]
    SUm = consts[:, 4, :]
    ones = consts[:, 5, :]
    identb = sb("identb", [128, 128], BF16)
    sT = sb("sT", [128, 16, 2])
    G1 = sb("G1", [128, 16, 2])
    shT = sb("shT", [128, 16, 2])
    adaT = sb("adaT", [128, 32, 2])
    badaT = sb("badaT", [128, 32])
    gpreT = sb("gpreT", [128, 16])
    convw = sb("convw", [128, 32, 5])
    convb = sb("convb", [128, 32])
    dtb = sb("dtb", [128, 64])
    Abc = sb("Abc", [128, 64])
    dskip = sb("dskip", [128, 32])
    bsT = sb("bsT", [128, 16])
    wsTb = sb("wsTb", [128, 16, 128], BF16)
    epst = sb("epst", [128, 1])
    selt = sb("selt", [128, 2])
    ps = nc.alloc_psum_tensor("ps", [128, 8, 512], F32)

    def B(i):
        return "ps%d" % i

    s.dma(consts[:], consts_in, (), ["consts"])
    s.dma(sT[:], cT_in, (), ["sT"])
    s.dma(selt[:], sel_in, (), ["selt"])
    s.cp("dve", identb[:], ident, ["consts"], ["identb"])
    s.act(sT[:], sT[:], AF.Silu, ["sT"], ["sT"])
    s.memset("dve", epst[:], EPS, ["epst"])
    s.emit()

    def rms_r(ssum, rr, n, R, W):
        s.ts("dve", rr, ssum, 1.0 / n, EPS, ALU.mult, ALU.add, R, W)
        s.act(rr, rr, AF.Sqrt, W, W)
        s.recip(rr, rr, W, W)

    for li, l in enumerate(layers):
        last = last_is_final and (li == L - 1)
        x_src = x_in if (li == 0) else out_d
        ctx_src = ctx_in if (li == 0) else ctx_out
        chunks2 = list(range(NCTX, C)) if last else list(range(C))

        def xrow(c, src_x=x_src, src_c=ctx_src):
            if c < NCTX:
                return src_c[c * 128:(c + 1) * 128, :]
            return src_x[(c - NCTX) * 128:(c - NCTX + 1) * 128, :]

        def xdst(c):
            if c < NCTX:
                return ctx_out[c * 128:(c + 1) * 128, :]
            return out_d[(c - NCTX) * 128:(c - NCTX + 1) * 128, :]

        with ExitStack() as es:
            def T(name, shape, dt=F32):
                uid[0] += 1
                return es.enter_context(nc.sbuf_tensor("t%d_%s" % (uid[0], name), list(shape), dt))
            s.dma(badaT[:], badaT_in[li], (), ["badaT"])
            s.dma(gpreT[:], gpreT_in[li], (), ["gpreT"])
            s.dma(convw[:], convw_in[li], (), ["convw"])
            s.dma(convb[:], convb_in[li], (), ["convb"])
            s.dma(dtb[:], dtb_in[li], (), ["dtb"])
            s.dma(Abc[:], alog_in[li], (), ["Abc"])
            s.dma(dskip[:], dskip_in[li], (), ["dskip"])
            s.dma(bsT[:], bsT_in[li], (), ["bsT"])
            wsf = T("wsf", [128, 16, 128])
            s.dma(wsf[:], wsT_in[li], (), ["wsf"])
            s.cp("dve", wsTb[:], wsf[:], ["wsf"], ["wsTb"])
            s.act(Abc[:], Abc[:], AF.Exp, ["Abc"], ["Abc"])
            s.ts("dve", Abc[:], Abc[:], -1.0, None, ALU.mult, None, ["Abc"], ["Abc"])
            bg = T("bg", [2, D])
            gp = T("gp", [2, D])
            gsb = T("gsb", [2, D])
            s.dma(bg[:], bgate_in[li], (), ["bg"])
            s.dma(gp[:], gpost_in[li], (), ["gp"])
            wblk = [T("wblk%d" % i, [128, 16, 512]) for i in range(2)]
            wa = w_ada[li].rearrange("(k p) c -> p k c", p=128)
            for jb in range(12):
                wb = wblk[jb % 2]
                wk = "wblk%d" % (jb % 2)
                for hh in range(2):
                    s.dma(wb[:, hh * 8:(hh + 1) * 8, :], wa[:, hh * 8:(hh + 1) * 8, jb * 512:(jb + 1) * 512],
                          (), [wk + "h%d" % hh])
                Rw = [wk + "h0", wk + "h1", "sT"]
                if jb < 8:
                    for t in range(4):
                        o = ps[:, 0, (jb * 4 + t) * 2:(jb * 4 + t) * 2 + 2]
                        for k in range(16):
                            s.mm(o, wb[:, k, t * 128:(t + 1) * 128], sT[:, k, :], k == 0, k == 15, Rw, [B(0)])
                else:
                    o = ps[0:2, jb - 7, :]
                    for k in range(16):
                        s.mm(o, sT[:, k, :], wb[:, k, :], k == 0, k == 15, Rw, [B(jb - 7)])
                    s.tt("dve", gsb[:, (jb - 8) * 512:(jb - 7) * 512], o, bg[:, (jb - 8) * 512:(jb - 7) * 512],
                         ALU.add, [B(jb - 7), "bg"], ["gsb"])
            s.tt("dve", adaT[:], ps[:, 0, 0:64].rearrange("p (j r) -> p j r", r=2),
                 badaT[:, :, None].broadcast_to([128, 32, 2]), ALU.add, [B(0), "badaT"], ["adaT"])
            s.cp("dve", shT[:], adaT[:, 0:16, :], ["adaT"], ["shT"])
            s.ts("dve", G1[:], adaT[:, 16:32, :], 1.0, None, ALU.add, None, ["adaT"], ["G1"])
            s.tt("dve", G1[:], G1[:], gpreT[:, :, None].broadcast_to([128, 16, 2]), ALU.mult, ["G1", "gpreT"], ["G1"])
            s.tt("dve", gsb[:], gsb[:], gp[:], ALU.mult, ["gsb", "gp"], ["gsb"])
            s.dma(gg_d, gsb[:], ["gsb"], ["gg_d"])
            stf = [T("stf%d" % i, [128, 4096]) for i in range(2)]
            stb = [T("stb%d" % i, [128, 4096], BF16) for i in range(2)]
            wi = w_in[li].rearrange("(k p) c -> p k c", p=128)
            wo = w_out[li].rearrange("(k p) c -> p k c", p=128)
            jobs = []
            for j in range(32):
                jobs.append((wi[:, :, j * 128:(j + 1) * 128], [16, 128], W1s_d[j]))
            jobs.append((wi[:, :, 4096:4160], [16, 64], W1dt_d))
            for jb in range(16):
                for hh in range(2):
                    jobs.append((wi[:, hh * 8:(hh + 1) * 8, 4160 + jb * 512:4160 + (jb + 1) * 512], [8, 512],
                                 W2s_d[jb][:, hh * 4096:(hh + 1) * 4096]))
            for jb in range(4):
                for hh in range(4):
                    jobs.append((wo[:, hh * 8:(hh + 1) * 8, jb * 512:(jb + 1) * 512], [8, 512],
                                 WOs_d[jb][:, hh * 4096:(hh + 1) * 4096]))
            cast_engs = ["act", "pool", "dve"]
            for n, (src, shp, dst) in enumerate(jobs):
                i = n % 2
                ne = shp[0] * shp[1]
                fv = stf[i][:, 0:ne].rearrange("p (a b) -> p a b", b=shp[1])
                s.dma(fv, src, (), ["stf%d" % i])
                s.cp(cast_engs[n % 3], stb[i][:, 0:ne], stf[i][:, 0:ne], ["stf%d" % i], ["stb%d" % i])
                s.dma(dst, stb[i][:, 0:ne], ["stb%d" % i], [("w", n)], key="st_stb%d" % i)
            s.emit()
            pc[0] += 1
            if pc[0] >= STOP:
                return nc, s

        with ExitStack() as es:
            def T(name, shape, dt=F32):
                uid[0] += 1
                return es.enter_context(nc.sbuf_tensor("t%d_%s" % (uid[0], name), list(shape), dt))
            xin = [T("xin%d" % i, [128, D]) for i in range(2)]
            xn = [T("xn%d" % i, [128, D]) for i in range(2)]
            junk = T("junk", [128, D], BF16)
            ssr = [T("ssr%d" % i, [128, 2]) for i in range(2)]
            hxs = [T("hxs%d" % i, [128, 16, 128], BF16) for i in range(2)]
            for c in range(C):
                i = c % 2
                r = 1 if c < NCTX else 0
                s.dma(xin[i][:], xrow(c), (), ["xin%d" % i])
                s.act(junk[:], xin[i][:], AF.Square, ["xin%d" % i], ["junk", "ss%d" % i], accum=ssr[i][:, 0:1])
                rms_r(ssr[i][:, 0:1], ssr[i][:, 1:2], D, ["ss%d" % i], ["rr%d" % i])
                s.ts("dve", xn[i][:], xin[i][:], ssr[i][:, 1:2], None, ALU.mult, None, ["xin%d" % i, "rr%d" % i], ["xn%d" % i])
                for kb in range(4):
                    bk = (c % 2) * 4 + kb
                    for kq in range(4):
                        k = kb * 4 + kq
                        s.mm(ps[:, bk, kq * 128:(kq + 1) * 128], xn[i][:, k * 128:(k + 1) * 128], ident, True, True,
                             ["xn%d" % i, "consts"], [B(bk)])
                    for kq in range(4):
                        k = kb * 4 + kq
                        s.ts("dve", hxs[i][:, k, :], ps[:, bk, kq * 128:(kq + 1) * 128], G1[:, k, r:r + 1], shT[:, k, r:r + 1],
                             ALU.mult, ALU.add, [B(bk), "G1", "shT"], [("hxs%d" % i, k)])
                s.dma(hxT_d[c].rearrange("p (k t) -> p k t", t=128), hxs[i][:],
                      [("hxs%d" % i, k) for k in range(16)], [("hxT_d", c)], key="st_hxs%d" % i)
            s.emit()
            pc[0] += 1
            if pc[0] >= STOP:
                return nc, s

        with ExitStack() as es:
            def T(name, shape, dt=F32):
                uid[0] += 1
                return es.enter_context(nc.sbuf_tensor("t%d_%s" % (uid[0], name), list(shape), dt))
            hx = [T("hx%d" % i, [128, 16, 512], BF16) for i in range(2)]
            w1 = [T("w1_%d" % i, [128, 16, 128], BF16) for i in range(3)]
            w1dt = T("w1dt", [128, 16, 64], BF16)
            acc = [T("acc%d" % i, [128, 512]) for i in range(2)]
            xbcj = [T("xbcj%d" % i, [128, 512], BF16) for i in range(2)]
            btsb = T("btsb", [128, 8, 512], BF16)
            ctsb = T("ctsb", [128, 8, 512], BF16)
            xtok = T("xtok", [128, 4, 2048], BF16)
            btok = T("btok", [128, 4, 1024], BF16)
            dtv = [T("dtv%d" % i, [128, 64]) for i in range(2)]
            asb = [T("asb%d" % i, [128, 64]) for i in range(2)]
            dec = [T("dec%d" % i, [128, 64]) for i in range(2)]
            cdec = [T("cdec%d" % i, [128, 64]) for i in range(2)]
            xdt = [[T("xdt%d_%d" % (d_, i), [128, 2048], BF16) for i in range(2)] for d_ in range(2)]
            xw = [T("xw%d" % d_, [128, 2048], BF16) for d_ in range(2)]
            Ssb = [T("Ssb%d" % i, [128, 2048]) for i in range(2)]
            s.dma(w1dt[:], W1dt_d.rearrange("p (k c) -> p k c", c=64), (), ["w1dt"])
            sbs = [list(range(NCTX))] + [list(range(NCTX + 4 * i, NCTX + 4 * i + 4)) for i in range(NLAT // 4)]
            wcnt = 0
            ccnt = 0
            for sbi, chs in enumerate(sbs):
                Tn = 128 * len(chs)
                rowlen = 256 if sbi == 0 else 64
                nrows = Tn // rowlen
                hb = hx[sbi % 2]
                hk = "hx%d" % (sbi % 2)
                for ci, c in enumerate(chs):
                    s.dma(hb[:, :, ci * 128:(ci + 1) * 128], hxT_d[c].rearrange("p (k t) -> p k t", t=128),
                          (), [(hk, ci)])
                hR = [(hk, ci) for ci in range(len(chs))]
                for j in range(32):
                    wi_ = wcnt % 3
                    wcnt += 1
                    s.dma(w1[wi_][:], W1s_d[j].rearrange("p (k c) -> p k c", c=128), (), ["w1_%d" % wi_])
                    gb = j % 2
                    o = ps[:, gb, 0:Tn]
                    for k in range(16):
                        s.mm(o, w1[wi_][:, k, :], hb[:, k, 0:Tn], k == 0, k == 15, ["w1_%d" % wi_] + hR, [B(gb)])
                    a_ = acc[gb]
                    ak = "acc%d" % gb
                    s.ts("dve", a_[:, 0:Tn], o, convw[:, j, 2:3], convb[:, j:j + 1], ALU.mult, ALU.add,
                         [B(gb), "convw", "convb"], [ak])
                    ov = o.rearrange("p (r t) -> p r t", t=rowlen)
                    av = a_[:, 0:Tn].rearrange("p (r t) -> p r t", t=rowlen)
                    for kk in (0, 1, 3, 4):
                        sh = kk - 2
                        if sh < 0:
                            src = ov[:, :, 0:rowlen + sh]
                            dst = av[:, :, -sh:rowlen]
                        else:
                            src = ov[:, :, sh:rowlen]
                            dst = av[:, :, 0:rowlen - sh]
                        s.stt(dst, src, convw[:, j, kk:kk + 1], dst, ALU.mult, ALU.add, [B(gb), "convw", ak], [ak])
                    if j < 16:
                        tgt = xbcj[gb][:, 0:Tn]
                        tk = "xbcj%d" % gb
                    elif j < 24:
                        tgt = btsb[:, j - 16, 0:Tn]
                        tk = ("btsb", j - 16)
                    else:
                        tgt = ctsb[:, j - 24, 0:Tn]
                        tk = ("ctsb", j - 24)
                    s.act(tgt, a_[:, 0:Tn], AF.Silu, [ak], [tk])
                    if j < 24:
                        nch = len(chs)
                        for ci in range(nch):
                            s.mm(ps[:, 2, ci * 128:(ci + 1) * 128], tgt[:, ci * 128:(ci + 1) * 128], identb[:], True, True,
                                 [tk, "identb"], [B(2)])
                        pv = ps[:, 2, 0:nch * 128].rearrange("p (c t) -> p c t", t=128)
                        if j < 16:
                            s.cp("act", xtok[:, 0:nch, j * 128:(j + 1) * 128], pv, [B(2)], [("xtok", j)])
                        else:
                            s.cp("act", btok[:, 0:nch, (j - 16) * 128:(j - 15) * 128], pv, [B(2)], [("btok", j - 16)])
                for ci, c in enumerate(chs):
                    i = ccnt % 2
                    ccnt += 1
                    o = ps[:, 3, 0:64]
                    for k in range(16):
                        s.mm(o, hb[:, k, ci * 128:(ci + 1) * 128], w1dt[:, k, :], k == 0, k == 15, [(hk, ci), "w1dt"], [B(3)])
                    dk_ = "dtv%d" % i
                    s.tt("dve", dtv[i][:], o, dtb[:], ALU.add, [B(3), "dtb"], [dk_])
                    s.act(dtv[i][:], dtv[i][:], AF.Exp, [dk_], [dk_])
                    s.ts("dve", dtv[i][:], dtv[i][:], 1.0, None, ALU.add, None, [dk_], [dk_])
                    s.act(dtv[i][:], dtv[i][:], AF.Ln, [dk_], [dk_])
                    s.tt("dve", asb[i][:], dtv[i][:], Abc[:], ALU.mult, [dk_, "Abc"], ["asb%d" % i])
                    o2 = ps[:, 3, 64:192]
                    s.mm(o2[:, 0:32], SLm, asb[i][:, 0:32], True, True, ["consts", "asb%d" % i], [B(3)])
                    s.mm(o2[:, 32:64], SUm, asb[i][:, 32:64], True, True, ["consts", "asb%d" % i], [B(3)])
                    s.mm(o2[:, 64:128], ones, asb[i][:, 0:64], True, True, ["consts", "asb%d" % i], [B(3)])
                    s.act(dec[i][:], o2[:, 0:64], AF.Exp, [B(3)], ["dec%d" % i])
                    s.act(cdec[i][:], o2[:, 64:128], AF.Exp, [B(3)], ["cdec%d" % i])
                    xR = [("xtok", j) for j in range(16)]
                    xv = xtok[:, ci, :].rearrange("p (h e) -> p h e", e=64)
                    for d_ in range(2):
                        xd = xdt[d_][i]
                        s.tt("dve" if d_ == 0 else "pool", xd[:].rearrange("p (h e) -> p h e", e=64), xv,
                             dtv[i][:, d_ * 32:(d_ + 1) * 32, None].broadcast_to([128, 32, 64]), ALU.mult,
                             xR + [dk_], ["xdt%d_%d" % (d_, i)])
                        s.tt("dve" if d_ == 0 else "pool", xw[d_][:].rearrange("p (h e) -> p h e", e=64),
                             xd[:].rearrange("p (h e) -> p h e", e=64),
                             dec[i][:, d_ * 32:(d_ + 1) * 32, None].broadcast_to([128, 32, 64]), ALU.mult,
                             ["xdt%d_%d" % (d_, i), "dec%d" % i], ["xw%d" % d_])
                        sk = "Ssb%d" % d_
                        for hf in range(2):
                            b0 = 4 + 2 * hf
                            for gq in range(4):
                                g = hf * 4 + gq
                                s.mm(ps[:, b0 + gq // 2, (gq % 2) * 256:(gq % 2 + 1) * 256], btok[:, ci, g * 128:(g + 1) * 128],
                                     xw[d_][:, g * 256:(g + 1) * 256], True, True,
                                     [("btok", g), "xw%d" % d_], [B(b0 + gq // 2)])
                            s.cp("act", Ssb[d_][:, hf * 1024:(hf + 1) * 1024].rearrange("p (b f) -> p b f", f=512),
                                 ps[:, b0:b0 + 2, :], [B(b0), B(b0 + 1)], [(sk, hf)])
                        s.dma(S_d[d_][c], Ssb[d_][:], [(sk, 0), (sk, 1)], [("S_d", d_, c)], key="st_" + sk)
                        s.dma(xdt_d[d_][c], xd[:], ["xdt%d_%d" % (d_, i)], [("xdt_d", d_, c)], key="st_xdt%d_%d" % (d_, i))
                    s.dma(xtok_d[c], xtok[:, ci, :], xR, [("xtok_d", c)], key="st_xtok%d" % ci)
                    s.dma(BT_d[c].rearrange("p (g t) -> p g t", t=128), btsb[:, :, ci * 128:(ci + 1) * 128],
                          [("btsb", g) for g in range(8)], [("BT_d", c)], key="st_bt%d" % ci)
                    s.dma(CT_d[c].rearrange("p (g t) -> p g t", t=128), ctsb[:, :, ci * 128:(ci + 1) * 128],
                          [("ctsb", g) for g in range(8)], [("CT_d", c)], key="st_ct%d" % ci)
                    s.dma(a_d[c], asb[i][:], ["asb%d" % i], [("a_d", c)], key="st_asb%d" % i)
                    s.dma(cdec_d[c], cdec[i][:], ["cdec%d" % i], [("cdec_d", c)], key="st_cdec%d" % i)
            s.emit()
            pc[0] += 1
            if pc[0] >= STOP:
                return nc, s

        with ExitStack() as es:
            def T(name, shape, dt=F32):
                uid[0] += 1
                return es.enter_context(nc.sbuf_tensor("t%d_%s" % (uid[0], name), list(shape), dt))
            hh_ = [T("h%d" % d_, [128, 2048]) for d_ in range(2)]
            Sl = [[T("Sl%d_%d" % (d_, i), [128, 2048]) for i in range(2)] for d_ in range(2)]
            cdl = [[T("cdl%d_%d" % (d_, i), [128, 64]) for i in range(2)] for d_ in range(2)]
            hbf = [[T("hbf%d_%d" % (d_, i), [128, 2048], BF16) for i in range(2)] for d_ in range(2)]
            Rv = [T("Rv%d" % i, [128, 2048]) for i in range(2)]
            cnts = [0, 0]

            def step(d_, c, e_):
                i = cnts[d_] % 2
                cnts[d_] += 1
                hk = "h%d" % d_
                s.dma(Sl[d_][i][:], S_d[d_][c], (), ["Sl%d_%d" % (d_, i)])
                s.dma(cdl[d_][i][:], cdec_d[c], (), ["cdl%d_%d" % (d_, i)])
                s.cp("act", hbf[d_][i][:], hh_[d_][:], [hk], ["hbf%d_%d" % (d_, i)])
                s.dma(hst_d[d_][c], hbf[d_][i][:], ["hbf%d_%d" % (d_, i)], [("hst_d", d_, c)], key="st_hbf%d_%d" % (d_, i))
                hv = hh_[d_][:].rearrange("p (h e) -> p h e", e=64)
                s.tt(e_, hv, hv, cdl[d_][i][:, d_ * 32:(d_ + 1) * 32, None].broadcast_to([128, 32, 64]), ALU.mult,
                     [hk, "cdl%d_%d" % (d_, i)], [hk])
                s.tt(e_, hh_[d_][:], hh_[d_][:], Sl[d_][i][:], ALU.add, [hk, "Sl%d_%d" % (d_, i)], [hk])

            s.memset("dve", hh_[0][:], 0.0, ["h0"])
            s.memset("pool", hh_[1][:], 0.0, ["h1"])
            for c in (1, 0):
                step(1, c, "pool")
            for c in range(C):
                step(0, c, "dve")
            s.dma(send_d[li], hh_[0][:], ["h0"], ["send_d"], key="st_send")
            s.coll(lambda e, a=send_d[li], b=recv_d[li]: e.collective_compute(
                "AllGather", ALU.bypass, replica_groups=[[0, 1], [2, 3], [4, 5], [6, 7]], ins=[a], outs=[b]),
                ["send_d"], ["recv_d"])
            for r_ in range(2):
                s.dma(Rv[r_][:], recv_d[li][r_ * 128:(r_ + 1) * 128, :], ["recv_d"], ["Rv%d" % r_])
            s.ts("dve", hh_[1][:], Rv[0][:], selt[:, 0:1], None, ALU.mult, None, ["Rv0", "selt", "h1"], ["h1"])
            s.stt(hh_[1][:], Rv[1][:], selt[:, 1:2], hh_[1][:], ALU.mult, ALU.add, ["Rv1", "selt", "h1"], ["h1"])
            for c in range(C - 1, NCTX - 1, -1):
                step(1, c, "dve")
            s.emit()
            pc[0] += 1
            if pc[0] >= STOP:
                return nc, s

        with ExitStack() as es:
            def T(name, shape, dt=F32):
                uid[0] += 1
                return es.enter_context(nc.sbuf_tensor("t%d_%s" % (uid[0], name), list(shape), dt))
            hx = [T("hx%d" % i, [128, 16, 1024], BF16) for i in range(2)]
            w2 = [T("w2_%d" % i, [128, 16, 512], BF16) for i in range(3)]
            zst = [T("zst%d" % i, [128, 512], BF16) for i in range(4)]
            sbs = [list(range(NCTX))] + [list(range(NCTX + 8 * i, NCTX + 8 * i + 8)) for i in range(NLAT // 8)]
            if last:
                sbs = sbs[1:]
            wcnt = 0
            ecnt = 0
            for sbi, chs in enumerate(sbs):
                hb = hx[sbi % 2]
                hk = "hx%d" % (sbi % 2)
                for ci, c in enumerate(chs):
                    s.dma(hb[:, :, ci * 128:(ci + 1) * 128], hxT_d[c].rearrange("p (k t) -> p k t", t=128), (), [(hk, ci)])
                for jb in range(16):
                    wi_ = wcnt % 3
                    wcnt += 1
                    for hh2 in range(2):
                        s.dma(w2[wi_][:, hh2 * 8:(hh2 + 1) * 8, :],
                              W2s_d[jb][:, hh2 * 4096:(hh2 + 1) * 4096].rearrange("p (k c) -> p k c", c=512),
                              (), [("w2_%d" % wi_, hh2)])
                    for ci, c in enumerate(chs):
                        bk = ecnt % 8
                        zi = ecnt % 4
                        ecnt += 1
                        o = ps[:, bk, :]
                        for k in range(16):
                            s.mm(o, hb[:, k, ci * 128:(ci + 1) * 128], w2[wi_][:, k, :], k == 0, k == 15,
                                 [(hk, ci), ("w2_%d" % wi_, k // 8)], [B(bk)])
                        s.cp("act" if ecnt % 2 == 0 else "dve", zst[zi][:], o, [B(bk)], ["zst%d" % zi])
                        s.dma(z2_d[c][:, jb * 512:(jb + 1) * 512], zst[zi][:], ["zst%d" % zi], [("z2_d", c, jb)],
                              key="st_zst%d" % zi)
            s.emit()
            pc[0] += 1
            if pc[0] >= STOP:
                return nc, s

        with ExitStack() as es:
            def T(name, shape, dt=F32):
                uid[0] += 1
                return es.enter_context(nc.sbuf_tensor("t%d_%s" % (uid[0], name), list(shape), dt))
            gssd = T("gssd", [128, D])
            gv = T("gv", [128, D])
            gmlp = T("gmlp", [128, D])
            s.dma(gssd[:], gssd_in[li].partition_broadcast(128), (), ["gssd"])
            s.dma(gv[:], gv_in[li].partition_broadcast(128), (), ["gv"])
            s.dma(gmlp[:], gmlp_in[li].partition_broadcast(128), (), ["gmlp"])
            z2 = T("z2", [128, 8192], BF16)
            NB = 2
            xtk = [T("xtk%d" % i, [128, 2048], BF16) for i in range(NB)]
            xdl = [[T("xdl%d_%d" % (d_, i), [128, 2048], BF16) for i in range(NB)] for d_ in range(2)]
            hsl = [[T("hsl%d_%d" % (d_, i), [128, 2048], BF16) for i in range(NB)] for d_ in range(2)]
            btl = [T("btl%d" % i, [128, 8, 128], BF16) for i in range(NB)]
            ctl = [T("ctl%d" % i, [128, 8, 128], BF16) for i in range(NB)]
            al = [T("al%d" % i, [128, 64]) for i in range(NB)]
            cbm = [T("cbm%d" % d_, [128, 8, 128]) for d_ in range(2)]
            rhs4 = [T("rhs4_%d" % i, [128, 4, 128]) for i in range(2)]
            E4 = [T("E4_%d" % i, [128, 4, 128]) for i in range(2)]
            F4 = [T("F4_%d" % i, [128, 4, 128]) for i in range(2)]
            M4 = [T("M4_%d" % i, [128, 4, 128], BF16) for i in range(2)]
            N4 = [T("N4_%d" % i, [128, 4, 128], BF16) for i in range(2)]
            ybuf = T("ybuf", [128, D])
            tmp = T("tmp", [128, D])
            mbuf = T("mbuf", [128, D])
            vn = T("vn", [128, D], BF16)
            ycat = T("ycat", [128, 2 * D], BF16)
            yTs = [T("yTs%d" % i, [128, 32, 128], BF16) for i in range(2)]
            junk = T("junk5", [128, D], BF16)
            st5 = T("st5", [128, 8])
            masks = [(SLm, Um, Um), (SUm, Lm, Lm)]

            def loads(n):
                c = chunks2[n]
                i = n % NB
                s.dma(xtk[i][:], xtok_d[c], (), ["xtk%d" % i])
                for d_ in range(2):
                    s.dma(xdl[d_][i][:], xdt_d[d_][c], (), ["xdl%d_%d" % (d_, i)])
                    s.dma(hsl[d_][i][:], hst_d[d_][c], (), ["hsl%d_%d" % (d_, i)])
                s.dma(btl[i][:], BT_d[c].rearrange("p (g t) -> p g t", t=128), (), ["btl%d" % i])
                s.dma(ctl[i][:], CT_d[c].rearrange("p (g t) -> p g t", t=128), (), ["ctl%d" % i])
                s.dma(al[i][:], a_d[c], (), ["al%d" % i])

            loads(0)
            gcnt = 0
            for n, c in enumerate(chunks2):
                i = n % NB
                if n + 1 < len(chunks2):
                    loads(n + 1)
                for q4 in range(4):
                    s.dma(z2[:, q4 * 2048:(q4 + 1) * 2048], z2_d[c][:, q4 * 2048:(q4 + 1) * 2048], (), [("z2", q4)])
                for g in range(8):
                    s.mm(ps[:, g // 4, (g % 4) * 128:(g % 4 + 1) * 128], btl[i][:, g, :], ctl[i][:, g, :], True, True,
                         ["btl%d" % i, "ctl%d" % i], [B(g // 4)])
                for d_ in range(2):
                    mk = masks[d_][2]
                    for hb_ in range(2):
                        s.tt("dve", cbm[d_][:, hb_ * 4:(hb_ + 1) * 4, :],
                             ps[:, hb_, :].rearrange("p (g t) -> p g t", t=128),
                             mk[:, None, :].broadcast_to([128, 4, 128]), ALU.mult,
                             [B(hb_), "consts"], [("cbm%d" % d_, hb_)])
                for d_ in range(2):
                    m1, m2, _ = masks[d_]
                    for g in range(8):
                        gi = gcnt % 2
                        gcnt += 1
                        hc0 = d_ * 32 + g * 4
                        s.tt("dve", rhs4[gi][:], m2[:, None, :].broadcast_to([128, 4, 128]),
                             al[i][:, hc0:hc0 + 4, None].broadcast_to([128, 4, 128]), ALU.mult,
                             ["consts", "al%d" % i], ["rhs4_%d" % gi])
                        rR = ["rhs4_%d" % gi, "consts"]
                        bD = 2 + gi * 2
                        bA = 3 + gi * 2
                        rv = rhs4[gi][:].rearrange("p a b -> p (a b)")
                        s.mm(ps[:, bD, :], m1, rv, True, True, rR, [B(bD)])
                        s.mm(ps[:, bA, :], ones, rv, True, True, rR, [B(bA)])
                        s.act(E4[gi][:].rearrange("p a b -> p (a b)"), ps[:, bD, :], AF.Exp, [B(bD)], ["E4_%d" % gi])
                        s.act(F4[gi][:].rearrange("p a b -> p (a b)"), ps[:, bA, :], AF.Exp, [B(bA)], ["F4_%d" % gi])
                        s.tt("dve", M4[gi][:], E4[gi][:], cbm[d_][:, g:g + 1, :].broadcast_to([128, 4, 128]), ALU.mult,
                             ["E4_%d" % gi, ("cbm%d" % d_, g // 4)], ["M4_%d" % gi])
                        s.tt("dve", N4[gi][:], F4[gi][:], ctl[i][:, g:g + 1, :].broadcast_to([128, 4, 128]), ALU.mult,
                             ["F4_%d" % gi, "ctl%d" % i], ["N4_%d" % gi])
                        yo = ps[:, 6 + g % 2, 0:256]
                        for r4 in range(4):
                            h = g * 4 + r4
                            s.mm(yo[:, r4 * 64:(r4 + 1) * 64], M4[gi][:, r4, :], xdl[d_][i][:, h * 64:(h + 1) * 64],
                                 True, False, ["M4_%d" % gi, "xdl%d_%d" % (d_, i)], [B(6 + g % 2)])
                            s.mm(yo[:, r4 * 64:(r4 + 1) * 64], N4[gi][:, r4, :], hsl[d_][i][:, h * 64:(h + 1) * 64],
                                 False, True, ["N4_%d" % gi, "hsl%d_%d" % (d_, i)], [B(6 + g % 2)])
                        yb = ybuf[:, g * 256:(g + 1) * 256]
                        if d_ == 0:
                            s.cp("dve", yb, yo, [B(6 + g % 2)], [("ybuf", g)])
                        else:
                            s.tt("dve", yb, yo, yb, ALU.add, [B(6 + g % 2), ("ybuf", g)], [("ybuf", g)])
                yR = [("ybuf", g) for g in range(8)]
                s.tt("pool", tmp[:].rearrange("p (h e) -> p h e", e=64), xtk[i][:].rearrange("p (h e) -> p h e", e=64),
                     dskip[:, :, None].broadcast_to([128, 32, 64]), ALU.mult, ["xtk%d" % i, "dskip"], ["tmp"])
                s.tt("dve", ybuf[:], ybuf[:], tmp[:], ALU.add, yR + ["tmp"], ["ybuf"])
                s.act(tmp[:], z2[:, 0:2048], AF.Silu, [("z2", 0)], ["tmp"])
                s.tt("dve", ybuf[:], ybuf[:], tmp[:], ALU.mult, ["ybuf", "tmp"], ["ybuf"])
                s.act(junk[:], ybuf[:], AF.Square, ["ybuf"], ["junk5", "ss_a"], accum=st5[:, 0:1])
                rms_r(st5[:, 0:1], st5[:, 1:2], D, ["ss_a"], ["r_a"])
                s.stt(ycat[:, 0:2048], ybuf[:], st5[:, 1:2], gssd[:], ALU.mult, ALU.mult, ["ybuf", "r_a", "gssd"], ["ycat_a"])
                s.act(junk[:], z2[:, 4096:6144], AF.Square, [("z2", 2)], ["junk5", "ss_v"], accum=st5[:, 2:3])
                rms_r(st5[:, 2:3], st5[:, 3:4], D, ["ss_v"], ["r_v"])
                s.stt(vn[:], z2[:, 4096:6144], st5[:, 3:4], gv[:], ALU.mult, ALU.mult, [("z2", 2), "r_v", "gv"], ["vn"])
                for g in range(16):
                    bk = 2 + g // 4
                    s.mm(ps[:, bk, (g % 4) * 128:(g % 4 + 1) * 128], wsTb[:, g, :], vn[:, g * 128:(g + 1) * 128], True, True,
                         ["wsTb", "vn"], [B(bk)])
                s.tt("dve", mbuf[:].rearrange("p (g e) -> p g e", e=128), ps[:, 2:6, :].rearrange("p b (g e) -> p (b g) e", e=128),
                     bsT[:, :, None].broadcast_to([128, 16, 128]), ALU.add, [B(2), B(3), B(4), B(5), "bsT"], ["mbuf"])
                s.tt("pool", mbuf[:], mbuf[:], z2[:, 2048:4096], ALU.mult, ["mbuf", ("z2", 1)], ["mbuf"])
                s.act(tmp[:], z2[:, 6144:8192], AF.Silu, [("z2", 3)], ["tmp"])
                s.tt("dve", mbuf[:], mbuf[:], tmp[:], ALU.mult, ["mbuf", "tmp"], ["mbuf"])
                s.act(junk[:], mbuf[:], AF.Square, ["mbuf"], ["junk5", "ss_m"], accum=st5[:, 4:5])
                rms_r(st5[:, 4:5], st5[:, 5:6], D, ["ss_m"], ["r_m"])
                s.stt(ycat[:, 2048:4096], mbuf[:], st5[:, 5:6], gmlp[:], ALU.mult, ALU.mult, ["mbuf", "r_m", "gmlp"], ["ycat_b"])
                yt = yTs[n % 2]
                ytk = "yTs%d" % (n % 2)
                for kb in range(8):
                    bk = kb % 2
                    for kq in range(4):
                        k = kb * 4 + kq
                        s.mm(ps[:, bk, kq * 128:(kq + 1) * 128], ycat[:, k * 128:(k + 1) * 128], identb[:], True, True,
                             ["ycat_a" if k < 16 else "ycat_b", "identb"], [B(bk)])
                    s.cp("act" if kb % 2 == 0 else "dve", yt[:, kb * 4:(kb + 1) * 4, :],
                         ps[:, bk, :].rearrange("p (c t) -> p c t", t=128), [B(bk)], [(ytk, kb)])
                s.dma(yT_d[c].rearrange("p (k t) -> p k t", t=128), yt[:], [(ytk, k) for k in range(8)], [("yT_d", c)],
                      key="st_" + ytk)
            s.emit()
            pc[0] += 1
            if pc[0] >= STOP:
                return nc, s

        with ExitStack() as es:
            def T(name, shape, dt=F32):
                uid[0] += 1
                return es.enter_context(nc.sbuf_tensor("t%d_%s" % (uid[0], name), list(shape), dt))
            ggb = [T("ggb%d" % r, [128, D]) for r in range(2)]
            for r in range(2):
                s.dma(ggb[r][:], gg_d[r].partition_broadcast(128), (), ["ggb%d" % r])
            yTl = T("yTl", [128, 32, 512], BF16)
            wo_ = [T("wo%d" % i, [128, 32, 512], BF16) for i in range(2)]
            osb = T("osb", [128, 4, D])
            xr = [T("xr%d" % i, [128, D]) for i in range(2)]
            junk = T("junk6", [128, D], BF16)
            st6 = [T("st6_%d" % i, [128, 2]) for i in range(2)]
            sbs = [list(range(NCTX))] + [list(range(NCTX + 4 * i, NCTX + 4 * i + 4)) for i in range(NLAT // 4)]
            if last:
                sbs = sbs[1:]
            wcnt = 0
            ecnt = 0
            ccnt = 0
            for sbi, chs in enumerate(sbs):
                for ci, c in enumerate(chs):
                    s.dma(yTl[:, :, ci * 128:(ci + 1) * 128], yT_d[c].rearrange("p (k t) -> p k t", t=128), (), [("yTl", ci)])
                for jb in range(4):
                    wi_ = wcnt % 2
                    wcnt += 1
                    for hh2 in range(4):
                        s.dma(wo_[wi_][:, hh2 * 8:(hh2 + 1) * 8, :],
                              WOs_d[jb][:, hh2 * 4096:(hh2 + 1) * 4096].rearrange("p (k c) -> p k c", c=512),
                              (), [("wo%d" % wi_, hh2)])
                    for ci, c in enumerate(chs):
                        bk = ecnt % 8
                        ecnt += 1
                        o = ps[:, bk, :]
                        for k in range(32):
                            s.mm(o, yTl[:, k, ci * 128:(ci + 1) * 128], wo_[wi_][:, k, :], k == 0, k == 31,
                                 [("yTl", ci), ("wo%d" % wi_, k // 8)], [B(bk)])
                        s.cp("act" if ecnt % 2 == 0 else "dve", osb[:, ci, jb * 512:(jb + 1) * 512], o, [B(bk)], [("osb", ci, jb)])
                for ci, c in enumerate(chs):
                    i = ccnt % 2
                    ccnt += 1
                    r = 1 if c < NCTX else 0
                    oR = [("osb", ci, jb) for jb in range(4)]
                    s.dma(xr[i][:], xrow(c), (), ["xr%d" % i])
                    s.act(junk[:], osb[:, ci, :], AF.Square, oR, ["junk6", "ss6_%d" % i], accum=st6[i][:, 0:1])
                    rms_r(st6[i][:, 0:1], st6[i][:, 1:2], D, ["ss6_%d" % i], ["r6_%d" % i])
                    s.stt(osb[:, ci, :], osb[:, ci, :], st6[i][:, 1:2], ggb[r][:], ALU.mult, ALU.mult,
                          oR + ["r6_%d" % i, "ggb%d" % r], oR)
                    s.tt("pool", xr[i][:], xr[i][:], osb[:, ci, :], ALU.add, ["xr%d" % i] + oR, ["xr%d" % i])
                    s.dma(xdst(c), xr[i][:], ["xr%d" % i], [("xout", c)], key="st_xr%d" % i)
            s.emit()
            pc[0] += 1
            if pc[0] >= STOP:
                return nc, s
    return nc, s


def _consts():
    j = np.arange(128)[:, None]
    q = np.arange(128)[None, :]
    c = np.zeros((128, 6, 128), np.float32)
    c[:, 0] = (j == q)
    c[:, 1] = (j <= q)
    c[:, 2] = (j >= q)
    c[:, 3] = (j > q)
    c[:, 4] = (j < q)
    c[:, 5] = 1.0
    return c


def _prep(inp, b, half, layers, x_b, ctx_b):
    ls = list(layers)
    nl = len(ls)
    f = lambda a: np.ascontiguousarray(a, dtype=np.float32)
    flip = (half == 1)
    cT = np.stack([inp["c"][b].reshape(16, 128).T, inp["c_ctx"].reshape(16, 128).T], axis=-1)
    b_ada = inp["b_ada"][ls]
    w_in = inp["w_in"][ls]
    dt_bias = inp["dt_bias"][ls]
    a_log = inp["a_log"][ls]
    conv_w = inp["conv_w"][ls]
    w_s = inp["w_s"][ls]
    b_s = inp["b_s"][ls]
    if flip:
        w_in = np.concatenate([w_in[:, :, :4096], w_in[:, :, 4128:4160], w_in[:, :, 4096:4128], w_in[:, :, 4160:]], axis=2)
        dt_bias = dt_bias[:, ::-1]
        a_log = a_log[:, ::-1]
        conv_w = conv_w[:, ::-1]
        w_s = w_s[:, :, ::-1, ::-1]
        b_s = b_s[:, :, ::-1]
        x_b = x_b[::-1]
        ctx_b = ctx_b[::-1]
    sel = np.zeros((128, 2), np.float32)
    sel[:, 1 - half] = 1.0
    m = {
        "x_in": f(x_b), "ctx_in": f(ctx_b), "cT": f(cT), "consts": _consts(), "sel": sel,
        "w_ada": f(inp["w_ada"][ls]), "w_in": f(w_in), "w_out": f(inp["w_out"][ls]),
        "badaT": f(b_ada[:, :4096].reshape(nl, 32, 128).transpose(0, 2, 1)),
        "bgate": f(np.repeat(b_ada[:, None, 4096:], 2, axis=1)),
        "gpost": f(np.repeat(inp["g_post"][ls][:, None, :], 2, axis=1)),
        "gpreT": f(inp["g_pre"][ls].reshape(nl, 16, 128).transpose(0, 2, 1)),
        "convwT": f(conv_w.reshape(nl, 5, 32, 128).transpose(0, 3, 2, 1)),
        "convbT": f(inp["conv_b"][ls].reshape(nl, 32, 128).transpose(0, 2, 1)),
        "dtb": f(np.broadcast_to(dt_bias.reshape(nl, 1, 64), (nl, 128, 64))),
        "alog": f(np.broadcast_to(a_log.reshape(nl, 1, 64), (nl, 128, 64))),
        "dskip": f(np.broadcast_to(inp["d_skip"][ls].reshape(nl, 1, 32), (nl, 128, 32))),
        "gssd": f(inp["g_ssd"][ls]), "gv": f(inp["g_v"][ls]), "gmlp": f(inp["g_mlp"][ls]),
        "wsT": f(w_s.transpose(0, 3, 1, 2)),
        "bsT": f(b_s.transpose(0, 2, 1)),
    }
    return m


LAUNCH_GROUPS = [[0, 1, 2, 3]]
STOP = 10 ** 9
DEBUG = False
LAST = {}


def kernel(**inp):
    inp = {k: np.asarray(v) for k, v in inp.items()}
    x = inp["x"]
    ctx = inp["ctx"]
    nb, Lx, _ = x.shape
    Lh = Lx // 2
    NLAT = Lh // 128
    ncore = 2 * nb
    xs = [x[c // 2, (c % 2) * Lh:(c % 2 + 1) * Lh] for c in range(ncore)]
    cs = [ctx[c // 2] for c in range(ncore)]
    res = None
    for gi, layers in enumerate(LAUNCH_GROUPS):
        final = (gi == len(LAUNCH_GROUPS) - 1)
        nc, _ = build(NLAT, layers, gi == 0, final)
        in_maps = [_prep(inp, c // 2, c % 2, layers, xs[c], cs[c]) for c in range(ncore)]
        res = run_bass_kernel_spmd(nc, in_maps, core_ids=list(range(ncore)))
        LAST["res"] = res
        xs = [np.asarray(res.results[c]["out"])[::-1] if c % 2 else np.asarray(res.results[c]["out"]) for c in range(ncore)]
        cs = [np.asarray(res.results[c]["ctx_out"])[::-1] if c % 2 else np.asarray(res.results[c]["ctx_out"]) for c in range(ncore)]
    out = np.empty((nb, Lx, x.shape[2]), np.float32)
    for c in range(ncore):
        out[c // 2, (c % 2) * Lh:(c % 2 + 1) * Lh] = xs[c]
    return out
```

```python
import numpy as np
from contextlib import ExitStack
import concourse.bass as bass
import concourse.mybir as mybir
from concourse.bass_utils import run_bass_kernel_spmd

F32 = mybir.dt.float32
BF16 = mybir.dt.bfloat16
AF = mybir.ActivationFunctionType
ALU = mybir.AluOpType

D = 2048
DEPTH = 4
NCTX = 2
H = 32
XBC = 4096
INW = 12352
EPS = 1e-6
NCORES = 4


class Sch:
    ENG = ("pe", "act", "dve", "pool", "sp")

    def __init__(s, nc):
        s.nc = nc
        s.esem = {k: nc.alloc_semaphore("es_" + k) for k in ("pe", "act", "dve", "pool")}
        s.ecnt = {k: 0 for k in s.esem}
        s.pool = []
        s.poolcnt = []
        s.csems = []
        s.ninstr = 0
        s.reset()

    def reset(s):
        s.st = {k: [] for k in s.ENG}
        s.lastw = {}
        s.rd = {}
        s.seen = {k: {} for k in s.ENG}
        s.keysem = {}

    def _deps(s, eng, reads, writes):
        deps = []
        for k in reads:
            t = s.lastw.get(k)
            if t is not None:
                deps.append(t)
        for k in writes:
            t = s.lastw.get(k)
            if t is not None:
                deps.append(t)
            deps.extend(s.rd.get(k, ()))
        out = []
        seen = s.seen[eng]
        for t in deps:
            if t[0] == "E" and t[1] == eng and eng == "pe":
                continue
            kk = (t[0], t[1])
            if seen.get(kk, -1) >= t[2]:
                continue
            seen[kk] = t[2]
            out.append(t)
        return out

    def _post(s, tok, reads, writes):
        for k in writes:
            s.lastw[k] = tok
            s.rd[k] = []
        for k in reads:
            s.rd.setdefault(k, []).append(tok)

    def op(s, eng, fn, reads=(), writes=()):
        deps = s._deps(eng, reads, writes)
        idx = len(s.st[eng])
        s.st[eng].append([fn, deps, False, None])
        tok = ("E", eng, idx)
        s._post(tok, reads, writes)
        return tok

    def dma(s, out, in_, reads=(), writes=(), key=None, q="sp"):
        if key is None:
            key = writes[0]
        deps = s._deps(q, reads, writes)
        if key not in s.keysem:
            n = len(s.keysem)
            if n >= len(s.pool):
                s.pool.append(s.nc.alloc_semaphore("ds%d" % n))
                s.poolcnt.append(0)
            s.keysem[key] = n
        si = s.keysem[key]
        s.poolcnt[si] += 16
        tok = ("D", si, s.poolcnt[si])
        s.st[q].append([lambda e: e.dma_start(out=out, in_=in_), deps, False, tok])
        s._post(tok, reads, writes)
        return tok

    def coll(s, fn, reads=(), writes=()):
        deps = s._deps("pool", reads, writes)
        sem = s.nc.alloc_semaphore("cc%d" % len(s.csems))
        s.csems.append(sem)
        tok = ("C", len(s.csems) - 1, 1)
        s.st["pool"].append([fn, deps, False, tok])
        s._post(tok, reads, writes)
        return tok

    def mm(s, out, lhsT, rhs, start, stop, R, W):
        s.op("pe", lambda e: e.matmul(out, lhsT=lhsT, rhs=rhs, start=start, stop=stop), R, W)

    def tr(s, out, in_, ident, R, W):
        s.op("pe", lambda e: e.transpose(out=out, in_=in_, identity=ident), R, W)

    def act(s, out, in_, func, R, W, bias=None, scale=None, accum=None):
        def f(e):
            kw = {}
            if bias is not None:
                kw["bias"] = bias
            if scale is not None:
                kw["scale"] = scale
            if accum is not None:
                kw["accum_out"] = accum
            return e.activation(out=out, in_=in_, func=func, **kw)
        s.op("act", f, R, W)

    def tt(s, eng, out, in0, in1, op, R, W):
        s.op(eng, lambda e: e.tensor_tensor(out=out, in0=in0, in1=in1, op=op), R, W)

    def ts(s, eng, out, in0, s1, s2, op0, op1, R, W):
        if op1 is None:
            s.op(eng, lambda e: e.tensor_scalar(out=out, in0=in0, scalar1=s1, scalar2=None, op0=op0), R, W)
        else:
            s.op(eng, lambda e: e.tensor_scalar(out=out, in0=in0, scalar1=s1, scalar2=s2, op0=op0, op1=op1), R, W)

    def stt(s, out, in0, sc, in1, op0, op1, R, W):
        s.op("dve", lambda e: e.scalar_tensor_tensor(out=out, in0=in0, scalar=sc, in1=in1, op0=op0, op1=op1), R, W)

    def cp(s, eng, out, in_, R, W):
        if eng == "act":
            s.op(eng, lambda e: e.copy(out=out, in_=in_), R, W)
        else:
            s.op(eng, lambda e: e.tensor_copy(out=out, in_=in_), R, W)

    def recip(s, out, in_, R, W):
        s.op("dve", lambda e: e.reciprocal(out=out, in_=in_), R, W)

    def memset(s, eng, ap, val, W):
        s.op(eng, lambda e: e.memset(ap, val), (), W)

    def emit(s):
        nc = s.nc
        fin = [("D", si, s.poolcnt[si]) for si in sorted(s.keysem.values())]
        s.st["sp"].append([None, fin, False, None])
        needed = {k: set() for k in s.esem}
        for eng in s.ENG:
            for ent in s.st[eng]:
                for t in ent[1]:
                    if t[0] == "E":
                        needed[t[1]].add(t[2])
        vals = {}
        for eng in s.esem:
            c = s.ecnt[eng]
            v = {}
            for i in sorted(needed[eng]):
                c += 1
                v[i] = c
            vals[eng] = v
            s.ecnt[eng] = c
        engobj = dict(pe="tensor", act="scalar", dve="vector", pool="gpsimd", sp="sync")
        with nc.Block() as block:
            for eng in s.ENG:
                stream = s.st[eng]
                if not stream:
                    continue

                def body(e, eng=eng, stream=stream):
                    nd = needed.get(eng, ())
                    for i, ent in enumerate(stream):
                        for t in ent[1]:
                            if t[0] == "E":
                                e.wait_ge(s.esem[t[1]], vals[t[1]][t[2]])
                            elif t[0] == "C":
                                e.wait_ge(s.csems[t[1]], 1)
                            else:
                                e.wait_ge(s.pool[t[1]], t[2])
                        if ent[0] is None:
                            continue
                        ins = ent[0](e)
                        s.ninstr += 1
                        if ent[3] is not None and ent[3][0] == "C":
                            ins.then_inc(s.csems[ent[3][1]])
                        elif ent[3] is not None:
                            ins.then_inc(s.pool[ent[3][1]], 16)
                        elif i in nd:
                            ins.then_inc(s.esem[eng], 1)
                getattr(block, engobj[eng])(body)
        s.reset()


def build(NLAT, layers, first, last_is_final):
    C = NCTX + NLAT
    L = len(layers)
    nc = bass.Bass("TRN2", target_bir_lowering=False)

    def din(name, shape, dt=F32):
        return nc.dram_tensor(name, list(shape), dt, kind="ExternalInput").ap()

    def dscr(name, shape, dt=F32):
        return nc.dram_tensor(name, list(shape), dt, kind=("ExternalOutput" if DEBUG else "Internal")).ap()

    x_in = din("x_in", [NLAT * 128, D])
    ctx_in = din("ctx_in", [NCTX * 128, D])
    cT_in = din("cT", [128, 16, 2])
    consts_in = din("consts", [128, 6, 128])
    w_ada = din("w_ada", [L, D, 3 * D])
    w_in = din("w_in", [L, D, INW])
    w_out = din("w_out", [L, 2 * D, D])
    badaT_in = din("badaT", [L, 128, 32])
    bgate_in = din("bgate", [L, 2, D])
    gpost_in = din("gpost", [L, 2, D])
    gpreT_in = din("gpreT", [L, 128, 16])
    convw_in = din("convwT", [L, 128, 32, 5])
    convb_in = din("convbT", [L, 128, 32])
    dtb_in = din("dtb", [L, 128, 64])
    alog_in = din("alog", [L, 128, 64])
    dskip_in = din("dskip", [L, 128, 32])
    gssd_in = din("gssd", [L, D])
    gv_in = din("gv", [L, D])
    gmlp_in = din("gmlp", [L, D])
    wsT_in = din("wsT", [L, 128, 16, 128])
    bsT_in = din("bsT", [L, 128, 16])
    sel_in = din("sel", [128, 2])

    out_d = nc.dram_tensor("out", [NLAT * 128, D], F32, kind="ExternalOutput").ap()
    ctx_out = nc.dram_tensor("ctx_out", [NCTX * 128, D], F32, kind="ExternalOutput").ap()

    hxT_d = dscr("hxT_d", [C, 128, 2048], BF16)
    W1s_d = dscr("W1s_d", [32, 128, 2048], BF16)
    W1dt_d = dscr("W1dt_d", [128, 1024], BF16)
    W2s_d = dscr("W2s_d", [16, 128, 8192], BF16)
    WOs_d = dscr("WOs_d", [4, 128, 16384], BF16)
    xtok_d = dscr("xtok_d", [C, 128, 2048], BF16)
    xdt_d = dscr("xdt_d", [2, C, 128, 2048], BF16)
    BT_d = dscr("BT_d", [C, 128, 1024], BF16)
    CT_d = dscr("CT_d", [C, 128, 1024], BF16)
    a_d = dscr("a_d", [C, 128, 64])
    cdec_d = dscr("cdec_d", [C, 128, 64])
    S_d = dscr("S_d", [2, C, 128, 2048])
    hst_d = dscr("hst_d", [2, C, 128, 2048], BF16)
    z2_d = dscr("z2_d", [C, 128, 8192], BF16)
    yT_d = dscr("yT_d", [C, 128, 4096], BF16)
    gg_d = dscr("gg_d", [2, D])
    send_d = [nc.dram_tensor("send_d%d" % i, [128, 2048], F32, kind="Internal").ap() for i in range(L)]
    recv_d = [nc.dram_tensor("recv_d%d" % i, [256, 2048], F32, kind="Internal").ap() for i in range(L)]

    s = Sch(nc)
    pc = [0]

    uid = [0]

    def sb(name, shape, dt=F32):
        return nc.alloc_sbuf_tensor("g_" + name, list(shape), dt)

    consts = sb("consts", [128, 6, 128])
    ident = consts[:, 0, :]
    Um = consts[:, 1, :]
    Lm = consts[:, 2, :]
    SLm = consts[:, 3, :]
    SUm = consts[:, 4, :]
    ones = consts[:, 5, :]
    identb = sb("identb", [128, 128], BF16)
    sT = sb("sT", [128, 16, 2])
    G1 = sb("G1", [128, 16, 2])
    shT = sb("shT", [128, 16, 2])
    adaT = sb("adaT", [128, 32, 2])
    badaT = sb("badaT", [128, 32])
    gpreT = sb("gpreT", [128, 16])
    convw = sb("convw", [128, 32, 5])
    convb = sb("convb", [128, 32])
    dtb = sb("dtb", [128, 64])
    Abc = sb("Abc", [128, 64])
    dskip = sb("dskip", [128, 32])
    bsT = sb("bsT", [128, 16])
    wsTb = sb("wsTb", [128, 16, 128], BF16)
    epst = sb("epst", [128, 1])
    selt = sb("selt", [128, 2])
    ps = nc.alloc_psum_tensor("ps", [128, 8, 512], F32)

    def B(i):
        return "ps%d" % i

    s.dma(consts[:], consts_in, (), ["consts"])
    s.dma(sT[:], cT_in, (), ["sT"])
    s.dma(selt[:], sel_in, (), ["selt"])
    s.cp("dve", identb[:], ident, ["consts"], ["identb"])
    s.act(sT[:], sT[:], AF.Silu, ["sT"], ["sT"])
    s.memset("dve", epst[:], EPS, ["epst"])
    s.emit()

    def rms_r(ssum, rr, n, R, W):
        s.ts("dve", rr, ssum, 1.0 / n, EPS, ALU.mult, ALU.add, R, W)
        s.act(rr, rr, AF.Sqrt, W, W)
        s.recip(rr, rr, W, W)

    for li, l in enumerate(layers):
        last = last_is_final and (li == L - 1)
        x_src = x_in if (li == 0) else out_d
        ctx_src = ctx_in if (li == 0) else ctx_out
        chunks2 = list(range(NCTX, C)) if last else list(range(C))

        def xrow(c, src_x=x_src, src_c=ctx_src):
            if c < NCTX:
                return src_c[c * 128:(c + 1) * 128, :]
            return src_x[(c - NCTX) * 128:(c - NCTX + 1) * 128, :]

        def xdst(c):
            if c < NCTX:
                return ctx_out[c * 128:(c + 1) * 128, :]
            return out_d[(c - NCTX) * 128:(c - NCTX + 1) * 128, :]

        with ExitStack() as es:
            def T(name, shape, dt=F32):
                uid[0] += 1
                return es.enter_context(nc.sbuf_tensor("t%d_%s" % (uid[0], name), list(shape), dt))
            s.dma(badaT[:], badaT_in[li], (), ["badaT"])
            s.dma(gpreT[:], gpreT_in[li], (), ["gpreT"])
            s.dma(convw[:], convw_in[li], (), ["convw"])
            s.dma(convb[:], convb_in[li], (), ["convb"])
            s.dma(dtb[:], dtb_in[li], (), ["dtb"])
            s.dma(Abc[:], alog_in[li], (), ["Abc"])
            s.dma(dskip[:], dskip_in[li], (), ["dskip"])
            s.dma(bsT[:], bsT_in[li], (), ["bsT"])
            wsf = T("wsf", [128, 16, 128])
            s.dma(wsf[:], wsT_in[li], (), ["wsf"])
            s.cp("dve", wsTb[:], wsf[:], ["wsf"], ["wsTb"])
            s.act(Abc[:], Abc[:], AF.Exp, ["Abc"], ["Abc"])
            s.ts("dve", Abc[:], Abc[:], -1.0, None, ALU.mult, None, ["Abc"], ["Abc"])
            bg = T("bg", [2, D])
            gp = T("gp", [2, D])
            gsb = T("gsb", [2, D])
            s.dma(bg[:], bgate_in[li], (), ["bg"])
            s.dma(gp[:], gpost_in[li], (), ["gp"])
            wblk = [T("wblk%d" % i, [128, 16, 512]) for i in range(2)]
            wa = w_ada[li].rearrange("(k p) c -> p k c", p=128)
            for jb in range(12):
                wb = wblk[jb % 2]
                wk = "wblk%d" % (jb % 2)
                for hh in range(2):
                    s.dma(wb[:, hh * 8:(hh + 1) * 8, :], wa[:, hh * 8:(hh + 1) * 8, jb * 512:(jb + 1) * 512],
                          (), [wk + "h%d" % hh])
                Rw = [wk + "h0", wk + "h1", "sT"]
                if jb < 8:
                    for t in range(4):
                        o = ps[:, 0, (jb * 4 + t) * 2:(jb * 4 + t) * 2 + 2]
                        for k in range(16):
                            s.mm(o, wb[:, k, t * 128:(t + 1) * 128], sT[:, k, :], k == 0, k == 15, Rw, [B(0)])
                else:
                    o = ps[0:2, jb - 7, :]
                    for k in range(16):
                        s.mm(o, sT[:, k, :], wb[:, k, :], k == 0, k == 15, Rw, [B(jb - 7)])
                    s.tt("dve", gsb[:, (jb - 8) * 512:(jb - 7) * 512], o, bg[:, (jb - 8) * 512:(jb - 7) * 512],
                         ALU.add, [B(jb - 7), "bg"], ["gsb"])
            s.tt("dve", adaT[:], ps[:, 0, 0:64].rearrange("p (j r) -> p j r", r=2),
                 badaT[:, :, None].broadcast_to([128, 32, 2]), ALU.add, [B(0), "badaT"], ["adaT"])
            s.cp("dve", shT[:], adaT[:, 0:16, :], ["adaT"], ["shT"])
            s.ts("dve", G1[:], adaT[:, 16:32, :], 1.0, None, ALU.add, None, ["adaT"], ["G1"])
            s.tt("dve", G1[:], G1[:], gpreT[:, :, None].broadcast_to([128, 16, 2]), ALU.mult, ["G1", "gpreT"], ["G1"])
            s.tt("dve", gsb[:], gsb[:], gp[:], ALU.mult, ["gsb", "gp"], ["gsb"])
            s.dma(gg_d, gsb[:], ["gsb"], ["gg_d"])
            stf = [T("stf%d" % i, [128, 4096]) for i in range(2)]
            stb = [T("stb%d" % i, [128, 4096], BF16) for i in range(2)]
            wi = w_in[li].rearrange("(k p) c -> p k c", p=128)
            wo = w_out[li].rearrange("(k p) c -> p k c", p=128)
            jobs = []
            for j in range(32):
                jobs.append((wi[:, :, j * 128:(j + 1) * 128], [16, 128], W1s_d[j]))
            jobs.append((wi[:, :, 4096:4160], [16, 64], W1dt_d))
            for jb in range(16):
                for hh in range(2):
                    jobs.append((wi[:, hh * 8:(hh + 1) * 8, 4160 + jb * 512:4160 + (jb + 1) * 512], [8, 512],
                                 W2s_d[jb][:, hh * 4096:(hh + 1) * 4096]))
            for jb in range(4):
                for hh in range(4):
                    jobs.append((wo[:, hh * 8:(hh + 1) * 8, jb * 512:(jb + 1) * 512], [8, 512],
                                 WOs_d[jb][:, hh * 4096:(hh + 1) * 4096]))
            cast_engs = ["act", "pool", "dve"]
            for n, (src, shp, dst) in enumerate(jobs):
                i = n % 2
                ne = shp[0] * shp[1]
                fv = stf[i][:, 0:ne].rearrange("p (a b) -> p a b", b=shp[1])
                s.dma(fv, src, (), ["stf%d" % i])
                s.cp(cast_engs[n % 3], stb[i][:, 0:ne], stf[i][:, 0:ne], ["stf%d" % i], ["stb%d" % i])
                s.dma(dst, stb[i][:, 0:ne], ["stb%d" % i], [("w", n)], key="st_stb%d" % i)
            s.emit()
            pc[0] += 1
            if pc[0] >= STOP:
                return nc, s

        with ExitStack() as es:
            def T(name, shape, dt=F32):
                uid[0] += 1
                return es.enter_context(nc.sbuf_tensor("t%d_%s" % (uid[0], name), list(shape), dt))
            xin = [T("xin%d" % i, [128, D]) for i in range(2)]
            xn = [T("xn%d" % i, [128, D]) for i in range(2)]
            junk = T("junk", [128, D], BF16)
            ssr = [T("ssr%d" % i, [128, 2]) for i in range(2)]
            hxs = [T("hxs%d" % i, [128, 16, 128], BF16) for i in range(2)]
            for c in range(C):
                i = c % 2
                r = 1 if c < NCTX else 0
                s.dma(xin[i][:], xrow(c), (), ["xin%d" % i])
                s.act(junk[:], xin[i][:], AF.Square, ["xin%d" % i], ["junk", "ss%d" % i], accum=ssr[i][:, 0:1])
                rms_r(ssr[i][:, 0:1], ssr[i][:, 1:2], D, ["ss%d" % i], ["rr%d" % i])
                s.ts("dve", xn[i][:], xin[i][:], ssr[i][:, 1:2], None, ALU.mult, None, ["xin%d" % i, "rr%d" % i], ["xn%d" % i])
                for kb in range(4):
                    bk = (c % 2) * 4 + kb
                    for kq in range(4):
                        k = kb * 4 + kq
                        s.mm(ps[:, bk, kq * 128:(kq + 1) * 128], xn[i][:, k * 128:(k + 1) * 128], ident, True, True,
                             ["xn%d" % i, "consts"], [B(bk)])
                    for kq in range(4):
                        k = kb * 4 + kq
                        s.ts("dve", hxs[i][:, k, :], ps[:, bk, kq * 128:(kq + 1) * 128], G1[:, k, r:r + 1], shT[:, k, r:r + 1],
                             ALU.mult, ALU.add, [B(bk), "G1", "shT"], [("hxs%d" % i, k)])
                s.dma(hxT_d[c].rearrange("p (k t) -> p k t", t=128), hxs[i][:],
                      [("hxs%d" % i, k) for k in range(16)], [("hxT_d", c)], key="st_hxs%d" % i)
            s.emit()
            pc[0] += 1
            if pc[0] >= STOP:
                return nc, s

        with ExitStack() as es:
            def T(name, shape, dt=F32):
                uid[0] += 1
                return es.enter_context(nc.sbuf_tensor("t%d_%s" % (uid[0], name), list(shape), dt))
            hx = [T("hx%d" % i, [128, 16, 512], BF16) for i in range(2)]
            w1 = [T("w1_%d" % i, [128, 16, 128], BF16) for i in range(3)]
            w1dt = T("w1dt", [128, 16, 64], BF16)
            acc = [T("acc%d" % i, [128, 512]) for i in range(2)]
            xbcj = [T("xbcj%d" % i, [128, 512], BF16) for i in range(2)]
            btsb = T("btsb", [128, 8, 512], BF16)
            ctsb = T("ctsb", [128, 8, 512], BF16)
            xtok = T("xtok", [128, 4, 2048], BF16)
            btok = T("btok", [128, 4, 1024], BF16)
            dtv = [T("dtv%d" % i, [128, 64]) for i in range(2)]
            asb = [T("asb%d" % i, [128, 64]) for i in range(2)]
            dec = [T("dec%d" % i, [128, 64]) for i in range(2)]
            cdec = [T("cdec%d" % i, [128, 64]) for i in range(2)]
            xdt = [[T("xdt%d_%d" % (d_, i), [128, 2048], BF16) for i in range(2)] for d_ in range(2)]
            xw = [T("xw%d" % d_, [128, 2048], BF16) for d_ in range(2)]
            Ssb = [T("Ssb%d" % i, [128, 2048]) for i in range(2)]
            s.dma(w1dt[:], W1dt_d.rearrange("p (k c) -> p k c", c=64), (), ["w1dt"])
            sbs = [list(range(NCTX))] + [list(range(NCTX + 4 * i, NCTX + 4 * i + 4)) for i in range(NLAT // 4)]
            wcnt = 0
            ccnt = 0
            for sbi, chs in enumerate(sbs):
                Tn = 128 * len(chs)
                rowlen = 256 if sbi == 0 else 64
                nrows = Tn // rowlen
                hb = hx[sbi % 2]
                hk = "hx%d" % (sbi % 2)
                for ci, c in enumerate(chs):
                    s.dma(hb[:, :, ci * 128:(ci + 1) * 128], hxT_d[c].rearrange("p (k t) -> p k t", t=128),
                          (), [(hk, ci)])
                hR = [(hk, ci) for ci in range(len(chs))]
                for j in range(32):
                    wi_ = wcnt % 3
                    wcnt += 1
                    s.dma(w1[wi_][:], W1s_d[j].rearrange("p (k c) -> p k c", c=128), (), ["w1_%d" % wi_])
                    gb = j % 2
                    o = ps[:, gb, 0:Tn]
                    for k in range(16):
                        s.mm(o, w1[wi_][:, k, :], hb[:, k, 0:Tn], k == 0, k == 15, ["w1_%d" % wi_] + hR, [B(gb)])
                    a_ = acc[gb]
                    ak = "acc%d" % gb
                    s.ts("dve", a_[:, 0:Tn], o, convw[:, j, 2:3], convb[:, j:j + 1], ALU.mult, ALU.add,
                         [B(gb), "convw", "convb"], [ak])
                    ov = o.rearrange("p (r t) -> p r t", t=rowlen)
                    av = a_[:, 0:Tn].rearrange("p (r t) -> p r t", t=rowlen)
                    for kk in (0, 1, 3, 4):
                        sh = kk - 2
                        if sh < 0:
                            src = ov[:, :, 0:rowlen + sh]
                            dst = av[:, :, -sh:rowlen]
                        else:
                            src = ov[:, :, sh:rowlen]
                            dst = av[:, :, 0:rowlen - sh]
                        s.stt(dst, src, convw[:, j, kk:kk + 1], dst, ALU.mult, ALU.add, [B(gb), "convw", ak], [ak])
                    if j < 16:
                        tgt = xbcj[gb][:, 0:Tn]
                        tk = "xbcj%d" % gb
                    elif j < 24:
                        tgt = btsb[:, j - 16, 0:Tn]
                        tk = ("btsb", j - 16)
                    else:
                        tgt = ctsb[:, j - 24, 0:Tn]
                        tk = ("ctsb", j - 24)
                    s.act(tgt, a_[:, 0:Tn], AF.Silu, [ak], [tk])
                    if j < 24:
                        nch = len(chs)
                        for ci in range(nch):
                            s.mm(ps[:, 2, ci * 128:(ci + 1) * 128], tgt[:, ci * 128:(ci + 1) * 128], identb[:], True, True,
                                 [tk, "identb"], [B(2)])
                        pv = ps[:, 2, 0:nch * 128].rearrange("p (c t) -> p c t", t=128)
                        if j < 16:
                            s.cp("act", xtok[:, 0:nch, j * 128:(j + 1) * 128], pv, [B(2)], [("xtok", j)])
                        else:
                            s.cp("act", btok[:, 0:nch, (j - 16) * 128:(j - 15) * 128], pv, [B(2)], [("btok", j - 16)])
                for ci, c in enumerate(chs):
                    i = ccnt % 2
                    ccnt += 1
                    o = ps[:, 3, 0:64]
                    for k in range(16):
                        s.mm(o, hb[:, k, ci * 128:(ci + 1) * 128], w1dt[:, k, :], k == 0, k == 15, [(hk, ci), "w1dt"], [B(3)])
                    dk_ = "dtv%d" % i
                    s.tt("dve", dtv[i][:], o, dtb[:], ALU.add, [B(3), "dtb"], [dk_])
                    s.act(dtv[i][:], dtv[i][:], AF.Exp, [dk_], [dk_])
                    s.ts("dve", dtv[i][:], dtv[i][:], 1.0, None, ALU.add, None, [dk_], [dk_])
                    s.act(dtv[i][:], dtv[i][:], AF.Ln, [dk_], [dk_])
                    s.tt("dve", asb[i][:], dtv[i][:], Abc[:], ALU.mult, [dk_, "Abc"], ["asb%d" % i])
                    o2 = ps[:, 3, 64:192]
                    s.mm(o2[:, 0:32], SLm, asb[i][:, 0:32], True, True, ["consts", "asb%d" % i], [B(3)])
                    s.mm(o2[:, 32:64], SUm, asb[i][:, 32:64], True, True, ["consts", "asb%d" % i], [B(3)])
                    s.mm(o2[:, 64:128], ones, asb[i][:, 0:64], True, True, ["consts", "asb%d" % i], [B(3)])
                    s.act(dec[i][:], o2[:, 0:64], AF.Exp, [B(3)], ["dec%d" % i])
                    s.act(cdec[i][:], o2[:, 64:128], AF.Exp, [B(3)], ["cdec%d" % i])
                    xR = [("xtok", j) for j in range(16)]
                    xv = xtok[:, ci, :].rearrange("p (h e) -> p h e", e=64)
                    for d_ in range(2):
                        xd = xdt[d_][i]
                        s.tt("dve" if d_ == 0 else "pool", xd[:].rearrange("p (h e) -> p h e", e=64), xv,
                             dtv[i][:, d_ * 32:(d_ + 1) * 32, None].broadcast_to([128, 32, 64]), ALU.mult,
                             xR + [dk_], ["xdt%d_%d" % (d_, i)])
                        s.tt("dve" if d_ == 0 else "pool", xw[d_][:].rearrange("p (h e) -> p h e", e=64),
                             xd[:].rearrange("p (h e) -> p h e", e=64),
                             dec[i][:, d_ * 32:(d_ + 1) * 32, None].broadcast_to([128, 32, 64]), ALU.mult,
                             ["xdt%d_%d" % (d_, i), "dec%d" % i], ["xw%d" % d_])
                        sk = "Ssb%d" % d_
                        for hf in range(2):
                            b0 = 4 + 2 * hf
                            for gq in range(4):
                                g = hf * 4 + gq
                                s.mm(ps[:, b0 + gq // 2, (gq % 2) * 256:(gq % 2 + 1) * 256], btok[:, ci, g * 128:(g + 1) * 128],
                                     xw[d_][:, g * 256:(g + 1) * 256], True, True,
                                     [("btok", g), "xw%d" % d_], [B(b0 + gq // 2)])
                            s.cp("act", Ssb[d_][:, hf * 1024:(hf + 1) * 1024].rearrange("p (b f) -> p b f", f=512),
                                 ps[:, b0:b0 + 2, :], [B(b0), B(b0 + 1)], [(sk, hf)])
                        s.dma(S_d[d_][c], Ssb[d_][:], [(sk, 0), (sk, 1)], [("S_d", d_, c)], key="st_" + sk)
                        s.dma(xdt_d[d_][c], xd[:], ["xdt%d_%d" % (d_, i)], [("xdt_d", d_, c)], key="st_xdt%d_%d" % (d_, i))
                    s.dma(xtok_d[c], xtok[:, ci, :], xR, [("xtok_d", c)], key="st_xtok%d" % ci)
                    s.dma(BT_d[c].rearrange("p (g t) -> p g t", t=128), btsb[:, :, ci * 128:(ci + 1) * 128],
                          [("btsb", g) for g in range(8)], [("BT_d", c)], key="st_bt%d" % ci)
                    s.dma(CT_d[c].rearrange("p (g t) -> p g t", t=128), ctsb[:, :, ci * 128:(ci + 1) * 128],
                          [("ctsb", g) for g in range(8)], [("CT_d", c)], key="st_ct%d" % ci)
                    s.dma(a_d[c], asb[i][:], ["asb%d" % i], [("a_d", c)], key="st_asb%d" % i)
                    s.dma(cdec_d[c], cdec[i][:], ["cdec%d" % i], [("cdec_d", c)], key="st_cdec%d" % i)
            s.emit()
            pc[0] += 1
            if pc[0] >= STOP:
                return nc, s

        with ExitStack() as es:
            def T(name, shape, dt=F32):
                uid[0] += 1
                return es.enter_context(nc.sbuf_tensor("t%d_%s" % (uid[0], name), list(shape), dt))
            hh_ = [T("h%d" % d_, [128, 2048]) for d_ in range(2)]
            Sl = [[T("Sl%d_%d" % (d_, i), [128, 2048]) for i in range(2)] for d_ in range(2)]
            cdl = [[T("cdl%d_%d" % (d_, i), [128, 64]) for i in range(2)] for d_ in range(2)]
            hbf = [[T("hbf%d_%d" % (d_, i), [128, 2048], BF16) for i in range(2)] for d_ in range(2)]
            Rv = [T("Rv%d" % i, [128, 2048]) for i in range(2)]
            cnts = [0, 0]

            def step(d_, c, e_):
                i = cnts[d_] % 2
                cnts[d_] += 1
                hk = "h%d" % d_
                s.dma(Sl[d_][i][:], S_d[d_][c], (), ["Sl%d_%d" % (d_, i)])
                s.dma(cdl[d_][i][:], cdec_d[c], (), ["cdl%d_%d" % (d_, i)])
                s.cp("act", hbf[d_][i][:], hh_[d_][:], [hk], ["hbf%d_%d" % (d_, i)])
                s.dma(hst_d[d_][c], hbf[d_][i][:], ["hbf%d_%d" % (d_, i)], [("hst_d", d_, c)], key="st_hbf%d_%d" % (d_, i))
                hv = hh_[d_][:].rearrange("p (h e) -> p h e", e=64)
                s.tt(e_, hv, hv, cdl[d_][i][:, d_ * 32:(d_ + 1) * 32, None].broadcast_to([128, 32, 64]), ALU.mult,
                     [hk, "cdl%d_%d" % (d_, i)], [hk])
                s.tt(e_, hh_[d_][:], hh_[d_][:], Sl[d_][i][:], ALU.add, [hk, "Sl%d_%d" % (d_, i)], [hk])

            s.memset("dve", hh_[0][:], 0.0, ["h0"])
            s.memset("pool", hh_[1][:], 0.0, ["h1"])
            for c in (1, 0):
                step(1, c, "pool")
            for c in range(C):
                step(0, c, "dve")
            s.dma(send_d[li], hh_[0][:], ["h0"], ["send_d"], key="st_send")
            s.coll(lambda e, a=send_d[li], b=recv_d[li]: e.collective_compute(
                "AllGather", ALU.bypass, replica_groups=[[0, 1], [2, 3], [4, 5], [6, 7]], ins=[a], outs=[b]),
                ["send_d"], ["recv_d"])
            for r_ in range(2):
                s.dma(Rv[r_][:], recv_d[li][r_ * 128:(r_ + 1) * 128, :], ["recv_d"], ["Rv%d" % r_])
            s.ts("dve", hh_[1][:], Rv[0][:], selt[:, 0:1], None, ALU.mult, None, ["Rv0", "selt", "h1"], ["h1"])
            s.stt(hh_[1][:], Rv[1][:], selt[:, 1:2], hh_[1][:], ALU.mult, ALU.add, ["Rv1", "selt", "h1"], ["h1"])
            for c in range(C - 1, NCTX - 1, -1):
                step(1, c, "dve")
            s.emit()
            pc[0] += 1
            if pc[0] >= STOP:
                return nc, s

        with ExitStack() as es:
            def T(name, shape, dt=F32):
                uid[0] += 1
                return es.enter_context(nc.sbuf_tensor("t%d_%s" % (uid[0], name), list(shape), dt))
            hx = [T("hx%d" % i, [128, 16, 1024], BF16) for i in range(2)]
            w2 = [T("w2_%d" % i, [128, 16, 512], BF16) for i in range(3)]
            zst = [T("zst%d" % i, [128, 512], BF16) for i in range(4)]
            sbs = [list(range(NCTX))] + [list(range(NCTX + 8 * i, NCTX + 8 * i + 8)) for i in range(NLAT // 8)]
            if last:
                sbs = sbs[1:]
            wcnt = 0
            ecnt = 0
            for sbi, chs in enumerate(sbs):
                hb = hx[sbi % 2]
                hk = "hx%d" % (sbi % 2)
                for ci, c in enumerate(chs):
                    s.dma(hb[:, :, ci * 128:(ci + 1) * 128], hxT_d[c].rearrange("p (k t) -> p k t", t=128), (), [(hk, ci)])
                for jb in range(16):
                    wi_ = wcnt % 3
                    wcnt += 1
                    for hh2 in range(2):
                        s.dma(w2[wi_][:, hh2 * 8:(hh2 + 1) * 8, :],
                              W2s_d[jb][:, hh2 * 4096:(hh2 + 1) * 4096].rearrange("p (k c) -> p k c", c=512),
                              (), [("w2_%d" % wi_, hh2)])
                    for ci, c in enumerate(chs):
                        bk = ecnt % 8
                        zi = ecnt % 4
                        ecnt += 1
                        o = ps[:, bk, :]
                        for k in range(16):
                            s.mm(o, hb[:, k, ci * 128:(ci + 1) * 128], w2[wi_][:, k, :], k == 0, k == 15,
                                 [(hk, ci), ("w2_%d" % wi_, k // 8)], [B(bk)])
                        s.cp("act" if ecnt % 2 == 0 else "dve", zst[zi][:], o, [B(bk)], ["zst%d" % zi])
                        s.dma(z2_d[c][:, jb * 512:(jb + 1) * 512], zst[zi][:], ["zst%d" % zi], [("z2_d", c, jb)],
                              key="st_zst%d" % zi)
            s.emit()
            pc[0] += 1
            if pc[0] >= STOP:
                return nc, s

        with ExitStack() as es:
            def T(name, shape, dt=F32):
                uid[0] += 1
                return es.enter_context(nc.sbuf_tensor("t%d_%s" % (uid[0], name), list(shape), dt))
            gssd = T("gssd", [128, D])
            gv = T("gv", [128, D])
            gmlp = T("gmlp", [128, D])
            s.dma(gssd[:], gssd_in[li].partition_broadcast(128), (), ["gssd"])
            s.dma(gv[:], gv_in[li].partition_broadcast(128), (), ["gv"])
            s.dma(gmlp[:], gmlp_in[li].partition_broadcast(128), (), ["gmlp"])
            z2s = [T("z2_%d" % i, [128, 8192], BF16) for i in range(2)]
            NB = 2
            xtk = [T("xtk%d" % i, [128, 2048], BF16) for i in range(3)]
            xdl = [[T("xdl%d_%d" % (d_, i), [128, 2048], BF16) for i in range(NB)] for d_ in range(2)]
            hsl = [[T("hsl%d_%d" % (d_, i), [128, 2048], BF16) for i in range(NB)] for d_ in range(2)]
            btl = [T("btl%d" % i, [128, 8, 128], BF16) for i in range(NB)]
            ctl = [T("ctl%d" % i, [128, 8, 128], BF16) for i in range(NB)]
            al = [T("al%d" % i, [128, 64]) for i in range(NB)]
            cbm = [T("cbm%d" % d_, [128, 8, 128]) for d_ in range(2)]
            rhs4 = [T("rhs4_%d" % i, [128, 4, 128]) for i in range(2)]
            E4 = [T("E4_%d" % i, [128, 4, 128]) for i in range(2)]
            F4 = [T("F4_%d" % i, [128, 4, 128]) for i in range(2)]
            M4 = [T("M4_%d" % i, [128, 4, 128], BF16) for i in range(2)]
            N4 = [T("N4_%d" % i, [128, 4, 128], BF16) for i in range(2)]
            ybufs = [T("ybuf%d" % i, [128, D]) for i in range(2)]
            tmp = T("tmp", [128, D])
            mbuf = T("mbuf", [128, D])
            vn = T("vn", [128, D], BF16)
            ycat = T("ycat", [128, 2 * D], BF16)
            yTs = [T("yTs%d" % i, [128, 32, 128], BF16) for i in range(2)]
            st5 = T("st5", [128, 8])
            masks = [(SLm, Um, Um), (SUm, Lm, Lm)]
            NCH = len(chunks2)

            def loads(n):
                c = chunks2[n]
                i = n % NB
                s.dma(xtk[n % 3][:], xtok_d[c], (), ["xtk%d" % (n % 3)])
                for d_ in range(2):
                    s.dma(xdl[d_][i][:], xdt_d[d_][c], (), ["xdl%d_%d" % (d_, i)])
                    s.dma(hsl[d_][i][:], hst_d[d_][c], (), ["hsl%d_%d" % (d_, i)])
                s.dma(btl[i][:], BT_d[c].rearrange("p (g t) -> p g t", t=128), (), ["btl%d" % i])
                s.dma(ctl[i][:], CT_d[c].rearrange("p (g t) -> p g t", t=128), (), ["ctl%d" % i])
                s.dma(al[i][:], a_d[c], (), ["al%d" % i])

            def z2load(n):
                c = chunks2[n]
                zi = n % 2
                for q4 in range(4):
                    s.dma(z2s[zi][:, q4 * 2048:(q4 + 1) * 2048], z2_d[c][:, q4 * 2048:(q4 + 1) * 2048], (),
                          [("z2", zi, q4)])

            def ssd(n, tick):
                i = n % NB
                yi = n % 2
                ybuf = ybufs[yi]
                for g in range(8):
                    s.mm(ps[:, g // 4, (g % 4) * 128:(g % 4 + 1) * 128], btl[i][:, g, :], ctl[i][:, g, :], True, True,
                         ["btl%d" % i, "ctl%d" % i], [B(g // 4)])
                for d_ in range(2):
                    mk = masks[d_][2]
                    for hb_ in range(2):
                        s.tt("dve", cbm[d_][:, hb_ * 4:(hb_ + 1) * 4, :],
                             ps[:, hb_, :].rearrange("p (g t) -> p g t", t=128),
                             mk[:, None, :].broadcast_to([128, 4, 128]), ALU.mult,
                             [B(hb_), "consts"], [("cbm%d" % d_, hb_)])
                items = [(d_, g) for d_ in range(2) for g in range(8)]

                def stageA(idx):
                    d_, g = items[idx]
                    gi = idx % 2
                    m1, m2, _ = masks[d_]
                    hc0 = d_ * 32 + g * 4
                    s.tt("dve", rhs4[gi][:], m2[:, None, :].broadcast_to([128, 4, 128]),
                         al[i][:, hc0:hc0 + 4, None].broadcast_to([128, 4, 128]), ALU.mult,
                         ["consts", "al%d" % i], ["rhs4_%d" % gi])
                    rR = ["rhs4_%d" % gi, "consts"]
                    bD = 2 + gi * 2
                    bA = 3 + gi * 2
                    rv = rhs4[gi][:].rearrange("p a b -> p (a b)")
                    s.mm(ps[:, bD, :], m1, rv, True, True, rR, [B(bD)])
                    s.mm(ps[:, bA, :], ones, rv, True, True, rR, [B(bA)])
                    s.act(E4[gi][:].rearrange("p a b -> p (a b)"), ps[:, bD, :], AF.Exp, [B(bD)], ["E4_%d" % gi])
                    s.act(F4[gi][:].rearrange("p a b -> p (a b)"), ps[:, bA, :], AF.Exp, [B(bA)], ["F4_%d" % gi])
                    s.tt("dve", M4[gi][:], E4[gi][:], cbm[d_][:, g:g + 1, :].broadcast_to([128, 4, 128]), ALU.mult,
                         ["E4_%d" % gi, ("cbm%d" % d_, g // 4)], ["M4_%d" % gi])
                    s.tt("dve", N4[gi][:], F4[gi][:], ctl[i][:, g:g + 1, :].broadcast_to([128, 4, 128]), ALU.mult,
                         ["F4_%d" % gi, "ctl%d" % i], ["N4_%d" % gi])

                def stageB(idx):
                    d_, g = items[idx]
                    gi = idx % 2
                    yo = ps[:, 6 + g % 2, 0:256]
                    for r4 in range(4):
                        h = g * 4 + r4
                        s.mm(yo[:, r4 * 64:(r4 + 1) * 64], M4[gi][:, r4, :], xdl[d_][i][:, h * 64:(h + 1) * 64],
                             True, False, ["M4_%d" % gi, "xdl%d_%d" % (d_, i)], [B(6 + g % 2)])
                        s.mm(yo[:, r4 * 64:(r4 + 1) * 64], N4[gi][:, r4, :], hsl[d_][i][:, h * 64:(h + 1) * 64],
                             False, True, ["N4_%d" % gi, "hsl%d_%d" % (d_, i)], [B(6 + g % 2)])
                    yb = ybuf[:, g * 256:(g + 1) * 256]
                    if d_ == 0:
                        s.cp("dve", yb, yo, [B(6 + g % 2)], [("ybuf", yi, g)])
                    else:
                        s.tt("dve", yb, yo, yb, ALU.add, [B(6 + g % 2), ("ybuf", yi, g)], [("ybuf", yi, g)])

                stageA(0)
                for idx in range(16):
                    if idx + 1 < 16:
                        stageA(idx + 1)
                    stageB(idx)
                    tick()

            def tail(n):
                c = chunks2[n]
                yi = n % 2
                zi = n % 2
                ybuf = ybufs[yi]
                z2 = z2s[zi]
                xt_ = xtk[n % 3]
                xk = "xtk%d" % (n % 3)
                yK = [("ybuf", yi, g) for g in range(8)]
                Z = lambda q: ("z2", zi, q)
                jA = ycat[:, 2048:4096]
                s.tt("pool", tmp[:].rearrange("p (h e) -> p h e", e=64), xt_[:].rearrange("p (h e) -> p h e", e=64),
                     dskip[:, :, None].broadcast_to([128, 32, 64]), ALU.mult, [xk, "dskip"], ["tmp"])
                yield
                s.tt("dve", ybuf[:], ybuf[:], tmp[:], ALU.add, yK + ["tmp"], yK)
                yield
                s.act(tmp[:], z2[:, 0:2048], AF.Silu, [Z(0)], ["tmp"])
                yield
                s.tt("dve", ybuf[:], ybuf[:], tmp[:], ALU.mult, yK + ["tmp"], yK)
                yield
                s.act(jA, ybuf[:], AF.Square, yK, ["ycat_b", "ss_a"], accum=st5[:, 0:1])
                rms_r(st5[:, 0:1], st5[:, 1:2], D, ["ss_a"], ["r_a"])
                yield
                s.stt(ycat[:, 0:2048], ybuf[:], st5[:, 1:2], gssd[:], ALU.mult, ALU.mult, yK + ["r_a", "gssd"], ["ycat_a"])
                yield
                s.act(jA, z2[:, 4096:6144], AF.Square, [Z(2)], ["ycat_b", "ss_v"], accum=st5[:, 2:3])
                rms_r(st5[:, 2:3], st5[:, 3:4], D, ["ss_v"], ["r_v"])
                yield
                s.stt(vn[:], z2[:, 4096:6144], st5[:, 3:4], gv[:], ALU.mult, ALU.mult, [Z(2), "r_v", "gv"], ["vn"])
                yield
                for qb in range(4):
                    bk = qb % 2
                    for gq in range(4):
                        g = qb * 4 + gq
                        s.mm(ps[:, bk, gq * 128:(gq + 1) * 128], wsTb[:, g, :], vn[:, g * 128:(g + 1) * 128], True, True,
                             ["wsTb", "vn"], [B(bk)])
                    s.tt("dve", mbuf[:, qb * 512:(qb + 1) * 512].rearrange("p (g e) -> p g e", e=128),
                         ps[:, bk, :].rearrange("p (g e) -> p g e", e=128),
                         bsT[:, qb * 4:(qb + 1) * 4, None].broadcast_to([128, 4, 128]), ALU.add, [B(bk), "bsT"], [("mbuf", qb)])
                    yield
                mK = [("mbuf", qb) for qb in range(4)]
                s.tt("pool", mbuf[:], mbuf[:], z2[:, 2048:4096], ALU.mult, mK + [Z(1)], mK)
                yield
                s.act(tmp[:], z2[:, 6144:8192], AF.Silu, [Z(3)], ["tmp"])
                yield
                s.tt("dve", mbuf[:], mbuf[:], tmp[:], ALU.mult, mK + ["tmp"], mK)
                yield
                s.act(jA, mbuf[:], AF.Square, mK, ["ycat_b", "ss_m"], accum=st5[:, 4:5])
                rms_r(st5[:, 4:5], st5[:, 5:6], D, ["ss_m"], ["r_m"])
                yield
                s.stt(ycat[:, 2048:4096], mbuf[:], st5[:, 5:6], gmlp[:], ALU.mult, ALU.mult, mK + ["r_m", "gmlp"], ["ycat_b"])
                yield
                yt = yTs[n % 2]
                ytk = "yTs%d" % (n % 2)
                for kb in range(8):
                    bk = kb % 2
                    for kq in range(4):
                        k = kb * 4 + kq
                        s.mm(ps[:, bk, kq * 128:(kq + 1) * 128], ycat[:, k * 128:(k + 1) * 128], identb[:], True, True,
                             ["ycat_a" if k < 16 else "ycat_b", "identb"], [B(bk)])
                    s.cp("act" if kb % 2 == 0 else "dve", yt[:, kb * 4:(kb + 1) * 4, :],
                         ps[:, bk, :].rearrange("p (c t) -> p c t", t=128), [B(bk)], [(ytk, kb)])
                    yield
                s.dma(yT_d[c].rearrange("p (k t) -> p k t", t=128), yt[:], [(ytk, k) for k in range(8)], [("yT_d", c)],
                      key="st_" + ytk)

            def drain(gen):
                if gen is not None:
                    for _ in gen:
                        pass

            loads(0)
            z2load(0)
            prev = None
            for n in range(NCH):
                if n + 1 < NCH:
                    loads(n + 1)

                def tick(gen=prev):
                    if gen is not None:
                        for _ in range(2):
                            try:
                                next(gen)
                            except StopIteration:
                                break
                ssd(n, tick)
                drain(prev)
                if n + 1 < NCH:
                    z2load(n + 1)
                prev = tail(n)
            drain(prev)
            s.emit()
            pc[0] += 1
            if pc[0] >= STOP:
                return nc, s

        with ExitStack() as es:
            def T(name, shape, dt=F32):
                uid[0] += 1
                return es.enter_context(nc.sbuf_tensor("t%d_%s" % (uid[0], name), list(shape), dt))
            ggb = [T("ggb%d" % r, [128, D]) for r in range(2)]
            for r in range(2):
                s.dma(ggb[r][:], gg_d[r].partition_broadcast(128), (), ["ggb%d" % r])
            yTl = T("yTl", [128, 32, 512], BF16)
            wo_ = [T("wo%d" % i, [128, 32, 512], BF16) for i in range(2)]
            osb = T("osb", [128, 4, D])
            xr = [T("xr%d" % i, [128, D]) for i in range(2)]
            junk = T("junk6", [128, D], BF16)
            st6 = [T("st6_%d" % i, [128, 2]) for i in range(2)]
            sbs = [list(range(NCTX))] + [list(range(NCTX + 4 * i, NCTX + 4 * i + 4)) for i in range(NLAT // 4)]
            if last:
                sbs = sbs[1:]
            wcnt = 0
            ecnt = 0
            ccnt = 0
            for sbi, chs in enumerate(sbs):
                for ci, c in enumerate(chs):
                    s.dma(yTl[:, :, ci * 128:(ci + 1) * 128], yT_d[c].rearrange("p (k t) -> p k t", t=128), (), [("yTl", ci)])
                for jb in range(4):
                    wi_ = wcnt % 2
                    wcnt += 1
                    for hh2 in range(4):
                        s.dma(wo_[wi_][:, hh2 * 8:(hh2 + 1) * 8, :],
                              WOs_d[jb][:, hh2 * 4096:(hh2 + 1) * 4096].rearrange("p (k c) -> p k c", c=512),
                              (), [("wo%d" % wi_, hh2)])
                    for ci, c in enumerate(chs):
                        bk = ecnt % 8
                        ecnt += 1
                        o = ps[:, bk, :]
                        for k in range(32):
                            s.mm(o, yTl[:, k, ci * 128:(ci + 1) * 128], wo_[wi_][:, k, :], k == 0, k == 31,
                                 [("yTl", ci), ("wo%d" % wi_, k // 8)], [B(bk)])
                        s.cp("act" if ecnt % 2 == 0 else "dve", osb[:, ci, jb * 512:(jb + 1) * 512], o, [B(bk)], [("osb", ci, jb)])
                for ci, c in enumerate(chs):
                    i = ccnt % 2
                    ccnt += 1
                    r = 1 if c < NCTX else 0
                    oR = [("osb", ci, jb) for jb in range(4)]
                    s.dma(xr[i][:], xrow(c), (), ["xr%d" % i])
                    s.act(junk[:], osb[:, ci, :], AF.Square, oR, ["junk6", "ss6_%d" % i], accum=st6[i][:, 0:1])
                    rms_r(st6[i][:, 0:1], st6[i][:, 1:2], D, ["ss6_%d" % i], ["r6_%d" % i])
                    s.stt(osb[:, ci, :], osb[:, ci, :], st6[i][:, 1:2], ggb[r][:], ALU.mult, ALU.mult,
                          oR + ["r6_%d" % i, "ggb%d" % r], oR)
                    s.tt("pool", xr[i][:], xr[i][:], osb[:, ci, :], ALU.add, ["xr%d" % i] + oR, ["xr%d" % i])
                    s.dma(xdst(c), xr[i][:], ["xr%d" % i], [("xout", c)], key="st_xr%d" % i)
            s.emit()
            pc[0] += 1
            if pc[0] >= STOP:
                return nc, s
    return nc, s


def _consts():
    j = np.arange(128)[:, None]
    q = np.arange(128)[None, :]
    c = np.zeros((128, 6, 128), np.float32)
    c[:, 0] = (j == q)
    c[:, 1] = (j <= q)
    c[:, 2] = (j >= q)
    c[:, 3] = (j > q)
    c[:, 4] = (j < q)
    c[:, 5] = 1.0
    return c


def _prep(inp, b, half, layers, x_b, ctx_b):
    ls = list(layers)
    nl = len(ls)
    f = lambda a: np.ascontiguousarray(a, dtype=np.float32)
    flip = (half == 1)
    cT = np.stack([inp["c"][b].reshape(16, 128).T, inp["c_ctx"].reshape(16, 128).T], axis=-1)
    b_ada = inp["b_ada"][ls]
    w_in = inp["w_in"][ls]
    dt_bias = inp["dt_bias"][ls]
    a_log = inp["a_log"][ls]
    conv_w = inp["conv_w"][ls]
    w_s = inp["w_s"][ls]
    b_s = inp["b_s"][ls]
    if flip:
        w_in = np.concatenate([w_in[:, :, :4096], w_in[:, :, 4128:4160], w_in[:, :, 4096:4128], w_in[:, :, 4160:]], axis=2)
        dt_bias = dt_bias[:, ::-1]
        a_log = a_log[:, ::-1]
        conv_w = conv_w[:, ::-1]
        w_s = w_s[:, :, ::-1, ::-1]
        b_s = b_s[:, :, ::-1]
        x_b = x_b[::-1]
        ctx_b = ctx_b[::-1]
    sel = np.zeros((128, 2), np.float32)
    sel[:, 1 - half] = 1.0
    m = {
        "x_in": f(x_b), "ctx_in": f(ctx_b), "cT": f(cT), "consts": _consts(), "sel": sel,
        "w_ada": f(inp["w_ada"][ls]), "w_in": f(w_in), "w_out": f(inp["w_out"][ls]),
        "badaT": f(b_ada[:, :4096].reshape(nl, 32, 128).transpose(0, 2, 1)),
        "bgate": f(np.repeat(b_ada[:, None, 4096:], 2, axis=1)),
        "gpost": f(np.repeat(inp["g_post"][ls][:, None, :], 2, axis=1)),
        "gpreT": f(inp["g_pre"][ls].reshape(nl, 16, 128).transpose(0, 2, 1)),
        "convwT": f(conv_w.reshape(nl, 5, 32, 128).transpose(0, 3, 2, 1)),
        "convbT": f(inp["conv_b"][ls].reshape(nl, 32, 128).transpose(0, 2, 1)),
        "dtb": f(np.broadcast_to(dt_bias.reshape(nl, 1, 64), (nl, 128, 64))),
        "alog": f(np.broadcast_to(a_log.reshape(nl, 1, 64), (nl, 128, 64))),
        "dskip": f(np.broadcast_to(inp["d_skip"][ls].reshape(nl, 1, 32), (nl, 128, 32))),
        "gssd": f(inp["g_ssd"][ls]), "gv": f(inp["g_v"][ls]), "gmlp": f(inp["g_mlp"][ls]),
        "wsT": f(w_s.transpose(0, 3, 1, 2)),
        "bsT": f(b_s.transpose(0, 2, 1)),
    }
    return m


LAUNCH_GROUPS = [[0, 1, 2, 3]]
STOP = 10 ** 9
DEBUG = False
LAST = {}


def kernel(**inp):
    inp = {k: np.asarray(v) for k, v in inp.items()}
    x = inp["x"]
    ctx = inp["ctx"]
    nb, Lx, _ = x.shape
    Lh = Lx // 2
    NLAT = Lh // 128
    ncore = 2 * nb
    xs = [x[c // 2, (c % 2) * Lh:(c % 2 + 1) * Lh] for c in range(ncore)]
    cs = [ctx[c // 2] for c in range(ncore)]
    res = None
    for gi, layers in enumerate(LAUNCH_GROUPS):
        final = (gi == len(LAUNCH_GROUPS) - 1)
        nc, _ = build(NLAT, layers, gi == 0, final)
        in_maps = [_prep(inp, c // 2, c % 2, layers, xs[c], cs[c]) for c in range(ncore)]
        res = run_bass_kernel_spmd(nc, in_maps, core_ids=list(range(ncore)))
        LAST["res"] = res
        xs = [np.asarray(res.results[c]["out"])[::-1] if c % 2 else np.asarray(res.results[c]["out"]) for c in range(ncore)]
        cs = [np.asarray(res.results[c]["ctx_out"])[::-1] if c % 2 else np.asarray(res.results[c]["ctx_out"]) for c in range(ncore)]
    out = np.empty((nb, Lx, x.shape[2]), np.float32)
    for c in range(ncore):
        out[c // 2, (c % 2) * Lh:(c % 2 + 1) * Lh] = xs[c]
    return out
```

```python
import numpy as np
from contextlib import ExitStack
import concourse.bass as bass
import concourse.mybir as mybir
from concourse.bass_utils import run_bass_kernel_spmd

F32 = mybir.dt.float32
BF16 = mybir.dt.bfloat16
AF = mybir.ActivationFunctionType
ALU = mybir.AluOpType

D = 2048
DEPTH = 4
NCTX = 2
H = 32
XBC = 4096
INW = 12352
EPS = 1e-6
NCORES = 4


class Sch:
    ENG = ("pe", "act", "dve", "pool", "sp")

    def __init__(s, nc):
        s.nc = nc
        s.esem = {k: nc.alloc_semaphore("es_" + k) for k in ("pe", "act", "dve", "pool")}
        s.ecnt = {k: 0 for k in s.esem}
        s.pool = []
        s.poolcnt = []
        s.csems = []
        s.ninstr = 0
        s.reset()

    def reset(s):
        s.st = {k: [] for k in s.ENG}
        s.lastw = {}
        s.rd = {}
        s.seen = {k: {} for k in s.ENG}
        s.keysem = {}

    def _deps(s, eng, reads, writes):
        deps = []
        for k in reads:
            t = s.lastw.get(k)
            if t is not None:
                deps.append(t)
        for k in writes:
            t = s.lastw.get(k)
            if t is not None:
                deps.append(t)
            deps.extend(s.rd.get(k, ()))
        out = []
        seen = s.seen[eng]
        for t in deps:
            if t[0] == "E" and t[1] == eng and eng == "pe":
                continue
            kk = (t[0], t[1])
            if seen.get(kk, -1) >= t[2]:
                continue
            seen[kk] = t[2]
            out.append(t)
        return out

    def _post(s, tok, reads, writes):
        for k in writes:
            s.lastw[k] = tok
            s.rd[k] = []
        for k in reads:
            s.rd.setdefault(k, []).append(tok)

    def op(s, eng, fn, reads=(), writes=()):
        deps = s._deps(eng, reads, writes)
        idx = len(s.st[eng])
        s.st[eng].append([fn, deps, False, None])
        tok = ("E", eng, idx)
        s._post(tok, reads, writes)
        return tok

    def dma(s, out, in_, reads=(), writes=(), key=None, q="sp"):
        if key is None:
            key = writes[0]
        deps = s._deps(q, reads, writes)
        if key not in s.keysem:
            n = len(s.keysem)
            if n >= len(s.pool):
                s.pool.append(s.nc.alloc_semaphore("ds%d" % n))
                s.poolcnt.append(0)
            s.keysem[key] = n
        si = s.keysem[key]
        s.poolcnt[si] += 16
        tok = ("D", si, s.poolcnt[si])
        s.st[q].append([lambda e: e.dma_start(out=out, in_=in_), deps, False, tok])
        s._post(tok, reads, writes)
        return tok

    def coll(s, fn, reads=(), writes=()):
        deps = s._deps("pool", reads, writes)
        sem = s.nc.alloc_semaphore("cc%d" % len(s.csems))
        s.csems.append(sem)
        tok = ("C", len(s.csems) - 1, 1)
        s.st["pool"].append([fn, deps, False, tok])
        s._post(tok, reads, writes)
        return tok

    def mm(s, out, lhsT, rhs, start, stop, R, W):
        s.op("pe", lambda e: e.matmul(out, lhsT=lhsT, rhs=rhs, start=start, stop=stop), R, W)

    def tr(s, out, in_, ident, R, W):
        s.op("pe", lambda e: e.transpose(out=out, in_=in_, identity=ident), R, W)

    def act(s, out, in_, func, R, W, bias=None, scale=None, accum=None):
        def f(e):
            kw = {}
            if bias is not None:
                kw["bias"] = bias
            if scale is not None:
                kw["scale"] = scale
            if accum is not None:
                kw["accum_out"] = accum
            return e.activation(out=out, in_=in_, func=func, **kw)
        s.op("act", f, R, W)

    def tt(s, eng, out, in0, in1, op, R, W):
        s.op(eng, lambda e: e.tensor_tensor(out=out, in0=in0, in1=in1, op=op), R, W)

    def ts(s, eng, out, in0, s1, s2, op0, op1, R, W):
        if op1 is None:
            s.op(eng, lambda e: e.tensor_scalar(out=out, in0=in0, scalar1=s1, scalar2=None, op0=op0), R, W)
        else:
            s.op(eng, lambda e: e.tensor_scalar(out=out, in0=in0, scalar1=s1, scalar2=s2, op0=op0, op1=op1), R, W)

    def stt(s, out, in0, sc, in1, op0, op1, R, W):
        s.op("dve", lambda e: e.scalar_tensor_tensor(out=out, in0=in0, scalar=sc, in1=in1, op0=op0, op1=op1), R, W)

    def cp(s, eng, out, in_, R, W):
        if eng == "act":
            s.op(eng, lambda e: e.copy(out=out, in_=in_), R, W)
        else:
            s.op(eng, lambda e: e.tensor_copy(out=out, in_=in_), R, W)

    def recip(s, out, in_, R, W):
        s.op("dve", lambda e: e.reciprocal(out=out, in_=in_), R, W)

    def memset(s, eng, ap, val, W):
        s.op(eng, lambda e: e.memset(ap, val), (), W)

    def emit(s):
        nc = s.nc
        fin = [("D", si, s.poolcnt[si]) for si in sorted(s.keysem.values())]
        s.st["sp"].append([None, fin, False, None])
        needed = {k: set() for k in s.esem}
        for eng in s.ENG:
            for ent in s.st[eng]:
                for t in ent[1]:
                    if t[0] == "E":
                        needed[t[1]].add(t[2])
        vals = {}
        for eng in s.esem:
            c = s.ecnt[eng]
            v = {}
            for i in sorted(needed[eng]):
                c += 1
                v[i] = c
            vals[eng] = v
            s.ecnt[eng] = c
        engobj = dict(pe="tensor", act="scalar", dve="vector", pool="gpsimd", sp="sync")
        with nc.Block() as block:
            for eng in s.ENG:
                stream = s.st[eng]
                if not stream:
                    continue

                def body(e, eng=eng, stream=stream):
                    nd = needed.get(eng, ())
                    for i, ent in enumerate(stream):
                        for t in ent[1]:
                            if t[0] == "E":
                                e.wait_ge(s.esem[t[1]], vals[t[1]][t[2]])
                            elif t[0] == "C":
                                e.wait_ge(s.csems[t[1]], 1)
                            else:
                                e.wait_ge(s.pool[t[1]], t[2])
                        if ent[0] is None:
                            continue
                        ins = ent[0](e)
                        s.ninstr += 1
                        if ent[3] is not None and ent[3][0] == "C":
                            ins.then_inc(s.csems[ent[3][1]])
                        elif ent[3] is not None:
                            ins.then_inc(s.pool[ent[3][1]], 16)
                        elif i in nd:
                            ins.then_inc(s.esem[eng], 1)
                getattr(block, engobj[eng])(body)
        s.reset()


def build(NLAT, layers, first, last_is_final):
    C = NCTX + NLAT
    L = len(layers)
    nc = bass.Bass("TRN2", target_bir_lowering=False)

    def din(name, shape, dt=F32):
        return nc.dram_tensor(name, list(shape), dt, kind="ExternalInput").ap()

    def dscr(name, shape, dt=F32):
        return nc.dram_tensor(name, list(shape), dt, kind=("ExternalOutput" if DEBUG else "Internal")).ap()

    x_in = din("x_in", [NLAT * 128, D])
    ctx_in = din("ctx_in", [NCTX * 128, D])
    cT_in = din("cT", [128, 16, 2])
    consts_in = din("consts", [128, 6, 128])
    w_ada = din("w_ada", [L, D, 3 * D])
    w_in = din("w_in", [L, D, INW])
    w_out = din("w_out", [L, 2 * D, D])
    badaT_in = din("badaT", [L, 128, 32])
    bgate_in = din("bgate", [L, 2, D])
    gpost_in = din("gpost", [L, 2, D])
    gpreT_in = din("gpreT", [L, 128, 16])
    convw_in = din("convwT", [L, 128, 32, 5])
    convb_in = din("convbT", [L, 128, 32])
    dtb_in = din("dtb", [L, 128, 64])
    alog_in = din("alog", [L, 128, 64])
    dskip_in = din("dskip", [L, 128, 32])
    gssd_in = din("gssd", [L, D])
    gv_in = din("gv", [L, D])
    gmlp_in = din("gmlp", [L, D])
    wsT_in = din("wsT", [L, 128, 16, 128])
    bsT_in = din("bsT", [L, 128, 16])
    sel_in = din("sel", [128, 2])

    out_d = nc.dram_tensor("out", [NLAT * 128, D], F32, kind="ExternalOutput").ap()
    ctx_out = nc.dram_tensor("ctx_out", [NCTX * 128, D], F32, kind="ExternalOutput").ap()

    hxT_d = dscr("hxT_d", [C, 128, 2048], BF16)
    W1s_d = dscr("W1s_d", [32, 128, 2048], BF16)
    W1dt_d = dscr("W1dt_d", [128, 1024], BF16)
    W2s_d = dscr("W2s_d", [16, 128, 8192], BF16)
    WOs_d = dscr("WOs_d", [4, 128, 16384], BF16)
    xtok_d = dscr("xtok_d", [C, 128, 2048], BF16)
    xdt_d = dscr("xdt_d", [2, C, 128, 2048], BF16)
    BT_d = dscr("BT_d", [C, 128, 1024], BF16)
    CT_d = dscr("CT_d", [C, 128, 1024], BF16)
    a_d = dscr("a_d", [C, 128, 64])
    cdec_d = dscr("cdec_d", [C, 128, 64])
    S_d = dscr("S_d", [2, C, 128, 2048])
    hst_d = dscr("hst_d", [2, C, 128, 2048], BF16)
    z2_d = dscr("z2_d", [C, 128, 8192], BF16)
    yT_d = dscr("yT_d", [C, 128, 4096], BF16)
    gg_d = dscr("gg_d", [2, D])
    send_d = [nc.dram_tensor("send_d%d" % i, [128, 2048], F32, kind="Internal").ap() for i in range(L)]
    recv_d = [nc.dram_tensor("recv_d%d" % i, [256, 2048], F32, kind="Internal").ap() for i in range(L)]

    s = Sch(nc)
    pc = [0]

    uid = [0]

    def sb(name, shape, dt=F32):
        return nc.alloc_sbuf_tensor("g_" + name, list(shape), dt)

    consts = sb("consts", [128, 6, 128])
    ident = consts[:, 0, :]
    Um = consts[:, 1, :]
    Lm = consts[:, 2, :]
    SLm = consts[:, 3, :]
    SUm = consts[:, 4, :]
    ones = consts[:, 5, :]
    identb = sb("identb", [128, 128], BF16)
    sT = sb("sT", [128, 16, 2])
    G1 = sb("G1", [128, 16, 2])
    shT = sb("shT", [128, 16, 2])
    adaT = sb("adaT", [128, 32, 2])
    badaT = sb("badaT", [128, 32])
    gpreT = sb("gpreT", [128, 16])
    convw = sb("convw", [128, 32, 5])
    convb = sb("convb", [128, 32])
    dtb = sb("dtb", [128, 64])
    Abc = sb("Abc", [128, 64])
    dskip = sb("dskip", [128, 32])
    bsT = sb("bsT", [128, 16])
    wsTb = sb("wsTb", [128, 16, 128], BF16)
    epst = sb("epst", [128, 1])
    selt = sb("selt", [128, 2])
    ps = nc.alloc_psum_tensor("ps", [128, 8, 512], F32)

    def B(i):
        return "ps%d" % i

    s.dma(consts[:], consts_in, (), ["consts"])
    s.dma(sT[:], cT_in, (), ["sT"])
    s.dma(selt[:], sel_in, (), ["selt"])
    s.cp("dve", identb[:], ident, ["consts"], ["identb"])
    s.act(sT[:], sT[:], AF.Silu, ["sT"], ["sT"])
    s.memset("dve", epst[:], EPS, ["epst"])
    s.emit()

    def rms_r(ssum, rr, n, R, W):
        s.ts("dve", rr, ssum, 1.0 / n, EPS, ALU.mult, ALU.add, R, W)
        s.act(rr, rr, AF.Sqrt, W, W)
        s.recip(rr, rr, W, W)

    for li, l in enumerate(layers):
        last = last_is_final and (li == L - 1)
        x_src = x_in if (li == 0) else out_d
        ctx_src = ctx_in if (li == 0) else ctx_out
        chunks2 = list(range(NCTX, C)) if last else list(range(C))

        def xrow(c, src_x=x_src, src_c=ctx_src):
            if c < NCTX:
                return src_c[c * 128:(c + 1) * 128, :]
            return src_x[(c - NCTX) * 128:(c - NCTX + 1) * 128, :]

        def xdst(c):
            if c < NCTX:
                return ctx_out[c * 128:(c + 1) * 128, :]
            return out_d[(c - NCTX) * 128:(c - NCTX + 1) * 128, :]

        with ExitStack() as es:
            def T(name, shape, dt=F32):
                uid[0] += 1
                return es.enter_context(nc.sbuf_tensor("t%d_%s" % (uid[0], name), list(shape), dt))
            s.dma(badaT[:], badaT_in[li], (), ["badaT"])
            s.dma(gpreT[:], gpreT_in[li], (), ["gpreT"])
            s.dma(convw[:], convw_in[li], (), ["convw"])
            s.dma(convb[:], convb_in[li], (), ["convb"])
            s.dma(dtb[:], dtb_in[li], (), ["dtb"])
            s.dma(Abc[:], alog_in[li], (), ["Abc"])
            s.dma(dskip[:], dskip_in[li], (), ["dskip"])
            s.dma(bsT[:], bsT_in[li], (), ["bsT"])
            wsf = T("wsf", [128, 16, 128])
            s.dma(wsf[:], wsT_in[li], (), ["wsf"])
            s.cp("dve", wsTb[:], wsf[:], ["wsf"], ["wsTb"])
            s.act(Abc[:], Abc[:], AF.Exp, ["Abc"], ["Abc"])
            s.ts("dve", Abc[:], Abc[:], -1.0, None, ALU.mult, None, ["Abc"], ["Abc"])
            bg = T("bg", [2, D])
            gp = T("gp", [2, D])
            gsb = T("gsb", [2, D])
            s.dma(bg[:], bgate_in[li], (), ["bg"])
            s.dma(gp[:], gpost_in[li], (), ["gp"])
            wblk = [T("wblk%d" % i, [128, 16, 512]) for i in range(2)]
            wa = w_ada[li].rearrange("(k p) c -> p k c", p=128)
            for jb in range(12):
                wb = wblk[jb % 2]
                wk = "wblk%d" % (jb % 2)
                for hh in range(2):
                    s.dma(wb[:, hh * 8:(hh + 1) * 8, :], wa[:, hh * 8:(hh + 1) * 8, jb * 512:(jb + 1) * 512],
                          (), [wk + "h%d" % hh])
                Rw = [wk + "h0", wk + "h1", "sT"]
                if jb < 8:
                    for t in range(4):
                        o = ps[:, 0, (jb * 4 + t) * 2:(jb * 4 + t) * 2 + 2]
                        for k in range(16):
                            s.mm(o, wb[:, k, t * 128:(t + 1) * 128], sT[:, k, :], k == 0, k == 15, Rw, [B(0)])
                else:
                    o = ps[0:2, jb - 7, :]
                    for k in range(16):
                        s.mm(o, sT[:, k, :], wb[:, k, :], k == 0, k == 15, Rw, [B(jb - 7)])
                    s.tt("dve", gsb[:, (jb - 8) * 512:(jb - 7) * 512], o, bg[:, (jb - 8) * 512:(jb - 7) * 512],
                         ALU.add, [B(jb - 7), "bg"], ["gsb"])
            s.tt("dve", adaT[:], ps[:, 0, 0:64].rearrange("p (j r) -> p j r", r=2),
                 badaT[:, :, None].broadcast_to([128, 32, 2]), ALU.add, [B(0), "badaT"], ["adaT"])
            s.cp("dve", shT[:], adaT[:, 0:16, :], ["adaT"], ["shT"])
            s.ts("dve", G1[:], adaT[:, 16:32, :], 1.0, None, ALU.add, None, ["adaT"], ["G1"])
            s.tt("dve", G1[:], G1[:], gpreT[:, :, None].broadcast_to([128, 16, 2]), ALU.mult, ["G1", "gpreT"], ["G1"])
            s.tt("dve", gsb[:], gsb[:], gp[:], ALU.mult, ["gsb", "gp"], ["gsb"])
            s.dma(gg_d, gsb[:], ["gsb"], ["gg_d"])
            stf = [T("stf%d" % i, [128, 4096]) for i in range(2)]
            stb = [T("stb%d" % i, [128, 4096], BF16) for i in range(2)]
            wi = w_in[li].rearrange("(k p) c -> p k c", p=128)
            wo = w_out[li].rearrange("(k p) c -> p k c", p=128)
            jobs = []
            for j in range(32):
                jobs.append((wi[:, :, j * 128:(j + 1) * 128], [16, 128], W1s_d[j]))
            jobs.append((wi[:, :, 4096:4160], [16, 64], W1dt_d))
            for jb in range(16):
                for hh in range(2):
                    jobs.append((wi[:, hh * 8:(hh + 1) * 8, 4160 + jb * 512:4160 + (jb + 1) * 512], [8, 512],
                                 W2s_d[jb][:, hh * 4096:(hh + 1) * 4096]))
            for jb in range(4):
                for hh in range(4):
                    jobs.append((wo[:, hh * 8:(hh + 1) * 8, jb * 512:(jb + 1) * 512], [8, 512],
                                 WOs_d[jb][:, hh * 4096:(hh + 1) * 4096]))
            cast_engs = ["act", "pool", "dve"]
            for n, (src, shp, dst) in enumerate(jobs):
                i = n % 2
                ne = shp[0] * shp[1]
                fv = stf[i][:, 0:ne].rearrange("p (a b) -> p a b", b=shp[1])
                s.dma(fv, src, (), ["stf%d" % i])
                s.cp(cast_engs[n % 3], stb[i][:, 0:ne], stf[i][:, 0:ne], ["stf%d" % i], ["stb%d" % i])
                s.dma(dst, stb[i][:, 0:ne], ["stb%d" % i], [("w", n)], key="st_stb%d" % i)
            s.emit()
            pc[0] += 1
            if pc[0] >= STOP:
                return nc, s

        with ExitStack() as es:
            def T(name, shape, dt=F32):
                uid[0] += 1
                return es.enter_context(nc.sbuf_tensor("t%d_%s" % (uid[0], name), list(shape), dt))
            xin = [T("xin%d" % i, [128, D]) for i in range(2)]
            xn = [T("xn%d" % i, [128, D]) for i in range(2)]
            junk = T("junk", [128, D], BF16)
            ssr = [T("ssr%d" % i, [128, 2]) for i in range(2)]
            hxs = [T("hxs%d" % i, [128, 16, 128], BF16) for i in range(2)]
            for c in range(C):
                i = c % 2
                r = 1 if c < NCTX else 0
                s.dma(xin[i][:], xrow(c), (), ["xin%d" % i])
                s.act(junk[:], xin[i][:], AF.Square, ["xin%d" % i], ["junk", "ss%d" % i], accum=ssr[i][:, 0:1])
                rms_r(ssr[i][:, 0:1], ssr[i][:, 1:2], D, ["ss%d" % i], ["rr%d" % i])
                s.ts("dve", xn[i][:], xin[i][:], ssr[i][:, 1:2], None, ALU.mult, None, ["xin%d" % i, "rr%d" % i], ["xn%d" % i])
                for kb in range(4):
                    bk = (c % 2) * 4 + kb
                    for kq in range(4):
                        k = kb * 4 + kq
                        s.mm(ps[:, bk, kq * 128:(kq + 1) * 128], xn[i][:, k * 128:(k + 1) * 128], ident, True, True,
                             ["xn%d" % i, "consts"], [B(bk)])
                    for kq in range(4):
                        k = kb * 4 + kq
                        s.ts("dve", hxs[i][:, k, :], ps[:, bk, kq * 128:(kq + 1) * 128], G1[:, k, r:r + 1], shT[:, k, r:r + 1],
                             ALU.mult, ALU.add, [B(bk), "G1", "shT"], [("hxs%d" % i, k)])
                s.dma(hxT_d[c].rearrange("p (k t) -> p k t", t=128), hxs[i][:],
                      [("hxs%d" % i, k) for k in range(16)], [("hxT_d", c)], key="st_hxs%d" % i)
            s.emit()
            pc[0] += 1
            if pc[0] >= STOP:
                return nc, s

        with ExitStack() as es:
            def T(name, shape, dt=F32):
                uid[0] += 1
                return es.enter_context(nc.sbuf_tensor("t%d_%s" % (uid[0], name), list(shape), dt))
            hx = [T("hx%d" % i, [128, 16, 512], BF16) for i in range(2)]
            w1 = [T("w1_%d" % i, [128, 16, 128], BF16) for i in range(3)]
            w1dt = T("w1dt", [128, 16, 64], BF16)
            acc = [T("acc%d" % i, [128, 512]) for i in range(2)]
            xbcj = [T("xbcj%d" % i, [128, 512], BF16) for i in range(2)]
            btsb = T("btsb", [128, 8, 512], BF16)
            ctsb = T("ctsb", [128, 8, 512], BF16)
            xtok = T("xtok", [128, 4, 2048], BF16)
            btok = T("btok", [128, 4, 1024], BF16)
            dtv = [T("dtv%d" % i, [128, 64]) for i in range(2)]
            asb = [T("asb%d" % i, [128, 64]) for i in range(2)]
            dec = [T("dec%d" % i, [128, 64]) for i in range(2)]
            cdec = [T("cdec%d" % i, [128, 64]) for i in range(2)]
            xdt = [[T("xdt%d_%d" % (d_, i), [128, 2048], BF16) for i in range(2)] for d_ in range(2)]
            xw = [T("xw%d" % d_, [128, 2048], BF16) for d_ in range(2)]
            Ssb = [T("Ssb%d" % i, [128, 2048]) for i in range(2)]
            s.dma(w1dt[:], W1dt_d.rearrange("p (k c) -> p k c", c=64), (), ["w1dt"])
            sbs = [list(range(NCTX))] + [list(range(NCTX + 4 * i, NCTX + 4 * i + 4)) for i in range(NLAT // 4)]
            wcnt = 0
            ccnt = 0
            for sbi, chs in enumerate(sbs):
                Tn = 128 * len(chs)
                rowlen = 256 if sbi == 0 else 64
                nrows = Tn // rowlen
                hb = hx[sbi % 2]
                hk = "hx%d" % (sbi % 2)
                for ci, c in enumerate(chs):
                    s.dma(hb[:, :, ci * 128:(ci + 1) * 128], hxT_d[c].rearrange("p (k t) -> p k t", t=128),
                          (), [(hk, ci)])
                hR = [(hk, ci) for ci in range(len(chs))]
                pend = None
                for j in range(32):
                    wi_ = wcnt % 3
                    wcnt += 1
                    s.dma(w1[wi_][:], W1s_d[j].rearrange("p (k c) -> p k c", c=128), (), ["w1_%d" % wi_])
                    gb = j % 2
                    o = ps[:, gb, 0:Tn]
                    for k in range(16):
                        s.mm(o, w1[wi_][:, k, :], hb[:, k, 0:Tn], k == 0, k == 15, ["w1_%d" % wi_] + hR, [B(gb)])
                    a_ = acc[gb]
                    ak = "acc%d" % gb
                    s.ts("dve", a_[:, 0:Tn], o, convw[:, j, 2:3], convb[:, j:j + 1], ALU.mult, ALU.add,
                         [B(gb), "convw", "convb"], [ak])
                    ov = o.rearrange("p (r t) -> p r t", t=rowlen)
                    av = a_[:, 0:Tn].rearrange("p (r t) -> p r t", t=rowlen)
                    for kk in (0, 1, 3, 4):
                        sh = kk - 2
                        if sh < 0:
                            src = ov[:, :, 0:rowlen + sh]
                            dst = av[:, :, -sh:rowlen]
                        else:
                            src = ov[:, :, sh:rowlen]
                            dst = av[:, :, 0:rowlen - sh]
                        s.stt(dst, src, convw[:, j, kk:kk + 1], dst, ALU.mult, ALU.add, [B(gb), "convw", ak], [ak])
                    if j < 16:
                        tgt = xbcj[gb][:, 0:Tn]
                        tk = "xbcj%d" % gb
                    elif j < 24:
                        tgt = btsb[:, j - 16, 0:Tn]
                        tk = ("btsb", j - 16)
                    else:
                        tgt = ctsb[:, j - 24, 0:Tn]
                        tk = ("ctsb", j - 24)
                    s.act(tgt, a_[:, 0:Tn], AF.Silu, [ak], [tk])
                    if pend is not None:
                        pend()
                        pend = None
                    if j < 24:
                        def pend(j=j, tgt=tgt, tk=tk, nch=len(chs)):
                            for ci in range(nch):
                                s.mm(ps[:, 2, ci * 128:(ci + 1) * 128], tgt[:, ci * 128:(ci + 1) * 128], identb[:], True, True,
                                     [tk, "identb"], [B(2)])
                            pv = ps[:, 2, 0:nch * 128].rearrange("p (c t) -> p c t", t=128)
                            if j < 16:
                                s.cp("act", xtok[:, 0:nch, j * 128:(j + 1) * 128], pv, [B(2)], [("xtok", j)])
                            else:
                                s.cp("act", btok[:, 0:nch, (j - 16) * 128:(j - 15) * 128], pv, [B(2)], [("btok", j - 16)])
                if pend is not None:
                    pend()
                    pend = None
                for ci, c in enumerate(chs):
                    i = ccnt % 2
                    ccnt += 1
                    o = ps[:, 3, 0:64]
                    for k in range(16):
                        s.mm(o, hb[:, k, ci * 128:(ci + 1) * 128], w1dt[:, k, :], k == 0, k == 15, [(hk, ci), "w1dt"], [B(3)])
                    dk_ = "dtv%d" % i
                    s.tt("dve", dtv[i][:], o, dtb[:], ALU.add, [B(3), "dtb"], [dk_])
                    s.act(dtv[i][:], dtv[i][:], AF.Exp, [dk_], [dk_])
                    s.ts("dve", dtv[i][:], dtv[i][:], 1.0, None, ALU.add, None, [dk_], [dk_])
                    s.act(dtv[i][:], dtv[i][:], AF.Ln, [dk_], [dk_])
                    s.tt("dve", asb[i][:], dtv[i][:], Abc[:], ALU.mult, [dk_, "Abc"], ["asb%d" % i])
                    o2 = ps[:, 3, 64:192]
                    s.mm(o2[:, 0:32], SLm, asb[i][:, 0:32], True, True, ["consts", "asb%d" % i], [B(3)])
                    s.mm(o2[:, 32:64], SUm, asb[i][:, 32:64], True, True, ["consts", "asb%d" % i], [B(3)])
                    s.mm(o2[:, 64:128], ones, asb[i][:, 0:64], True, True, ["consts", "asb%d" % i], [B(3)])
                    s.act(dec[i][:], o2[:, 0:64], AF.Exp, [B(3)], ["dec%d" % i])
                    s.act(cdec[i][:], o2[:, 64:128], AF.Exp, [B(3)], ["cdec%d" % i])
                    xR = [("xtok", j) for j in range(16)]
                    xv = xtok[:, ci, :].rearrange("p (h e) -> p h e", e=64)
                    for d_ in range(2):
                        xd = xdt[d_][i]
                        s.tt("dve" if d_ == 0 else "pool", xd[:].rearrange("p (h e) -> p h e", e=64), xv,
                             dtv[i][:, d_ * 32:(d_ + 1) * 32, None].broadcast_to([128, 32, 64]), ALU.mult,
                             xR + [dk_], ["xdt%d_%d" % (d_, i)])
                        s.tt("dve" if d_ == 0 else "pool", xw[d_][:].rearrange("p (h e) -> p h e", e=64),
                             xd[:].rearrange("p (h e) -> p h e", e=64),
                             dec[i][:, d_ * 32:(d_ + 1) * 32, None].broadcast_to([128, 32, 64]), ALU.mult,
                             ["xdt%d_%d" % (d_, i), "dec%d" % i], ["xw%d" % d_])
                        sk = "Ssb%d" % d_
                        for hf in range(2):
                            b0 = 4 + 2 * hf
                            for gq in range(4):
                                g = hf * 4 + gq
                                s.mm(ps[:, b0 + gq // 2, (gq % 2) * 256:(gq % 2 + 1) * 256], btok[:, ci, g * 128:(g + 1) * 128],
                                     xw[d_][:, g * 256:(g + 1) * 256], True, True,
                                     [("btok", g), "xw%d" % d_], [B(b0 + gq // 2)])
                            s.cp("act", Ssb[d_][:, hf * 1024:(hf + 1) * 1024].rearrange("p (b f) -> p b f", f=512),
                                 ps[:, b0:b0 + 2, :], [B(b0), B(b0 + 1)], [(sk, hf)])
                        s.dma(S_d[d_][c], Ssb[d_][:], [(sk, 0), (sk, 1)], [("S_d", d_, c)], key="st_" + sk)
                        s.dma(xdt_d[d_][c], xd[:], ["xdt%d_%d" % (d_, i)], [("xdt_d", d_, c)], key="st_xdt%d_%d" % (d_, i))
                    s.dma(xtok_d[c], xtok[:, ci, :], xR, [("xtok_d", c)], key="st_xtok%d" % ci)
                    s.dma(BT_d[c].rearrange("p (g t) -> p g t", t=128), btsb[:, :, ci * 128:(ci + 1) * 128],
                          [("btsb", g) for g in range(8)], [("BT_d", c)], key="st_bt%d" % ci)
                    s.dma(CT_d[c].rearrange("p (g t) -> p g t", t=128), ctsb[:, :, ci * 128:(ci + 1) * 128],
                          [("ctsb", g) for g in range(8)], [("CT_d", c)], key="st_ct%d" % ci)
                    s.dma(a_d[c], asb[i][:], ["asb%d" % i], [("a_d", c)], key="st_asb%d" % i)
                    s.dma(cdec_d[c], cdec[i][:], ["cdec%d" % i], [("cdec_d", c)], key="st_cdec%d" % i)
            s.emit()
            pc[0] += 1
            if pc[0] >= STOP:
                return nc, s

        with ExitStack() as es:
            def T(name, shape, dt=F32):
                uid[0] += 1
                return es.enter_context(nc.sbuf_tensor("t%d_%s" % (uid[0], name), list(shape), dt))
            hh_ = [T("h%d" % d_, [128, 2048]) for d_ in range(2)]
            Sl = [[T("Sl%d_%d" % (d_, i), [128, 2048]) for i in range(2)] for d_ in range(2)]
            cdl = [[T("cdl%d_%d" % (d_, i), [128, 64]) for i in range(2)] for d_ in range(2)]
            hbf = [[T("hbf%d_%d" % (d_, i), [128, 2048], BF16) for i in range(2)] for d_ in range(2)]
            Rv = [T("Rv%d" % i, [128, 2048]) for i in range(2)]
            cnts = [0, 0]

            def step(d_, c, e_):
                i = cnts[d_] % 2
                cnts[d_] += 1
                hk = "h%d" % d_
                s.dma(Sl[d_][i][:], S_d[d_][c], (), ["Sl%d_%d" % (d_, i)])
                s.dma(cdl[d_][i][:], cdec_d[c], (), ["cdl%d_%d" % (d_, i)])
                s.cp("act", hbf[d_][i][:], hh_[d_][:], [hk], ["hbf%d_%d" % (d_, i)])
                s.dma(hst_d[d_][c], hbf[d_][i][:], ["hbf%d_%d" % (d_, i)], [("hst_d", d_, c)], key="st_hbf%d_%d" % (d_, i))
                hv = hh_[d_][:].rearrange("p (h e) -> p h e", e=64)
                s.tt(e_, hv, hv, cdl[d_][i][:, d_ * 32:(d_ + 1) * 32, None].broadcast_to([128, 32, 64]), ALU.mult,
                     [hk, "cdl%d_%d" % (d_, i)], [hk])
                s.tt(e_, hh_[d_][:], hh_[d_][:], Sl[d_][i][:], ALU.add, [hk, "Sl%d_%d" % (d_, i)], [hk])

            s.memset("dve", hh_[0][:], 0.0, ["h0"])
            s.memset("pool", hh_[1][:], 0.0, ["h1"])
            for c in (1, 0):
                step(1, c, "pool")
            for c in range(C):
                step(0, c, "dve")
            s.dma(send_d[li], hh_[0][:], ["h0"], ["send_d"], key="st_send")
            s.coll(lambda e, a=send_d[li], b=recv_d[li]: e.collective_compute(
                "AllGather", ALU.bypass, replica_groups=[[0, 1], [2, 3], [4, 5], [6, 7]], ins=[a], outs=[b]),
                ["send_d"], ["recv_d"])
            for r_ in range(2):
                s.dma(Rv[r_][:], recv_d[li][r_ * 128:(r_ + 1) * 128, :], ["recv_d"], ["Rv%d" % r_])
            s.ts("dve", hh_[1][:], Rv[0][:], selt[:, 0:1], None, ALU.mult, None, ["Rv0", "selt", "h1"], ["h1"])
            s.stt(hh_[1][:], Rv[1][:], selt[:, 1:2], hh_[1][:], ALU.mult, ALU.add, ["Rv1", "selt", "h1"], ["h1"])
            for c in range(C - 1, NCTX - 1, -1):
                step(1, c, "dve")
            s.emit()
            pc[0] += 1
            if pc[0] >= STOP:
                return nc, s

        with ExitStack() as es:
            def T(name, shape, dt=F32):
                uid[0] += 1
                return es.enter_context(nc.sbuf_tensor("t%d_%s" % (uid[0], name), list(shape), dt))
            hx = [T("hx%d" % i, [128, 16, 1024], BF16) for i in range(2)]
            w2 = [T("w2_%d" % i, [128, 16, 512], BF16) for i in range(3)]
            zst = [T("zst%d" % i, [128, 512], BF16) for i in range(4)]
            sbs = [list(range(NCTX))] + [list(range(NCTX + 8 * i, NCTX + 8 * i + 8)) for i in range(NLAT // 8)]
            if last:
                sbs = sbs[1:]
            wcnt = 0
            ecnt = 0
            for sbi, chs in enumerate(sbs):
                hb = hx[sbi % 2]
                hk = "hx%d" % (sbi % 2)
                for ci, c in enumerate(chs):
                    s.dma(hb[:, :, ci * 128:(ci + 1) * 128], hxT_d[c].rearrange("p (k t) -> p k t", t=128), (), [(hk, ci)])
                for jb in range(16):
                    wi_ = wcnt % 3
                    wcnt += 1
                    for hh2 in range(2):
                        s.dma(w2[wi_][:, hh2 * 8:(hh2 + 1) * 8, :],
                              W2s_d[jb][:, hh2 * 4096:(hh2 + 1) * 4096].rearrange("p (k c) -> p k c", c=512),
                              (), [("w2_%d" % wi_, hh2)])
                    for ci, c in enumerate(chs):
                        bk = ecnt % 8
                        zi = ecnt % 4
                        ecnt += 1
                        o = ps[:, bk, :]
                        for k in range(16):
                            s.mm(o, hb[:, k, ci * 128:(ci + 1) * 128], w2[wi_][:, k, :], k == 0, k == 15,
                                 [(hk, ci), ("w2_%d" % wi_, k // 8)], [B(bk)])
                        s.cp("act" if ecnt % 2 == 0 else "dve", zst[zi][:], o, [B(bk)], ["zst%d" % zi])
                        s.dma(z2_d[c][:, jb * 512:(jb + 1) * 512], zst[zi][:], ["zst%d" % zi], [("z2_d", c, jb)],
                              key="st_zst%d" % zi)
            s.emit()
            pc[0] += 1
            if pc[0] >= STOP:
                return nc, s

        with ExitStack() as es:
            def T(name, shape, dt=F32):
                uid[0] += 1
                return es.enter_context(nc.sbuf_tensor("t%d_%s" % (uid[0], name), list(shape), dt))
            gssd = T("gssd", [128, D])
            gv = T("gv", [128, D])
            gmlp = T("gmlp", [128, D])
            s.dma(gssd[:], gssd_in[li].partition_broadcast(128), (), ["gssd"])
            s.dma(gv[:], gv_in[li].partition_broadcast(128), (), ["gv"])
            s.dma(gmlp[:], gmlp_in[li].partition_broadcast(128), (), ["gmlp"])
            z2s = [T("z2_%d" % i, [128, 8192], BF16) for i in range(2)]
            NB = 2
            xtk = [T("xtk%d" % i, [128, 2048], BF16) for i in range(3)]
            xdl = [[T("xdl%d_%d" % (d_, i), [128, 2048], BF16) for i in range(NB)] for d_ in range(2)]
            hsl = [[T("hsl%d_%d" % (d_, i), [128, 2048], BF16) for i in range(NB)] for d_ in range(2)]
            btl = [T("btl%d" % i, [128, 8, 128], BF16) for i in range(NB)]
            ctl = [T("ctl%d" % i, [128, 8, 128], BF16) for i in range(NB)]
            al = [T("al%d" % i, [128, 64]) for i in range(NB)]
            cbm = [T("cbm%d" % d_, [128, 8, 128]) for d_ in range(2)]
            rhs4 = [T("rhs4_%d" % i, [128, 4, 128]) for i in range(3)]
            E4 = [T("E4_%d" % i, [128, 4, 128]) for i in range(2)]
            F4 = [T("F4_%d" % i, [128, 4, 128]) for i in range(2)]
            M4 = [T("M4_%d" % i, [128, 4, 128], BF16) for i in range(2)]
            N4 = [T("N4_%d" % i, [128, 4, 128], BF16) for i in range(2)]
            ybufs = [T("ybuf%d" % i, [128, D]) for i in range(2)]
            tmp = T("tmp", [128, D])
            mbuf = T("mbuf", [128, D])
            vn = T("vn", [128, D], BF16)
            ycat = T("ycat", [128, 2 * D], BF16)
            yTs = [T("yTs%d" % i, [128, 32, 128], BF16) for i in range(2)]
            st5 = T("st5", [128, 8])
            masks = [(SLm, Um, Um), (SUm, Lm, Lm)]
            NCH = len(chunks2)

            def loads(n):
                c = chunks2[n]
                i = n % NB
                s.dma(xtk[n % 3][:], xtok_d[c], (), ["xtk%d" % (n % 3)])
                for d_ in range(2):
                    s.dma(xdl[d_][i][:], xdt_d[d_][c], (), ["xdl%d_%d" % (d_, i)])
                    s.dma(hsl[d_][i][:], hst_d[d_][c], (), ["hsl%d_%d" % (d_, i)])
                s.dma(btl[i][:], BT_d[c].rearrange("p (g t) -> p g t", t=128), (), ["btl%d" % i])
                s.dma(ctl[i][:], CT_d[c].rearrange("p (g t) -> p g t", t=128), (), ["ctl%d" % i])
                s.dma(al[i][:], a_d[c], (), ["al%d" % i])

            def z2load(n):
                c = chunks2[n]
                zi = n % 2
                for q4 in range(4):
                    s.dma(z2s[zi][:, q4 * 2048:(q4 + 1) * 2048], z2_d[c][:, q4 * 2048:(q4 + 1) * 2048], (),
                          [("z2", zi, q4)])

            def ssd(n, tick):
                i = n % NB
                yi = n % 2
                ybuf = ybufs[yi]
                for g in range(8):
                    s.mm(ps[:, g // 4, (g % 4) * 128:(g % 4 + 1) * 128], btl[i][:, g, :], ctl[i][:, g, :], True, True,
                         ["btl%d" % i, "ctl%d" % i], [B(g // 4)])
                for d_ in range(2):
                    mk = masks[d_][2]
                    for hb_ in range(2):
                        s.tt("dve", cbm[d_][:, hb_ * 4:(hb_ + 1) * 4, :],
                             ps[:, hb_, :].rearrange("p (g t) -> p g t", t=128),
                             mk[:, None, :].broadcast_to([128, 4, 128]), ALU.mult,
                             [B(hb_), "consts"], [("cbm%d" % d_, hb_)])
                items = [(d_, g) for d_ in range(2) for g in range(8)]

                def stageA0(idx):
                    d_, g = items[idx]
                    ri = idx % 3
                    m2 = masks[d_][1]
                    hc0 = d_ * 32 + g * 4
                    s.tt("dve", rhs4[ri][:], m2[:, None, :].broadcast_to([128, 4, 128]),
                         al[i][:, hc0:hc0 + 4, None].broadcast_to([128, 4, 128]), ALU.mult,
                         ["consts", "al%d" % i], ["rhs4_%d" % ri])

                def stageA1(idx):
                    d_, g = items[idx]
                    gi = idx % 2
                    ri = idx % 3
                    m1 = masks[d_][0]
                    rR = ["rhs4_%d" % ri, "consts"]
                    bD = 2 + gi * 2
                    bA = 3 + gi * 2
                    rv = rhs4[ri][:].rearrange("p a b -> p (a b)")
                    s.mm(ps[:, bD, :], m1, rv, True, True, rR, [B(bD)])
                    s.mm(ps[:, bA, :], ones, rv, True, True, rR, [B(bA)])
                    s.act(E4[gi][:].rearrange("p a b -> p (a b)"), ps[:, bD, :], AF.Exp, [B(bD)], ["E4_%d" % gi])
                    s.act(F4[gi][:].rearrange("p a b -> p (a b)"), ps[:, bA, :], AF.Exp, [B(bA)], ["F4_%d" % gi])

                def stageA2(idx):
                    d_, g = items[idx]
                    gi = idx % 2
                    s.tt("dve", M4[gi][:], E4[gi][:], cbm[d_][:, g:g + 1, :].broadcast_to([128, 4, 128]), ALU.mult,
                         ["E4_%d" % gi, ("cbm%d" % d_, g // 4)], ["M4_%d" % gi])
                    s.tt("dve", N4[gi][:], F4[gi][:], ctl[i][:, g:g + 1, :].broadcast_to([128, 4, 128]), ALU.mult,
                         ["F4_%d" % gi, "ctl%d" % i], ["N4_%d" % gi])

                def stageB(idx):
                    d_, g = items[idx]
                    gi = idx % 2
                    yo = ps[:, 6 + g % 2, 0:256]
                    for r4 in range(4):
                        h = g * 4 + r4
                        s.mm(yo[:, r4 * 64:(r4 + 1) * 64], M4[gi][:, r4, :], xdl[d_][i][:, h * 64:(h + 1) * 64],
                             True, False, ["M4_%d" % gi, "xdl%d_%d" % (d_, i)], [B(6 + g % 2)])
                        s.mm(yo[:, r4 * 64:(r4 + 1) * 64], N4[gi][:, r4, :], hsl[d_][i][:, h * 64:(h + 1) * 64],
                             False, True, ["N4_%d" % gi, "hsl%d_%d" % (d_, i)], [B(6 + g % 2)])
                    yb = ybuf[:, g * 256:(g + 1) * 256]
                    if d_ == 0:
                        s.cp("dve", yb, yo, [B(6 + g % 2)], [("ybuf", yi, g)])
                    else:
                        s.tt("dve", yb, yo, yb, ALU.add, [B(6 + g % 2), ("ybuf", yi, g)], [("ybuf", yi, g)])

                stageA0(0)
                stageA0(1)
                stageA0(2)
                stageA1(0)
                stageA1(1)
                stageA2(0)
                for idx in range(16):
                    if idx + 3 < 16:
                        stageA0(idx + 3)
                    if idx + 2 < 16:
                        stageA1(idx + 2)
                    if idx + 1 < 16:
                        stageA2(idx + 1)
                    stageB(idx)
                    tick()

            def tail(n):
                c = chunks2[n]
                yi = n % 2
                zi = n % 2
                ybuf = ybufs[yi]
                z2 = z2s[zi]
                xt_ = xtk[n % 3]
                xk = "xtk%d" % (n % 3)
                yK = [("ybuf", yi, g) for g in range(8)]
                Z = lambda q: ("z2", zi, q)
                jA = ycat[:, 2048:4096]
                s.tt("pool", tmp[:].rearrange("p (h e) -> p h e", e=64), xt_[:].rearrange("p (h e) -> p h e", e=64),
                     dskip[:, :, None].broadcast_to([128, 32, 64]), ALU.mult, [xk, "dskip"], ["tmp"])
                yield
                s.tt("dve", ybuf[:], ybuf[:], tmp[:], ALU.add, yK + ["tmp"], yK)
                yield
                s.act(tmp[:], z2[:, 0:2048], AF.Silu, [Z(0)], ["tmp"])
                yield
                s.tt("dve", ybuf[:], ybuf[:], tmp[:], ALU.mult, yK + ["tmp"], yK)
                yield
                s.act(jA, ybuf[:], AF.Square, yK, ["ycat_b", "ss_a"], accum=st5[:, 0:1])
                rms_r(st5[:, 0:1], st5[:, 1:2], D, ["ss_a"], ["r_a"])
                yield
                s.stt(ycat[:, 0:2048], ybuf[:], st5[:, 1:2], gssd[:], ALU.mult, ALU.mult, yK + ["r_a", "gssd"], ["ycat_a"])
                yield
                s.act(jA, z2[:, 4096:6144], AF.Square, [Z(2)], ["ycat_b", "ss_v"], accum=st5[:, 2:3])
                rms_r(st5[:, 2:3], st5[:, 3:4], D, ["ss_v"], ["r_v"])
                yield
                s.stt(vn[:], z2[:, 4096:6144], st5[:, 3:4], gv[:], ALU.mult, ALU.mult, [Z(2), "r_v", "gv"], ["vn"])
                yield
                for qb in range(4):
                    bk = qb % 2
                    for gq in range(4):
                        g = qb * 4 + gq
                        s.mm(ps[:, bk, gq * 128:(gq + 1) * 128], wsTb[:, g, :], vn[:, g * 128:(g + 1) * 128], True, True,
                             ["wsTb", "vn"], [B(bk)])
                    s.tt("dve", mbuf[:, qb * 512:(qb + 1) * 512].rearrange("p (g e) -> p g e", e=128),
                         ps[:, bk, :].rearrange("p (g e) -> p g e", e=128),
                         bsT[:, qb * 4:(qb + 1) * 4, None].broadcast_to([128, 4, 128]), ALU.add, [B(bk), "bsT"], [("mbuf", qb)])
                    yield
                mK = [("mbuf", qb) for qb in range(4)]
                s.tt("pool", mbuf[:], mbuf[:], z2[:, 2048:4096], ALU.mult, mK + [Z(1)], mK)
                yield
                s.act(tmp[:], z2[:, 6144:8192], AF.Silu, [Z(3)], ["tmp"])
                yield
                s.tt("dve", mbuf[:], mbuf[:], tmp[:], ALU.mult, mK + ["tmp"], mK)
                yield
                s.act(jA, mbuf[:], AF.Square, mK, ["ycat_b", "ss_m"], accum=st5[:, 4:5])
                rms_r(st5[:, 4:5], st5[:, 5:6], D, ["ss_m"], ["r_m"])
                yield
                s.stt(ycat[:, 2048:4096], mbuf[:], st5[:, 5:6], gmlp[:], ALU.mult, ALU.mult, mK + ["r_m", "gmlp"], ["ycat_b"])
                yield
                yt = yTs[n % 2]
                ytk = "yTs%d" % (n % 2)
                for kb in range(8):
                    bk = kb % 2
                    for kq in range(4):
                        k = kb * 4 + kq
                        s.mm(ps[:, bk, kq * 128:(kq + 1) * 128], ycat[:, k * 128:(k + 1) * 128], identb[:], True, True,
                             ["ycat_a" if k < 16 else "ycat_b", "identb"], [B(bk)])
                    s.cp("act" if kb % 2 == 0 else "dve", yt[:, kb * 4:(kb + 1) * 4, :],
                         ps[:, bk, :].rearrange("p (c t) -> p c t", t=128), [B(bk)], [(ytk, kb)])
                    yield
                s.dma(yT_d[c].rearrange("p (k t) -> p k t", t=128), yt[:], [(ytk, k) for k in range(8)], [("yT_d", c)],
                      key="st_" + ytk)

            def drain(gen):
                if gen is not None:
                    for _ in gen:
                        pass

            loads(0)
            z2load(0)
            prev = None
            for n in range(NCH):
                if n + 1 < NCH:
                    loads(n + 1)

                def tick(gen=prev):
                    if gen is not None:
                        for _ in range(2):
                            try:
                                next(gen)
                            except StopIteration:
                                break
                ssd(n, tick)
                drain(prev)
                if n + 1 < NCH:
                    z2load(n + 1)
                prev = tail(n)
            drain(prev)
            s.emit()
            pc[0] += 1
            if pc[0] >= STOP:
                return nc, s

        with ExitStack() as es:
            def T(name, shape, dt=F32):
                uid[0] += 1
                return es.enter_context(nc.sbuf_tensor("t%d_%s" % (uid[0], name), list(shape), dt))
            ggb = [T("ggb%d" % r, [128, D]) for r in range(2)]
            for r in range(2):
                s.dma(ggb[r][:], gg_d[r].partition_broadcast(128), (), ["ggb%d" % r])
            yTl = T("yTl", [128, 32, 512], BF16)
            wo_ = [T("wo%d" % i, [128, 32, 512], BF16) for i in range(2)]
            osb = T("osb", [128, 4, D])
            xr = [T("xr%d" % i, [128, D]) for i in range(2)]
            junk = T("junk6", [128, D], BF16)
            st6 = [T("st6_%d" % i, [128, 2]) for i in range(2)]
            sbs = [list(range(NCTX))] + [list(range(NCTX + 4 * i, NCTX + 4 * i + 4)) for i in range(NLAT // 4)]
            if last:
                sbs = sbs[1:]
            wcnt = 0
            ecnt = 0
            ccnt = 0
            for sbi, chs in enumerate(sbs):
                for ci, c in enumerate(chs):
                    s.dma(yTl[:, :, ci * 128:(ci + 1) * 128], yT_d[c].rearrange("p (k t) -> p k t", t=128), (), [("yTl", ci)])
                for jb in range(4):
                    wi_ = wcnt % 2
                    wcnt += 1
                    for hh2 in range(4):
                        s.dma(wo_[wi_][:, hh2 * 8:(hh2 + 1) * 8, :],
                              WOs_d[jb][:, hh2 * 4096:(hh2 + 1) * 4096].rearrange("p (k c) -> p k c", c=512),
                              (), [("wo%d" % wi_, hh2)])
                    for ci, c in enumerate(chs):
                        bk = ecnt % 8
                        ecnt += 1
                        o = ps[:, bk, :]
                        for k in range(32):
                            s.mm(o, yTl[:, k, ci * 128:(ci + 1) * 128], wo_[wi_][:, k, :], k == 0, k == 31,
                                 [("yTl", ci), ("wo%d" % wi_, k // 8)], [B(bk)])
                        s.cp("act" if ecnt % 2 == 0 else "dve", osb[:, ci, jb * 512:(jb + 1) * 512], o, [B(bk)], [("osb", ci, jb)])
                for ci, c in enumerate(chs):
                    i = ccnt % 2
                    ccnt += 1
                    r = 1 if c < NCTX else 0
                    oR = [("osb", ci, jb) for jb in range(4)]
                    s.dma(xr[i][:], xrow(c), (), ["xr%d" % i])
                    s.act(junk[:], osb[:, ci, :], AF.Square, oR, ["junk6", "ss6_%d" % i], accum=st6[i][:, 0:1])
                    rms_r(st6[i][:, 0:1], st6[i][:, 1:2], D, ["ss6_%d" % i], ["r6_%d" % i])
                    s.stt(osb[:, ci, :], osb[:, ci, :], st6[i][:, 1:2], ggb[r][:], ALU.mult, ALU.mult,
                          oR + ["r6_%d" % i, "ggb%d" % r], oR)
                    s.tt("pool", xr[i][:], xr[i][:], osb[:, ci, :], ALU.add, ["xr%d" % i] + oR, ["xr%d" % i])
                    s.dma(xdst(c), xr[i][:], ["xr%d" % i], [("xout", c)], key="st_xr%d" % i)
            s.emit()
            pc[0] += 1
            if pc[0] >= STOP:
                return nc, s
    return nc, s


def _consts():
    j = np.arange(128)[:, None]
    q = np.arange(128)[None, :]
    c = np.zeros((128, 6, 128), np.float32)
    c[:, 0] = (j == q)
    c[:, 1] = (j <= q)
    c[:, 2] = (j >= q)
    c[:, 3] = (j > q)
    c[:, 4] = (j < q)
    c[:, 5] = 1.0
    return c


def _prep(inp, b, half, layers, x_b, ctx_b):
    ls = list(layers)
    nl = len(ls)
    f = lambda a: np.ascontiguousarray(a, dtype=np.float32)
    flip = (half == 1)
    cT = np.stack([inp["c"][b].reshape(16, 128).T, inp["c_ctx"].reshape(16, 128).T], axis=-1)
    b_ada = inp["b_ada"][ls]
    w_in = inp["w_in"][ls]
    dt_bias = inp["dt_bias"][ls]
    a_log = inp["a_log"][ls]
    conv_w = inp["conv_w"][ls]
    w_s = inp["w_s"][ls]
    b_s = inp["b_s"][ls]
    if flip:
        w_in = np.concatenate([w_in[:, :, :4096], w_in[:, :, 4128:4160], w_in[:, :, 4096:4128], w_in[:, :, 4160:]], axis=2)
        dt_bias = dt_bias[:, ::-1]
        a_log = a_log[:, ::-1]
        conv_w = conv_w[:, ::-1]
        w_s = w_s[:, :, ::-1, ::-1]
        b_s = b_s[:, :, ::-1]
        x_b = x_b[::-1]
        ctx_b = ctx_b[::-1]
    sel = np.zeros((128, 2), np.float32)
    sel[:, 1 - half] = 1.0
    m = {
        "x_in": f(x_b), "ctx_in": f(ctx_b), "cT": f(cT), "consts": _consts(), "sel": sel,
        "w_ada": f(inp["w_ada"][ls]), "w_in": f(w_in), "w_out": f(inp["w_out"][ls]),
        "badaT": f(b_ada[:, :4096].reshape(nl, 32, 128).transpose(0, 2, 1)),
        "bgate": f(np.repeat(b_ada[:, None, 4096:], 2, axis=1)),
        "gpost": f(np.repeat(inp["g_post"][ls][:, None, :], 2, axis=1)),
        "gpreT": f(inp["g_pre"][ls].reshape(nl, 16, 128).transpose(0, 2, 1)),
        "convwT": f(conv_w.reshape(nl, 5, 32, 128).transpose(0, 3, 2, 1)),
        "convbT": f(inp["conv_b"][ls].reshape(nl, 32, 128).transpose(0, 2, 1)),
        "dtb": f(np.broadcast_to(dt_bias.reshape(nl, 1, 64), (nl, 128, 64))),
        "alog": f(np.broadcast_to(a_log.reshape(nl, 1, 64), (nl, 128, 64))),
        "dskip": f(np.broadcast_to(inp["d_skip"][ls].reshape(nl, 1, 32), (nl, 128, 32))),
        "gssd": f(inp["g_ssd"][ls]), "gv": f(inp["g_v"][ls]), "gmlp": f(inp["g_mlp"][ls]),
        "wsT": f(w_s.transpose(0, 3, 1, 2)),
        "bsT": f(b_s.transpose(0, 2, 1)),
    }
    return m


LAUNCH_GROUPS = [[0, 1, 2, 3]]
STOP = 10 ** 9
DEBUG = False
LAST = {}


def kernel(**inp):
    inp = {k: np.asarray(v) for k, v in inp.items()}
    x = inp["x"]
    ctx = inp["ctx"]
    nb, Lx, _ = x.shape
    Lh = Lx // 2
    NLAT = Lh // 128
    ncore = 2 * nb
    xs = [x[c // 2, (c % 2) * Lh:(c % 2 + 1) * Lh] for c in range(ncore)]
    cs = [ctx[c // 2] for c in range(ncore)]
    res = None
    for gi, layers in enumerate(LAUNCH_GROUPS):
        final = (gi == len(LAUNCH_GROUPS) - 1)
        nc, _ = build(NLAT, layers, gi == 0, final)
        in_maps = [_prep(inp, c // 2, c % 2, layers, xs[c], cs[c]) for c in range(ncore)]
        res = run_bass_kernel_spmd(nc, in_maps, core_ids=list(range(ncore)))
        LAST["res"] = res
        xs = [np.asarray(res.results[c]["out"])[::-1] if c % 2 else np.asarray(res.results[c]["out"]) for c in range(ncore)]
        cs = [np.asarray(res.results[c]["ctx_out"])[::-1] if c % 2 else np.asarray(res.results[c]["ctx_out"]) for c in range(ncore)]
    out = np.empty((nb, Lx, x.shape[2]), np.float32)
    for c in range(ncore):
        out[c // 2, (c % 2) * Lh:(c % 2 + 1) * Lh] = xs[c]
    return out
```
